# Optimizing a Trainium2 kernel written in Bass

```python
import math
import jax
import jax.numpy as jnp
from jax import lax
import numpy as np

D_MODEL = 1024
BATCH = 4
SEQ = 8192
DEPTH = 4

PLE_DIM = 256
N_MIXERS = 3
N_A_LAYERS = (DEPTH + 2) // 3
N_B_LAYERS = (DEPTH + 1) // 3
N_C_LAYERS = DEPTH // 3
RMS_EPS = 1e-6
NEG = -1e30

RET_HEADS = 4
RET_DK = D_MODEL // RET_HEADS
RET_DV = 2 * RET_DK
RET_CHUNK = 128
RET_IN = 2 * RET_HEADS * RET_DK + 2 * RET_HEADS * RET_DV

DIL_GROUPS = ((128, 1), (512, 4), (2048, 16))
DIL_HEADS = 8
DIL_HEAD_DIM = D_MODEL // DIL_HEADS
DIL_BLOCK = 128
DIL_QK = len(DIL_GROUPS) * DIL_HEADS * DIL_HEAD_DIM
DIL_IN = 2 * DIL_QK + DIL_HEADS * DIL_HEAD_DIM

DIFF_HEADS = 8
DIFF_HEAD_DIM = D_MODEL // DIFF_HEADS // 2
DIFF_QK = 2 * DIFF_HEADS * DIFF_HEAD_DIM
DIFF_V = DIFF_HEADS * 2 * DIFF_HEAD_DIM
DIFF_IN = 2 * DIFF_QK + DIFF_V
ATTN_BLOCK = 128

FFN_HIDDEN = -(-8 * D_MODEL // (3 * 256)) * 256

kernel_name = "hybrid_retention_dilated_diffattn_trunk"


def rms_norm(x, gain=None, eps=RMS_EPS):
    xf = x.astype(jnp.float32)
    y = xf * lax.rsqrt(jnp.mean(xf * xf, axis=-1, keepdims=True) + eps)
    if gain is not None:
        y = y * gain.astype(jnp.float32)
    return y.astype(x.dtype)


def alibi_slopes(n):
    ratio = 2.0 ** (-8.0 / n)
    return jnp.asarray(np.array([ratio ** (h + 1) for h in range(n)], dtype=np.float32))


def diff_lambda_init(layer_idx):
    return 0.8 - 0.6 * math.exp(-0.3 * layer_idx)


def retention(hn, w_in, w_out):
    B_, S, _ = hn.shape
    H, dk, dv, C = RET_HEADS, RET_DK, RET_DV, RET_CHUNK
    proj = hn @ w_in
    o1, o2, o3 = H * dk, 2 * H * dk, 2 * H * dk + H * dv
    q = proj[..., :o1].reshape(B_, S, H, dk).astype(jnp.float32)
    k = proj[..., o1:o2].reshape(B_, S, H, dk).astype(jnp.float32) * (dk ** -0.5)
    v = proj[..., o2:o3].reshape(B_, S, H, dv).astype(jnp.float32)
    g = proj[..., o3:]
    n_chunks = S // C

    def chunks(a):
        return a.reshape(B_, n_chunks, C, H, a.shape[-1]).transpose(1, 0, 3, 2, 4)

    log_g = jnp.log(1.0 - 2.0 ** (-5.0 - jnp.arange(H, dtype=jnp.float32)))
    pos = jnp.arange(C, dtype=jnp.float32)
    rel = pos[:, None] - pos[None, :]
    decay_in = jnp.where(rel >= 0, jnp.exp(jnp.maximum(rel, 0.0)[None] * log_g[:, None, None]), 0.0)
    decay_q = jnp.exp((pos + 1.0)[None] * log_g[:, None])
    decay_k = jnp.exp((C - 1.0 - pos)[None] * log_g[:, None])
    decay_chunk = jnp.exp(C * log_g)

    def step(R, inp):
        qc, kc, vc = inp
        att = jnp.einsum('bhnd,bhmd->bhnm', qc, kc) * decay_in
        y = (jnp.einsum('bhnm,bhme->bhne', att, vc)
             + jnp.einsum('bhnd,bhde->bhne', qc, R) * decay_q[None, :, :, None])
        R = (R * decay_chunk[None, :, None, None]
             + jnp.einsum('bhmd,bhme->bhde', kc * decay_k[None, :, :, None], vc))
        return R, y

    R0 = jnp.zeros((B_, H, dk, dv), jnp.float32)
    _, ys = lax.scan(step, R0, (chunks(q), chunks(k), chunks(v)))
    y = ys.transpose(1, 0, 3, 2, 4).reshape(B_, S, H, dv)
    y = rms_norm(y).reshape(B_, S, H * dv)
    y = (jax.nn.silu(g) * y).astype(hn.dtype)
    return y @ w_out


def dilated_group(q, k, v, window, dilation, slopes):
    B_, S, H, hd = q.shape
    J = window // dilation
    unit = dilation * DIL_BLOCK
    Sp = -(-S // unit) * unit
    L = Sp // dilation
    nb = L // DIL_BLOCK

    def to_streams(a):
        a = jnp.pad(a, ((0, 0), (0, Sp - S), (0, 0), (0, 0)))
        a = a.reshape(B_, L, dilation, H, hd).transpose(0, 2, 3, 1, 4)
        return a.reshape(B_, dilation, H, nb, DIL_BLOCK, hd)

    def with_prev(a):
        prev = jnp.pad(a[:, :, :, :-1], ((0, 0), (0, 0), (0, 0), (1, 0), (0, 0), (0, 0)))
        return jnp.concatenate([prev, a], axis=4)

    qs = to_streams(q)
    kc = with_prev(to_streams(k))
    vc = with_prev(to_streams(v))
    s = jnp.einsum('brhnqd,brhnkd->brhnqk', qs, kc).astype(jnp.float32) * (hd ** -0.5)
    rel = (DIL_BLOCK + jnp.arange(DIL_BLOCK))[:, None] - jnp.arange(2 * DIL_BLOCK)[None, :]
    first = (jnp.arange(nb)[:, None, None] == 0) & (jnp.arange(2 * DIL_BLOCK)[None, None, :] < DIL_BLOCK)
    valid = ((rel >= 0) & (rel <= J))[None] & jnp.logical_not(first)
    bias = -(slopes[:, None, None] * (rel * dilation).astype(jnp.float32)[None])
    s = jnp.where(valid, s + bias[:, None], NEG)
    lse = jax.nn.logsumexp(s, axis=-1)
    pr = jnp.exp(s - lse[..., None]).astype(v.dtype)
    o = jnp.einsum('brhnqk,brhnkd->brhnqd', pr, vc)
    o = o.reshape(B_, dilation, H, L, hd).transpose(0, 3, 1, 2, 4).reshape(B_, Sp, H, hd)[:, :S]
    lse = lse.reshape(B_, dilation, H, L).transpose(0, 3, 1, 2).reshape(B_, Sp, H)[:, :S]
    return o, lse


def dilated_attention(hn, w_in, w_out, q_gain, k_gain):
    B_, S, _ = hn.shape
    NG = len(DIL_GROUPS)
    proj = hn @ w_in
    q = proj[..., :DIL_QK].reshape(B_, S, NG, DIL_HEADS, DIL_HEAD_DIM)
    k = proj[..., DIL_QK:2 * DIL_QK].reshape(B_, S, NG, DIL_HEADS, DIL_HEAD_DIM)
    v = proj[..., 2 * DIL_QK:].reshape(B_, S, DIL_HEADS, DIL_HEAD_DIM)
    q = rms_norm(q, q_gain)
    k = rms_norm(k, k_gain)
    slopes = alibi_slopes(DIL_HEADS)
    outs, lses = [], []
    for gi, (window, dilation) in enumerate(DIL_GROUPS):
        o, l = dilated_group(q[:, :, gi], k[:, :, gi], v, window, dilation, slopes)
        outs.append(o)
        lses.append(l)
    w = jax.nn.softmax(jnp.stack(lses, axis=0), axis=0)
    o = jnp.einsum('gbsh,gbshd->bshd', w, jnp.stack(outs, axis=0).astype(jnp.float32))
    return o.astype(hn.dtype).reshape(B_, S, DIL_HEADS * DIL_HEAD_DIM) @ w_out


def diff_attention(hn, w_in, w_out, q_gain, k_gain, lq1, lk1, lq2, lk2, subln_gain, lambda_init):
    B_, S, _ = hn.shape
    proj = hn @ w_in
    q = proj[..., :DIFF_QK].reshape(B_, S, 2 * DIFF_HEADS, DIFF_HEAD_DIM)
    k = proj[..., DIFF_QK:2 * DIFF_QK].reshape(B_, S, 2 * DIFF_HEADS, DIFF_HEAD_DIM)
    v = proj[..., 2 * DIFF_QK:].reshape(B_, S, DIFF_HEADS, 2 * DIFF_HEAD_DIM)
    q = rms_norm(q, q_gain)
    k = rms_norm(k, k_gain)
    f32 = jnp.float32
    lam = (jnp.exp(jnp.sum(lq1.astype(f32) * lk1.astype(f32)))
           - jnp.exp(jnp.sum(lq2.astype(f32) * lk2.astype(f32))) + lambda_init)
    slopes = jnp.repeat(alibi_slopes(DIFF_HEADS), 2)
    kpos = jnp.arange(S, dtype=f32)
    scale = DIFF_HEAD_DIM ** -0.5

    def block(n):
        qb = lax.dynamic_slice_in_dim(q, n * ATTN_BLOCK, ATTN_BLOCK, axis=1)
        s = jnp.einsum('bqhd,bkhd->bhqk', qb, k).astype(f32) * scale
        qpos = (n * ATTN_BLOCK + jnp.arange(ATTN_BLOCK)).astype(f32)
        rel = qpos[:, None] - kpos[None, :]
        s = jnp.where(rel >= 0, s - slopes[:, None, None] * rel, NEG)
        a = jax.nn.softmax(s, axis=-1).reshape(B_, DIFF_HEADS, 2, ATTN_BLOCK, S)
        w = (a[:, :, 0] - lam * a[:, :, 1]).astype(v.dtype)
        return jnp.einsum('bhqk,bkhe->bqhe', w, v)

    o = lax.map(block, jnp.arange(S // ATTN_BLOCK))
    o = o.transpose(1, 0, 2, 3, 4).reshape(B_, S, DIFF_HEADS, 2 * DIFF_HEAD_DIM)
    o = rms_norm(o, subln_gain) * (1.0 - lambda_init)
    return o.reshape(B_, S, DIFF_HEADS * 2 * DIFF_HEAD_DIM) @ w_out


def swiglu(hn, w_in, w_out):
    u = hn @ w_in
    a, b = u[..., :FFN_HIDDEN], u[..., FFN_HIDDEN:]
    return (jax.nn.silu(a) * b) @ w_out


def setup_inputs(seed: int = 0) -> dict:
    key = jax.random.key(seed)
    ks = jax.random.split(key, 32)
    f32 = jnp.float32

    def dense(k, shape):
        return jax.random.normal(k, shape, f32) * (shape[-2] ** -0.5)

    def gain(k, shape):
        return 1.0 + 0.02 * jax.random.normal(k, shape, f32)

    def small(k, shape, scale):
        return scale * jax.random.normal(k, shape, f32)

    return {
        "x": jax.random.normal(ks[0], (BATCH, SEQ, D_MODEL), f32),
        "p": jax.random.normal(ks[1], (DEPTH, BATCH, SEQ, PLE_DIM), f32),
        "mix_norm": gain(ks[2], (DEPTH, D_MODEL)),
        "ffn_norm": gain(ks[3], (DEPTH, D_MODEL)),
        "a_w_in": dense(ks[4], (N_A_LAYERS, D_MODEL, RET_IN)),
        "a_w_out": dense(ks[5], (N_A_LAYERS, RET_HEADS * RET_DV, D_MODEL)),
        "b_w_in": dense(ks[6], (N_B_LAYERS, D_MODEL, DIL_IN)),
        "b_q_norm": gain(ks[7], (N_B_LAYERS, DIL_HEAD_DIM)),
        "b_k_norm": gain(ks[8], (N_B_LAYERS, DIL_HEAD_DIM)),
        "b_w_out": dense(ks[9], (N_B_LAYERS, DIL_HEADS * DIL_HEAD_DIM, D_MODEL)),
        "c_w_in": dense(ks[10], (N_C_LAYERS, D_MODEL, DIFF_IN)),
        "c_q_norm": gain(ks[11], (N_C_LAYERS, DIFF_HEAD_DIM)),
        "c_k_norm": gain(ks[12], (N_C_LAYERS, DIFF_HEAD_DIM)),
        "c_lambda_q1": small(ks[13], (N_C_LAYERS, DIFF_HEAD_DIM), 0.1),
        "c_lambda_k1": small(ks[14], (N_C_LAYERS, DIFF_HEAD_DIM), 0.1),
        "c_lambda_q2": small(ks[15], (N_C_LAYERS, DIFF_HEAD_DIM), 0.1),
        "c_lambda_k2": small(ks[16], (N_C_LAYERS, DIFF_HEAD_DIM), 0.1),
        "c_subln": gain(ks[17], (N_C_LAYERS, 2 * DIFF_HEAD_DIM)),
        "c_w_out": dense(ks[18], (N_C_LAYERS, DIFF_V, D_MODEL)),
        "ffn_w_in": dense(ks[19], (DEPTH, D_MODEL, 2 * FFN_HIDDEN)),
        "ffn_w_out": dense(ks[20], (DEPTH, FFN_HIDDEN, D_MODEL)),
        "ple_w_proj": dense(ks[21], (DEPTH, PLE_DIM, D_MODEL)),
        "ple_norm": gain(ks[22], (DEPTH, D_MODEL)),
        "ple_gate_norm": gain(ks[23], (DEPTH, D_MODEL)),
        "ple_w_gate": dense(ks[24], (DEPTH, D_MODEL, D_MODEL)),
    }


def reference(x, p, mix_norm, ffn_norm, a_w_in, a_w_out, b_w_in, b_q_norm, b_k_norm, b_w_out,
              c_w_in, c_q_norm, c_k_norm, c_lambda_q1, c_lambda_k1, c_lambda_q2, c_lambda_k2,
              c_subln, c_w_out, ffn_w_in, ffn_w_out, ple_w_proj, ple_norm, ple_gate_norm, ple_w_gate):
    h = x
    for i in range(DEPTH):
        kind, j = i % N_MIXERS, i // N_MIXERS
        hn = rms_norm(h, mix_norm[i])
        if kind == 0:
            y = retention(hn, a_w_in[j], a_w_out[j])
        elif kind == 1:
            y = dilated_attention(hn, b_w_in[j], b_w_out[j], b_q_norm[j], b_k_norm[j])
        else:
            y = diff_attention(hn, c_w_in[j], c_w_out[j], c_q_norm[j], c_k_norm[j],
                               c_lambda_q1[j], c_lambda_k1[j], c_lambda_q2[j], c_lambda_k2[j],
                               c_subln[j], diff_lambda_init(i))
        h = h + y.astype(h.dtype)
        h = h + swiglu(rms_norm(h, ffn_norm[i]), ffn_w_in[i], ffn_w_out[i]).astype(h.dtype)
        gate = jax.nn.sigmoid(rms_norm(h, ple_gate_norm[i]) @ ple_w_gate[i])
        e = rms_norm(p[i] @ ple_w_proj[i], ple_norm[i])
        h = h + (gate * e).astype(h.dtype)
    return h
```

```python
import contextlib
import math
import numpy as np
import concourse.bass as bass
import concourse.mybir as mybir
from concourse.bass_utils import run_bass_kernel_spmd

F32 = mybir.dt.float32
BF16 = mybir.dt.bfloat16
ALU = mybir.AluOpType
AF = mybir.ActivationFunctionType

D = 1024
S = 8192
B = 4
DEPTH = 4
HID = 2816
EPS = 1e-6

ENGS = ("pe", "act", "dve", "pool", "sp")
NDMASEM = 8
SAME_ENGINE_SYNC = {"pe": False, "act": True, "dve": True, "pool": True, "sp": False}


class Res:
    __slots__ = ("w", "rs")

    def __init__(self):
        self.w = None
        self.rs = []


class Op:
    __slots__ = ("eng", "fn", "deps", "dma", "need_inc", "sem", "val", "dslot")

    def __init__(self, eng, fn, dma):
        self.eng = eng
        self.fn = fn
        self.dma = dma
        self.deps = []
        self.need_inc = False
        self.sem = None
        self.val = None
        self.dslot = 0


class Ctx:
    def __init__(self, nc):
        self.nc = nc
        self.stack = contextlib.ExitStack()
        st = self.stack
        self.esem = {e: st.enter_context(nc.semaphore(f"s_{e}")) for e in ENGS}
        self.dsem = {e: [st.enter_context(nc.semaphore(f"d_{e}{i}")) for i in range(NDMASEM)] for e in ENGS}
        self.ecount = {e: 0 for e in ENGS}
        self.dcount = {e: 0 for e in ENGS}
        self.seen = {e: {} for e in ENGS}
        self.barrier = []
        self.nuid = 0

    def all_signals(self):
        sig = []
        for e in ENGS:
            if self.ecount[e] > 0:
                sig.append((self.esem[e], self.ecount[e]))
            nd = self.dcount[e]
            for i in range(min(nd, NDMASEM)):
                n_i = (nd - 1 - i) // NDMASEM + 1
                sig.append((self.dsem[e][i], 16 * n_i))
        return sig

    def finish(self):
        nc = self.nc
        sig = self.all_signals()
        ctx = self
        with nc.Block() as block:
            @block.sync
            def _(eng):
                for (k, v) in sig:
                    if ctx.seen["sp"].get(id(k), 0) < v:
                        eng.wait_ge(k, v)
        self.stack.close()


class Phase:
    def __init__(self, ctx):
        self.ctx = ctx
        self.nc = ctx.nc
        self.q = {e: [] for e in ENGS}
        self.stack = contextlib.ExitStack()

    def sbuf(self, shape, dtype):
        self.ctx.nuid += 1
        return self.stack.enter_context(self.nc.sbuf_tensor(f"sb{self.ctx.nuid}", list(shape), dtype))

    def psum(self, shape, dtype=F32):
        self.ctx.nuid += 1
        return self.stack.enter_context(self.nc.psum_tensor(f"ps{self.ctx.nuid}", list(shape), dtype))

    def op(self, eng, fn, reads=(), writes=(), dma=False):
        o = Op(eng, fn, dma)
        deps = []
        for r in reads:
            if r.w is not None:
                deps.append(r.w)
        for w in writes:
            if w.w is not None:
                deps.append(w.w)
            deps.extend(w.rs)
        seen = set()
        for d in deps:
            if id(d) in seen or d is o:
                continue
            seen.add(id(d))
            if d.eng == eng and not d.dma and not SAME_ENGINE_SYNC[eng]:
                continue
            o.deps.append(d)
            d.need_inc = True
        for r in reads:
            r.rs.append(o)
        for w in writes:
            w.w = o
            w.rs = []
        self.q[eng].append(o)
        return o

    def dma(self, eng, out, in_, reads=(), writes=()):
        return self.op(eng, lambda e: e.dma_start(out=out, in_=in_), reads, writes, dma=True)

    def end(self):
        ctx = self.ctx
        nc = self.nc
        for e in ENGS:
            last = None
            for o in self.q[e]:
                if not o.dma:
                    last = o
            if last is not None:
                last.need_inc = True
        for e in ENGS:
            for o in self.q[e]:
                if o.dma:
                    nd = ctx.dcount[e]
                    o.sem = ctx.dsem[e][nd % NDMASEM]
                    o.val = 16 * (nd // NDMASEM + 1)
                    o.dslot = nd
                    ctx.dcount[e] = nd + 1
                elif o.need_inc:
                    ctx.ecount[e] += 1
                    o.sem = ctx.esem[e]
                    o.val = ctx.ecount[e]
        barrier = ctx.barrier
        ph = self

        def replay(e, eng):
            seen = ctx.seen[e]
            for (k, v) in barrier:
                if seen.get(id(k), 0) < v:
                    eng.wait_ge(k, v)
                    seen[id(k)] = v
            for o in ph.q[e]:
                need = {}
                for d in o.deps:
                    kk = id(d.sem)
                    if kk not in need or need[kk][1] < d.val:
                        need[kk] = (d.sem, d.val)
                if o.dma and o.dslot >= NDMASEM:
                    kk = id(o.sem)
                    v = o.val - 16
                    if kk not in need or need[kk][1] < v:
                        need[kk] = (o.sem, v)
                for kk, (k, v) in need.items():
                    if seen.get(kk, 0) < v:
                        eng.wait_ge(k, v)
                        seen[kk] = v
                ins = o.fn(eng)
                if o.dma:
                    ins.then_inc(o.sem, 16)
                elif o.need_inc:
                    ins.then_inc(o.sem, 1)

        with nc.Block() as block:
            @block.tensor
            def _(eng):
                replay("pe", eng)

            @block.scalar
            def _(eng):
                replay("act", eng)

            @block.vector
            def _(eng):
                replay("dve", eng)

            @block.gpsimd
            def _(eng):
                replay("pool", eng)

            @block.sync
            def _(eng):
                replay("sp", eng)
        ctx.barrier = ctx.all_signals()
        self.stack.close()


class Rot:
    def __init__(self, ph, n, shape, dtype, psum=False):
        self.bufs = [(ph.psum(shape, dtype) if psum else ph.sbuf(shape, dtype), Res()) for _ in range(n)]
        self.i = 0

    def next(self):
        b = self.bufs[self.i % len(self.bufs)]
        self.i += 1
        return b


def emit_rmsnorm_fm(ph, src, r_src, nchunk, T0, TN, gain, r_gain, dst, r_dst, env, dim, gain_col0=0):
    ones, r_ones, mh, r_mh = env["ones"], env["r_ones"], env["mh"], env["r_mh"]
    pst, r_pst = env["psrot"].next()
    for c in range(nchunk):
        sq, r_sq = env["sqrot"].next()
        ph.op("act", lambda e, c=c, sq=sq: e.activation(out=sq[:, :TN], in_=src[:, c, T0:T0 + TN], func=AF.Square),
              reads=[r_src], writes=[r_sq])
        ph.op("pe", lambda e, c=c, sq=sq, pst=pst: e.matmul(pst[:, :TN], ones[:], sq[:, :TN], start=(c == 0), stop=(c == nchunk - 1)),
              reads=[r_ones, r_sq], writes=[r_pst])
    tmp, r_tmp = env["tmprot"].next()
    ph.op("act", lambda e, tmp=tmp, pst=pst: e.activation(out=tmp[:, :TN], in_=pst[:, :TN], func=AF.Ln, scale=1.0 / dim, bias=env["epsc"][:, 0:1]),
          reads=[r_pst, r_mh], writes=[r_tmp])
    rstd, r_rstd = env["tmprot"].next()
    ph.op("act", lambda e, tmp=tmp, rstd=rstd: e.activation(out=rstd[:, :TN], in_=tmp[:, :TN], func=AF.Exp, scale=-0.5),
          reads=[r_tmp], writes=[r_rstd])
    for c in range(nchunk):
        ph.op("dve", lambda e, c=c, rstd=rstd: e.scalar_tensor_tensor(
            out=dst[:, c, T0:T0 + TN], in0=src[:, c, T0:T0 + TN], scalar=gain[:, gain_col0 + c:gain_col0 + c + 1], in1=rstd[:, :TN],
            op0=ALU.mult, op1=ALU.mult), reads=[r_src, r_gain, r_rstd], writes=[r_dst])


def make_env(ph):
    env = {}
    env["ones"] = ph.sbuf([128, 128], F32)
    env["r_ones"] = Res()
    env["mh"] = ph.sbuf([128, 512], F32)
    env["r_mh"] = Res()
    ph.op("pool", lambda e: e.memset(env["ones"][:], 1.0), writes=[env["r_ones"]])
    ph.op("pool", lambda e: e.memset(env["mh"][:], -0.5), writes=[env["r_mh"]])
    env["epsc"] = ph.sbuf([128, 1], F32)
    ph.op("pool", lambda e: e.memset(env["epsc"][:], EPS), writes=[env["r_mh"]])
    env["sqrot"] = Rot(ph, 3, [128, 512], F32)
    env["tmprot"] = Rot(ph, 4, [128, 512], F32)
    return env


TT = 1024
ST = 512


def phase_token_local(ctx, t, kout, ntok, first, last):
    ph = Phase(ctx)
    env = make_env(ph)
    psrot = Rot(ph, 8, [128, 512], F32, psum=True)
    env["psrot"] = psrot
    kc_out = kout // 128
    vec = ph.sbuf([128, 32], F32)
    r_vec = Res()
    ph.dma("sp", vec[:], t["vecs"], writes=[r_vec])
    h = ph.sbuf([128, 8, TT], F32)
    r_h = Res()
    hn = ph.sbuf([128, 8, TT], BF16)
    r_hn = Res()
    if not first:
        yg = ph.sbuf([128, 22, TT], BF16)
        r_yg = Res()
        pt = ph.sbuf([128, 2, TT], BF16)
        r_pt = Res()
        et = ph.sbuf([128, 8, TT], F32)
        r_et = Res()
        wrot = Rot(ph, 4, [128, 4096], BF16)
        sarot = Rot(ph, 3, [128, 512], F32)

    hT_in = t["hT"].rearrange("(c p) t -> p c t", p=128)
    for tt in range(ntok // TT):
        tsl = slice(tt * TT, (tt + 1) * TT)
        ph.dma("sp", h[:], hT_in[:, :, tsl], writes=[r_h])
        if first:
            for st in range(TT // ST):
                emit_rmsnorm_fm(ph, h, r_h, 8, st * ST, ST, vec, r_vec, hn, r_hn, env, D, gain_col0=24)
            ph.dma("sp", t["hnT_out"].rearrange("(c p) t -> p c t", p=128)[:, :, tsl], hn[:], reads=[r_hn])
            continue
        ph.dma("sp", yg[:, :kc_out, :], t["yT"].rearrange("(c p) t -> p c t", p=128)[:, :, tsl], writes=[r_yg])
        ph.dma("pool", pt[:], t["pT"].rearrange("(c p) t -> p c t", p=128)[:, :, tsl], writes=[r_pt])
        ncol = 4096 // kc_out
        wv = t["wout"].rearrange("(c p) n -> p c n", p=128)
        for blk in range(D // ncol):
            wb, r_wb = wrot.next()
            wbv = wb[:, :kc_out * ncol].rearrange("p (c n) -> p c n", c=kc_out)
            ph.dma("pool", wbv, wv[:, :, blk * ncol:(blk + 1) * ncol], writes=[r_wb])
            for st in range(TT // ST):
                for fi in range(ncol // 128):
                    f = blk * (ncol // 128) + fi
                    ps, r_ps = psrot.next()
                    for k in range(kc_out):
                        ph.op("pe", lambda e, ps=ps, wbv=wbv, fi=fi, k=k, st=st: e.matmul(
                            ps[:], wbv[:, k, fi * 128:(fi + 1) * 128], yg[:, k, st * ST:(st + 1) * ST],
                            start=(k == 0), stop=(k == kc_out - 1)), reads=[r_wb, r_yg], writes=[r_ps])
                    ph.op("dve", lambda e, ps=ps, f=f, st=st: e.tensor_tensor(
                        out=h[:, f, st * ST:(st + 1) * ST], in0=h[:, f, st * ST:(st + 1) * ST], in1=ps[:], op=ALU.add),
                        reads=[r_ps, r_h], writes=[r_h])
        for st in range(TT // ST):
            emit_rmsnorm_fm(ph, h, r_h, 8, st * ST, ST, vec, r_vec, hn, r_hn, env, D, gain_col0=0)
        wv = t["win"].rearrange("(c p) n -> p c n", p=128)
        for blk in range(HID // 256):
            wb, r_wb = wrot.next()
            wbv = wb[:].rearrange("p (c n) -> p c n", c=8)
            ph.dma("pool", wbv[:, :, 0:256], wv[:, :, blk * 256:(blk + 1) * 256], writes=[r_wb])
            ph.dma("pool", wbv[:, :, 256:512], wv[:, :, HID + blk * 256:HID + (blk + 1) * 256], writes=[r_wb])
            for st in range(TT // ST):
                for jj in range(2):
                    j = blk * 2 + jj
                    pa, r_pa = psrot.next()
                    pb, r_pb = psrot.next()
                    for c in range(8):
                        ph.op("pe", lambda e, pa=pa, wbv=wbv, jj=jj, c=c, st=st: e.matmul(
                            pa[:], wbv[:, c, jj * 128:(jj + 1) * 128], hn[:, c, st * ST:(st + 1) * ST],
                            start=(c == 0), stop=(c == 7)), reads=[r_wb, r_hn], writes=[r_pa])
                    for c in range(8):
                        ph.op("pe", lambda e, pb=pb, wbv=wbv, jj=jj, c=c, st=st: e.matmul(
                            pb[:], wbv[:, c, 256 + jj * 128:256 + (jj + 1) * 128], hn[:, c, st * ST:(st + 1) * ST],
                            start=(c == 0), stop=(c == 7)), reads=[r_wb, r_hn], writes=[r_pb])
                    sa, r_sa = sarot.next()
                    ph.op("act", lambda e, sa=sa, pa=pa: e.activation(out=sa[:], in_=pa[:], func=AF.Silu), reads=[r_pa], writes=[r_sa])
                    ph.op("dve", lambda e, sa=sa, pb=pb, j=j, st=st: e.tensor_tensor(
                        out=yg[:, j, st * ST:(st + 1) * ST], in0=sa[:], in1=pb[:], op=ALU.mult),
                        reads=[r_sa, r_pb], writes=[r_yg])
        wv = t["wo"].rearrange("(c p) n -> p c n", p=128)
        for f in range(8):
            wb, r_wb = wrot.next()
            wbv = wb[:, :22 * 128].rearrange("p (c n) -> p c n", c=22)
            ph.dma("pool", wbv, wv[:, :, f * 128:(f + 1) * 128], writes=[r_wb])
            for st in range(TT // ST):
                ps, r_ps = psrot.next()
                for k in range(22):
                    ph.op("pe", lambda e, ps=ps, wbv=wbv, k=k, st=st: e.matmul(
                        ps[:], wbv[:, k, :], yg[:, k, st * ST:(st + 1) * ST], start=(k == 0), stop=(k == 21)),
                        reads=[r_wb, r_yg], writes=[r_ps])
                ph.op("dve", lambda e, ps=ps, f=f, st=st: e.tensor_tensor(
                    out=h[:, f, st * ST:(st + 1) * ST], in0=h[:, f, st * ST:(st + 1) * ST], in1=ps[:], op=ALU.add),
                    reads=[r_ps, r_h], writes=[r_h])
        for st in range(TT // ST):
            emit_rmsnorm_fm(ph, h, r_h, 8, st * ST, ST, vec, r_vec, hn, r_hn, env, D, gain_col0=8)
        wb, r_wb = wrot.next()
        wbv = wb[:, :2048].rearrange("p (c n) -> p c n", c=2)
        ph.dma("pool", wbv, t["wp"].rearrange("(c p) n -> p c n", p=128), writes=[r_wb])
        for st in range(TT // ST):
            for f in range(8):
                ps, r_ps = psrot.next()
                for k in range(2):
                    ph.op("pe", lambda e, ps=ps, wbv=wbv, k=k, f=f, st=st: e.matmul(
                        ps[:], wbv[:, k, f * 128:(f + 1) * 128], pt[:, k, st * ST:(st + 1) * ST], start=(k == 0), stop=(k == 1)),
                        reads=[r_wb, r_pt], writes=[r_ps])
                ph.op("act", lambda e, ps=ps, f=f, st=st: e.activation(out=et[:, f, st * ST:(st + 1) * ST], in_=ps[:], func=AF.Copy),
                      reads=[r_ps], writes=[r_et])
            emit_rmsnorm_fm(ph, et, r_et, 8, st * ST, ST, vec, r_vec, et, r_et, env, D, gain_col0=16)
        wv = t["wg"].rearrange("(c p) n -> p c n", p=128)
        for blk in range(2):
            wb, r_wb = wrot.next()
            wbv = wb[:].rearrange("p (c n) -> p c n", c=8)
            ph.dma("pool", wbv, wv[:, :, blk * 512:(blk + 1) * 512], writes=[r_wb])
            for st in range(TT // ST):
                for fi in range(4):
                    f = blk * 4 + fi
                    ps, r_ps = psrot.next()
                    for c in range(8):
                        ph.op("pe", lambda e, ps=ps, wbv=wbv, fi=fi, c=c, st=st: e.matmul(
                            ps[:], wbv[:, c, fi * 128:(fi + 1) * 128], hn[:, c, st * ST:(st + 1) * ST],
                            start=(c == 0), stop=(c == 7)), reads=[r_wb, r_hn], writes=[r_ps])
                    sa, r_sa = sarot.next()
                    ph.op("act", lambda e, sa=sa, ps=ps: e.activation(out=sa[:], in_=ps[:], func=AF.Tanh, scale=0.5),
                          reads=[r_ps], writes=[r_sa])
                    ph.op("dve", lambda e, sa=sa: e.tensor_scalar(out=sa[:], in0=sa[:], scalar1=0.5, scalar2=0.5, op0=ALU.mult, op1=ALU.add),
                          reads=[r_sa], writes=[r_sa])
                    ph.op("dve", lambda e, sa=sa, f=f, st=st: e.tensor_tensor(
                        out=sa[:], in0=sa[:], in1=et[:, f, st * ST:(st + 1) * ST], op=ALU.mult), reads=[r_sa, r_et], writes=[r_sa])
                    ph.op("dve", lambda e, sa=sa, f=f, st=st: e.tensor_tensor(
                        out=h[:, f, st * ST:(st + 1) * ST], in0=h[:, f, st * ST:(st + 1) * ST], in1=sa[:], op=ALU.add),
                        reads=[r_sa, r_h], writes=[r_h])
        ph.dma("sp", t["hT_out"].rearrange("(c p) t -> p c t", p=128)[:, :, tsl], h[:], reads=[r_h])
        if not last:
            for st in range(TT // ST):
                emit_rmsnorm_fm(ph, h, r_h, 8, st * ST, ST, vec, r_vec, hn, r_hn, env, D, gain_col0=24)
            ph.dma("sp", t["hnT_out"].rearrange("(c p) t -> p c t", p=128)[:, :, tsl], hn[:], reads=[r_hn])
    ph.end()


def ret_gamma(h):
    return float(np.float32(1.0) - np.float32(2.0) ** np.float32(-5.0 - h))


def ret_consts(r):
    import ml_dtypes
    pos = np.arange(512) % 128
    dq = np.zeros((128, 2, 512), np.float32)
    dkt = np.zeros((128, 2, 512), np.float32)
    dkm = np.zeros((128, 2), np.float32)
    for hh in range(2):
        lg = np.log(np.float64(ret_gamma(2 * r + hh)))
        dq[:, hh, :] = np.exp((pos + 1.0) * lg)[None, :]
        dkt[:, hh, :] = (np.exp(-(pos + 1.0) * lg) * (256.0 ** -0.5))[None, :]
        dkm[:, hh] = np.exp((127.0 - np.arange(128)) * lg) * (256.0 ** -0.5)
    m = np.arange(128)
    mask = (m[:, None] <= m[None, :]).astype(np.float32)
    ident = np.eye(128, dtype=np.float32).astype(ml_dtypes.bfloat16)
    return dict(dq=dq, dkt=dkt, dkm=dkm, mask=mask, ident=ident)


def phase_retention(ctx, t, r_role_gammas, ntok):
    ph = Phase(ctx)
    NT = ntok // 512
    gC = [g ** 128 for g in r_role_gammas]
    w = ph.sbuf([128, 8, 3072], BF16)
    r_w = Res()
    wv = t["w"].rearrange("(c p) n -> p c n", p=128)
    for hh in range(2):
        for part in range(3):
            sl = slice(hh * 1536 + part * 512, hh * 1536 + (part + 1) * 512)
            ph.dma("pool", w[:, :, sl], wv[:, :, sl], writes=[r_w])
    dq = ph.sbuf([128, 2, 512], F32); r_c = Res()
    dkt = ph.sbuf([128, 2, 512], F32)
    dkm = ph.sbuf([128, 2], F32)
    mask = ph.sbuf([128, 128], F32)
    ident = ph.sbuf([128, 128], BF16)
    mh = ph.sbuf([128, 1], F32)
    ph.dma("sp", dq[:], t["dq"], writes=[r_c])
    ph.dma("sp", dkt[:], t["dkt"], writes=[r_c])
    ph.dma("sp", dkm[:], t["dkm"], writes=[r_c])
    ph.dma("sp", mask[:], t["mask"], writes=[r_c])
    ph.dma("sp", ident[:], t["ident"], writes=[r_c])
    epsc = ph.sbuf([128, 1], F32)
    ph.op("pool", lambda e: e.memset(epsc[:], EPS), writes=[r_c])

    hnrot = Rot(ph, 2, [128, 8, 512], BF16)
    qrot = Rot(ph, 2, [128, 2, 512], BF16)
    ktrot = Rot(ph, 2, [128, 2, 512], BF16)
    kmrot = Rot(ph, 2, [128, 4, 256], BF16)
    vrot = Rot(ph, 2, [128, 4, 512], BF16)
    sgrot = Rot(ph, 2, [128, 4, 512], BF16)
    R32 = [ph.sbuf([128, 2, 512], F32) for _ in range(2)]
    r_R32 = [Res(), Res()]
    Rb = [ph.sbuf([128, 2, 512], BF16) for _ in range(2)]
    r_Rb = [Res(), Res()]
    attrot = Rot(ph, 2, [128, 128], BF16)
    ygrot = Rot(ph, 2, [128, 512], BF16)
    ygtrot = Rot(ph, 2, [128, 4, 512], BF16)
    junk = ph.sbuf([128, 512], BF16); r_junk = Res()
    ssrot = Rot(ph, 4, [128, 1], F32)
    projrot = Rot(ph, 2, [128, 512], F32, psum=True)
    attp = ph.psum([128, 512], F32)
    attps = [(attp[:, i * 128:(i + 1) * 128], Res()) for i in range(4)]
    yrot = Rot(ph, 2, [128, 512], F32, psum=True)
    rprot = Rot(ph, 2, [128, 512], F32, psum=True)
    trp = ph.psum([128, 1024], BF16)
    trps = [(trp[:, i * 512:(i + 1) * 512], Res()) for i in range(2)]
    cnt = {"att": 0, "tr": 0}

    hn_v = t["hnT"].rearrange("(c p) t -> p c t", p=128)
    y_v = t["yT"].rearrange("(c p) t -> p c t", p=128)

    def stage_a(tt, hh, hn, r_hn):
        q, r_q = qrot.next(); kt, r_kt = ktrot.next(); km, r_km = kmrot.next(); v, r_v = vrot.next(); sg, r_sg = sgrot.next()
        base = hh * 1536
        for dc in range(2):
            ps, r_ps = projrot.next()
            for c in range(8):
                ph.op("pe", lambda e, ps=ps, c=c, dc=dc: e.matmul(ps[:], w[:, c, base + dc * 128: base + (dc + 1) * 128], hn[:, c, :],
                                                                 start=(c == 0), stop=(c == 7)), reads=[r_w, r_hn], writes=[r_ps])
            ph.op("dve", lambda e, ps=ps, dc=dc: e.tensor_tensor(out=q[:, dc, :], in0=ps[:], in1=dq[:, hh, :], op=ALU.mult),
                  reads=[r_ps, r_c], writes=[r_q])
        for dc in range(2):
            ps, r_ps = projrot.next()
            for c in range(8):
                ph.op("pe", lambda e, ps=ps, c=c, dc=dc: e.matmul(ps[:], w[:, c, base + 256 + dc * 128: base + 256 + (dc + 1) * 128], hn[:, c, :],
                                                                 start=(c == 0), stop=(c == 7)), reads=[r_w, r_hn], writes=[r_ps])
            ph.op("dve", lambda e, ps=ps, dc=dc: e.tensor_tensor(out=kt[:, dc, :], in0=ps[:], in1=dkt[:, hh, :], op=ALU.mult),
                  reads=[r_ps, r_c], writes=[r_kt])
        for ci in range(4):
            ps, r_ps = projrot.next()
            for c in range(8):
                ph.op("pe", lambda e, ps=ps, c=c, ci=ci: e.matmul(ps[:, :256], hn[:, c, ci * 128:(ci + 1) * 128], w[:, c, base + 256: base + 512],
                                                                 start=(c == 0), stop=(c == 7)), reads=[r_w, r_hn], writes=[r_ps])
            ph.op("act", lambda e, ps=ps, ci=ci: e.activation(out=km[:, ci, :], in_=ps[:, :256], func=AF.Identity, scale=dkm[:, hh:hh + 1]),
                  reads=[r_ps, r_c], writes=[r_km])
            ps, r_ps = projrot.next()
            for c in range(8):
                ph.op("pe", lambda e, ps=ps, c=c, ci=ci: e.matmul(ps[:], hn[:, c, ci * 128:(ci + 1) * 128], w[:, c, base + 512: base + 1024],
                                                                 start=(c == 0), stop=(c == 7)), reads=[r_w, r_hn], writes=[r_ps])
            ph.op("act", lambda e, ps=ps, ci=ci: e.activation(out=v[:, ci, :], in_=ps[:], func=AF.Copy), reads=[r_ps], writes=[r_v])
            ps, r_ps = projrot.next()
            for c in range(8):
                ph.op("pe", lambda e, ps=ps, c=c, ci=ci: e.matmul(ps[:], hn[:, c, ci * 128:(ci + 1) * 128], w[:, c, base + 1024: base + 1536],
                                                                 start=(c == 0), stop=(c == 7)), reads=[r_w, r_hn], writes=[r_ps])
            ph.op("act", lambda e, ps=ps, ci=ci: e.activation(out=sg[:, ci, :], in_=ps[:], func=AF.Silu), reads=[r_ps], writes=[r_sg])
        return (q, r_q, kt, r_kt, km, r_km, v, r_v, sg, r_sg)

    def stage_b(tt, hh, bufs):
        q, r_q, kt, r_kt, km, r_km, v, r_v, sg, r_sg = bufs
        ygt, r_ygt = ygtrot.next()
        for ci in range(4):
            first = (tt == 0 and ci == 0)
            csl = slice(ci * 128, (ci + 1) * 128)
            ap_, r_ap = attps[cnt["att"] % 4]; cnt["att"] += 1
            for dc in range(2):
                ph.op("pe", lambda e, ap_=ap_, dc=dc, csl=csl: e.matmul(ap_, kt[:, dc, csl], q[:, dc, csl], start=(dc == 0), stop=(dc == 1)),
                      reads=[r_kt, r_q], writes=[r_ap])
            at, r_at = attrot.next()
            ph.op("dve", lambda e, at=at, ap_=ap_: e.tensor_tensor(out=at[:], in0=ap_, in1=mask[:], op=ALU.mult),
                  reads=[r_ap, r_c], writes=[r_at])
            yp, r_yp = yrot.next()
            ph.op("pe", lambda e, yp=yp, at=at, ci=ci, first=first: e.matmul(yp[:], at[:], v[:, ci, :], start=True, stop=first),
                  reads=[r_at, r_v], writes=[r_yp])
            if not first:
                for dc in range(2):
                    ph.op("pe", lambda e, yp=yp, dc=dc, csl=csl: e.matmul(yp[:], q[:, dc, csl], Rb[hh][:, dc, :], start=False, stop=(dc == 1)),
                          reads=[r_q, r_Rb[hh]], writes=[r_yp])
            for dc in range(2):
                rp, r_rp = rprot.next()
                ph.op("pe", lambda e, rp=rp, dc=dc, ci=ci: e.matmul(rp[:], km[:, ci, dc * 128:(dc + 1) * 128], v[:, ci, :], start=True, stop=True),
                      reads=[r_km, r_v], writes=[r_rp])
                if first:
                    ph.op("dve", lambda e, rp=rp, dc=dc: e.tensor_copy(out=R32[hh][:, dc, :], in_=rp[:]), reads=[r_rp], writes=[r_R32[hh]])
                else:
                    ph.op("dve", lambda e, rp=rp, dc=dc: e.scalar_tensor_tensor(
                        out=R32[hh][:, dc, :], in0=R32[hh][:, dc, :], scalar=gC[hh], in1=rp[:], op0=ALU.mult, op1=ALU.add),
                        reads=[r_rp, r_R32[hh]], writes=[r_R32[hh]])
            ph.op("act", lambda e: e.activation(out=Rb[hh][:], in_=R32[hh][:], func=AF.Copy), reads=[r_R32[hh]], writes=[r_Rb[hh]])
            ss, r_ss = ssrot.next()
            ph.op("act", lambda e, yp=yp, ss=ss: e.activation(out=junk[:], in_=yp[:], func=AF.Square, accum_out=ss[:]),
                  reads=[r_yp], writes=[r_junk, r_ss])
            ph.op("act", lambda e, ss=ss: e.activation(out=ss[:], in_=ss[:], func=AF.Ln, scale=1.0 / 512, bias=epsc[:, 0:1]),
                  reads=[r_ss, r_c], writes=[r_ss])
            ph.op("act", lambda e, ss=ss: e.activation(out=ss[:], in_=ss[:], func=AF.Exp, scale=-0.5), reads=[r_ss], writes=[r_ss])
            yg, r_yg = ygrot.next()
            ph.op("dve", lambda e, yg=yg, yp=yp, ss=ss, ci=ci: e.scalar_tensor_tensor(
                out=yg[:], in0=yp[:], scalar=ss[:, 0:1], in1=sg[:, ci, :], op0=ALU.mult, op1=ALU.mult),
                reads=[r_yp, r_ss, r_sg], writes=[r_yg])
            tp, r_tp = trps[cnt["tr"] % 2]; cnt["tr"] += 1
            for ec in range(4):
                ph.op("pe", lambda e, tp=tp, yg=yg, ec=ec: e.transpose(tp[:, ec * 128:(ec + 1) * 128], yg[:, ec * 128:(ec + 1) * 128], ident[:]),
                      reads=[r_yg, r_c], writes=[r_tp])
            ph.op("act", lambda e, tp=tp, ygt=ygt, csl=csl: e.activation(
                out=ygt[:, :, csl], in_=tp.rearrange("p (c n) -> p c n", c=4), func=AF.Copy), reads=[r_tp], writes=[r_ygt])
        ph.dma("sp", y_v[:, hh * 4:(hh + 1) * 4, tt * 512:(tt + 1) * 512], ygt[:], reads=[r_ygt])

    pend = None
    for tt in range(NT):
        hn, r_hn = hnrot.next()
        ph.dma("sp", hn[:], hn_v[:, :, tt * 512:(tt + 1) * 512], writes=[r_hn])
        for hh in range(2):
            bufs = stage_a(tt, hh, hn, r_hn)
            if pend is not None:
                stage_b(*pend)
            pend = (tt, hh, bufs)
    stage_b(*pend)
    ph.end()


DIL_GROUPS = ((128, 1), (512, 4), (2048, 16))
NEGB = -1e30


def alibi_slopes8():
    ratio = 2.0 ** (-8.0 / 8)
    return [ratio ** (h + 1) for h in range(8)]


def dil_consts():
    import ml_dtypes
    sl = alibi_slopes8()
    bias = np.zeros((128, 24, 256), np.float32)
    kj = np.arange(128)[:, None]
    qi = np.arange(128)[None, :]
    for g, (win, dil) in enumerate(DIL_GROUPS):
        for h in range(8):
            for part, off in ((0, 0), (1, 128)):
                rel = off + qi - kj
                ok = (rel >= 0) & (rel <= win // dil)
                bias[:, g * 8 + h, part * 128:(part + 1) * 128] = np.where(ok, -(sl[h] * dil) * rel, NEGB)
    return dict(bias=bias.astype(ml_dtypes.bfloat16), ident=np.eye(128, dtype=np.float32).astype(ml_dtypes.bfloat16))


def phase_dilated(ctx, t, ntok):
    ph = Phase(ctx)
    NT = ntok // 512
    bias = ph.sbuf([128, 24, 256], BF16); r_c = Res()
    ident = ph.sbuf([128, 128], BF16)
    onesb = ph.sbuf([128, 128], BF16)
    ones32 = ph.sbuf([128, 128], F32)
    mh = ph.sbuf([128, 512], F32)
    gqk = ph.sbuf([128, 2], F32)
    ph.dma("sp", bias[:], t["bias"], writes=[r_c])
    ph.dma("sp", ident[:], t["ident"], writes=[r_c])
    ph.dma("sp", gqk[:, 0:1], t["gq"], writes=[r_c])
    ph.dma("sp", gqk[:, 1:2], t["gk"], writes=[r_c])
    ph.op("pool", lambda e: e.memset(onesb[:], 1.0), writes=[r_c])
    ph.op("pool", lambda e: e.memset(ones32[:], 1.0), writes=[r_c])
    epsc = ph.sbuf([128, 1], F32)
    ph.op("pool", lambda e: e.memset(epsc[:], EPS), writes=[r_c])
    ph.op("dve", lambda e: e.tensor_scalar(out=gqk[:, 0:1], in0=gqk[:, 0:1], scalar1=128.0 ** -0.5, scalar2=None, op0=ALU.mult),
          reads=[r_c], writes=[r_c])

    whrot = Rot(ph, 2, [128, 8, 7, 128], BF16)
    hnrot = Rot(ph, 2, [128, 8, 512], BF16)
    qd = ph.sbuf([128, ntok], BF16); r_qd = Res()
    kd = ph.sbuf([128, ntok], BF16); r_kd = Res()
    vd = ph.sbuf([128, ntok], BF16); r_vd = Res()
    uz = ph.sbuf([128, 2, ntok], F32); r_uz = Res()
    ob, r_ob = qd, r_qd
    sqrot = Rot(ph, 2, [128, 512], F32)
    tmprot = Rot(ph, 4, [128, 512], F32)
    ptrot = Rot(ph, 3, [128, 256], BF16)
    vtrot = Rot(ph, 3, [128, 128], BF16)
    projrot = Rot(ph, 2, [128, 512], F32, psum=True)
    nrmrot = Rot(ph, 1, [128, 512], F32, psum=True)
    scrot = Rot(ph, 2, [128, 512], F32, psum=True)
    trp = ph.psum([128, 1024], BF16)
    trps = [(trp[:, i * 128:(i + 1) * 128], Res()) for i in range(4)]
    uzrot = Rot(ph, 2, [128, 512], F32, psum=True)
    cnt = {"tr": 0}

    hn_v = t["hnT"].rearrange("(c p) t -> p c t", p=128)
    wv = t["w"].rearrange("(c p) n -> p c n", p=128)
    y_v = t["yT"].rearrange("(c p) t -> p c t", p=128)

    def dview(buf, dil, l0, nl):
        if dil == 1:
            return buf[:, l0:l0 + nl]
        return buf[:].rearrange("p (r l) -> p r l", r=dil)[:, :, l0:l0 + nl].rearrange("p r l -> p l r")

    def pview(ps, dil):
        if dil == 1:
            return ps[:]
        return ps[:].rearrange("p (l r) -> p l r", r=dil)

    for h in range(8):
        wh, r_wh = whrot.next()
        for i in range(7):
            col = (i * 1024 + h * 128) if i < 6 else (6144 + h * 128)
            ph.dma("pool", wh[:, :, i, :], wv[:, :, col:col + 128], writes=[r_wh])
        ph.op("pool", lambda e: e.memset(uz[:], 0.0), writes=[r_uz])
        for g, (win, dil) in enumerate(DIL_GROUPS):
            L = ntok // dil
            nblk = L // 128
            for tt in range(NT):
                hn, r_hn = hnrot.next()
                ph.dma("sp", hn[:], hn_v[:, :, tt * 512:(tt + 1) * 512], writes=[r_hn])
                l0, nl = tt * 512 // dil, 512 // dil
                for which, dst, r_dst in ((0, qd, r_qd), (1, kd, r_kd)):
                    ps, r_ps = projrot.next()
                    wi = which * 3 + g
                    for c in range(8):
                        ph.op("pe", lambda e, ps=ps, c=c, wi=wi, hn=hn, wh=wh: e.matmul(ps[:], wh[:, c, wi, :], hn[:, c, :], start=(c == 0), stop=(c == 7)),
                              reads=[r_wh, r_hn], writes=[r_ps])
                    sq, r_sq = sqrot.next()
                    ph.op("act", lambda e, sq=sq, ps=ps: e.activation(out=sq[:], in_=ps[:], func=AF.Square), reads=[r_ps], writes=[r_sq])
                    pn, r_pn = nrmrot.next()
                    ph.op("pe", lambda e, pn=pn, sq=sq: e.matmul(pn[:], ones32[:], sq[:], start=True, stop=True), reads=[r_sq, r_c], writes=[r_pn])
                    tmp, r_tmp = tmprot.next()
                    ph.op("act", lambda e, tmp=tmp, pn=pn: e.activation(out=tmp[:], in_=pn[:], func=AF.Ln, scale=1.0 / 128, bias=epsc[:, 0:1]),
                          reads=[r_pn, r_c], writes=[r_tmp])
                    rstd, r_rstd = tmprot.next()
                    ph.op("act", lambda e, tmp=tmp, rstd=rstd: e.activation(out=rstd[:], in_=tmp[:], func=AF.Exp, scale=-0.5),
                          reads=[r_tmp], writes=[r_rstd])
                    ph.op("dve", lambda e, ps=ps, rstd=rstd, dst=dst, which=which, dil=dil, l0=l0, nl=nl: e.scalar_tensor_tensor(
                        out=dview(dst, dil, l0, nl), in0=pview(ps, dil), scalar=gqk[:, which:which + 1], in1=pview(rstd, dil),
                        op0=ALU.mult, op1=ALU.mult), reads=[r_ps, r_rstd, r_c], writes=[r_dst])
                ps, r_ps = projrot.next()
                for c in range(8):
                    ph.op("pe", lambda e, ps=ps, c=c, hn=hn, wh=wh: e.matmul(ps[:], wh[:, c, 6, :], hn[:, c, :], start=(c == 0), stop=(c == 7)),
                          reads=[r_wh, r_hn], writes=[r_ps])
                ph.op("act", lambda e, ps=ps, dil=dil, l0=l0, nl=nl: e.activation(out=dview(vd, dil, l0, nl), in_=pview(ps, dil), func=AF.Copy),
                      reads=[r_ps], writes=[r_vd])
            uzv = uz[:].rearrange("p a (l r) -> p a r l", r=dil) if dil > 1 else None
            for r in range(dil):
                base = r * L
                cur = None
                for j in range(nblk):
                    nq = 256 if j < nblk - 1 else 128
                    ksl = slice(base + j * 128, base + (j + 1) * 128)
                    sc, r_sc = scrot.next()
                    ph.op("pe", lambda e, sc=sc, ksl=ksl, nq=nq, base=base, j=j: e.matmul(
                        sc[:, :nq], kd[:, ksl], qd[:, base + j * 128: base + j * 128 + nq], start=True, stop=False),
                        reads=[r_kd, r_qd], writes=[r_sc])
                    ph.op("pe", lambda e, sc=sc, nq=nq, g=g, h=h: e.matmul(sc[:, :nq], ident[:], bias[:, g * 8 + h, :nq], start=False, stop=True),
                          reads=[r_c], writes=[r_sc])
                    pt, r_pt = ptrot.next()
                    ph.op("act", lambda e, pt=pt, sc=sc, nq=nq: e.activation(out=pt[:, :nq], in_=sc[:, :nq], func=AF.Exp), reads=[r_sc], writes=[r_pt])
                    tp, r_tp = trps[cnt["tr"] % 4]; cnt["tr"] += 1
                    ph.op("pe", lambda e, tp=tp, ksl=ksl: e.transpose(tp, vd[:, ksl], ident[:]), reads=[r_vd, r_c], writes=[r_tp])
                    vt, r_vt = vtrot.next()
                    ph.op("dve", lambda e, vt=vt, tp=tp: e.tensor_copy(out=vt[:], in_=tp), reads=[r_tp], writes=[r_vt])
                    if cur is None:
                        cur = uzrot.next()
                    cu, r_cu = cur
                    ph.op("pe", lambda e, cu=cu, vt=vt, pt=pt, j=j: e.matmul(cu[:, 0:128], vt[:], pt[:, 0:128], start=(j == 0), stop=True, skip_group_check=True),
                          reads=[r_vt, r_pt], writes=[r_cu])
                    ph.op("pe", lambda e, cu=cu, pt=pt, j=j: e.matmul(cu[:, 128:256], onesb[:], pt[:, 0:128], start=False, stop=True, skip_group_check=True),
                          reads=[r_pt, r_c], writes=[r_cu])
                    nxt = None
                    if j < nblk - 1:
                        nxt = uzrot.next()
                        nu, r_nu = nxt
                        ph.op("pe", lambda e, nu=nu, vt=vt, pt=pt: e.matmul(nu[:, 0:128], vt[:], pt[:, 128:256], start=True, stop=False, skip_group_check=True),
                              reads=[r_vt, r_pt], writes=[r_nu])
                        ph.op("pe", lambda e, nu=nu, pt=pt: e.matmul(nu[:, 128:256], onesb[:], pt[:, 128:256], start=False, stop=False, skip_group_check=True),
                              reads=[r_pt, r_c], writes=[r_nu])
                    if dil == 1:
                        dstv = uz[:, :, j * 128:(j + 1) * 128]
                    else:
                        dstv = uzv[:, :, r, j * 128:(j + 1) * 128]
                    ph.op("dve", lambda e, dstv=dstv, cu=cu: e.tensor_tensor(
                        out=dstv, in0=dstv, in1=cu[:, 0:256].rearrange("p (a n) -> p a n", a=2), op=ALU.add),
                        reads=[r_cu, r_uz], writes=[r_uz])
                    cur = nxt
        for tt in range(NT):
            tsl = slice(tt * 512, (tt + 1) * 512)
            rz, r_rz = tmprot.next()
            ph.op("dve", lambda e, rz=rz, tsl=tsl: e.reciprocal(out=rz[:], in_=uz[:, 1, tsl]), reads=[r_uz], writes=[r_rz])
            ph.op("dve", lambda e, rz=rz, tsl=tsl: e.tensor_tensor(out=ob[:, tsl], in0=uz[:, 0, tsl], in1=rz[:], op=ALU.mult),
                  reads=[r_uz, r_rz], writes=[r_ob])
        ph.dma("sp", y_v[:, h, :], ob[:], reads=[r_ob])
    ph.end()


def diff_lambda_init(layer_idx):
    return 0.8 - 0.6 * math.exp(-0.3 * layer_idx)


def diff_consts(ntok):
    import ml_dtypes
    sl = alibi_slopes8()
    col = np.arange(512)
    qaug = np.zeros((3, 8, 512), np.float32)
    kaug = np.zeros((3, 8, ntok), np.float32)
    kj = np.arange(ntok) % 128
    for i in range(8):
        qaug[0, i] = -sl[i] * 128.0 * (col // 128)
        qaug[1, i] = -sl[i] * (col % 128)
        qaug[2, i] = 1.0
        kaug[0, i] = 1.0
        kaug[1, i] = 1.0
        kaug[2, i] = sl[i] * kj
    maskb = np.zeros((128, 4, 512), np.float32)
    kk = np.arange(128)[:, None]
    for d in range(4):
        maskb[:, d, :] = np.where(128 * d + kk <= col[None, :], 0.0, NEGB)
    bf = ml_dtypes.bfloat16
    return dict(qaug=qaug.astype(bf), kaug=kaug.astype(bf), maskb=maskb.astype(bf), ident=np.eye(128, dtype=np.float32).astype(bf))


def phase_diff(ctx, t, ntok, layer_idx):
    ph = Phase(ctx)
    NT = ntok // 512
    lam_init = diff_lambda_init(layer_idx)
    sl = alibi_slopes8()
    r_c = Res()
    maskb = ph.sbuf([128, 4, 512], BF16)
    ident = ph.sbuf([128, 128], BF16)
    ones32 = ph.sbuf([128, 128], F32)
    mh = ph.sbuf([128, 512], F32)
    gqk = ph.sbuf([64, 2], F32)
    lamv = ph.sbuf([64, 4], F32)
    gsub = ph.sbuf([128, 1], F32)
    lamw = ph.sbuf([128, 4], F32)
    ph.dma("sp", maskb[:], t["maskb"], writes=[r_c])
    ph.dma("sp", ident[:], t["ident"], writes=[r_c])
    ph.dma("sp", gqk[:, 0:1], t["gq"], writes=[r_c])
    ph.dma("sp", gqk[:, 1:2], t["gk"], writes=[r_c])
    ph.dma("sp", lamv[:], t["lam"], writes=[r_c])
    ph.dma("sp", gsub[:], t["gsub"], writes=[r_c])
    ph.op("pool", lambda e: e.memset(ones32[:], 1.0), writes=[r_c])
    epsc = ph.sbuf([128, 1], F32)
    ph.op("pool", lambda e: e.memset(epsc[:], EPS), writes=[r_c])
    ph.op("dve", lambda e: e.tensor_scalar(out=gqk[:, 0:1], in0=gqk[:, 0:1], scalar1=64.0 ** -0.5, scalar2=None, op0=ALU.mult),
          reads=[r_c], writes=[r_c])
    ph.op("dve", lambda e: e.tensor_scalar(out=gsub[:], in0=gsub[:], scalar1=1.0 - lam_init, scalar2=None, op0=ALU.mult),
          reads=[r_c], writes=[r_c])
    projrot = Rot(ph, 2, [128, 512], F32, psum=True)
    nrmrot = Rot(ph, 1, [128, 512], F32, psum=True)
    scrot = Rot(ph, 2, [128, 512], F32, psum=True)
    urot = Rot(ph, 2, [128, 512], F32, psum=True)
    trp = ph.psum([128, 1024], BF16)
    trps = [(trp[:, i * 512:(i + 1) * 512], Res()) for i in range(2)]
    ph.op("dve", lambda e: e.tensor_tensor(out=lamv[:, 0:1], in0=lamv[:, 0:1], in1=lamv[:, 1:2], op=ALU.mult), reads=[r_c], writes=[r_c])
    ph.op("dve", lambda e: e.tensor_tensor(out=lamv[:, 1:2], in0=lamv[:, 2:3], in1=lamv[:, 3:4], op=ALU.mult), reads=[r_c], writes=[r_c])
    pn, r_pn = nrmrot.next()
    ph.op("pe", lambda e: e.matmul(pn[:, 0:2], ones32[0:64, :], lamv[:, 0:2], start=True, stop=True), reads=[r_c], writes=[r_pn])
    ph.op("act", lambda e: e.activation(out=lamw[:, 0:2], in_=pn[:, 0:2], func=AF.Exp), reads=[r_pn], writes=[r_c])
    ph.op("dve", lambda e: e.tensor_tensor(out=lamw[:, 2:3], in0=lamw[:, 1:2], in1=lamw[:, 0:1], op=ALU.subtract), reads=[r_c], writes=[r_c])
    ph.op("dve", lambda e: e.tensor_scalar(out=lamw[:, 2:3], in0=lamw[:, 2:3], scalar1=-lam_init, scalar2=None, op0=ALU.add),
          reads=[r_c], writes=[r_c])

    whrot = Rot(ph, 2, [128, 8, 384], BF16)
    hnrot = Rot(ph, 2, [128, 8, 512], BF16)
    ka = [ph.sbuf([128, ntok], BF16) for _ in range(2)]
    r_ka = [[Res() for _ in range(NT)] for _ in range(2)]
    r_kaug = Res()
    vtm = ph.sbuf([128, ntok // 128, 128], BF16)
    r_vtm = [Res() for _ in range(NT)]
    qa = [[ph.sbuf([128, 512], BF16) for _ in range(2)] for _ in range(2)]
    r_qa = [[Res() for _ in range(2)] for _ in range(2)]
    vtrot = Rot(ph, 2, [128, 512], BF16)
    sqrot = Rot(ph, 2, [128, 512], F32)
    tmprot = Rot(ph, 4, [128, 512], F32)
    ptrot = Rot(ph, 4, [128, 512], BF16)
    zD = Rot(ph, 2, [128, 512], F32)
    zP = Rot(ph, 2, [128, 512], F32)
    o1rot = Rot(ph, 2, [128, 512], F32)
    outrot = Rot(ph, 2, [128, 512], BF16)
    cnt = {"tr": 0}

    hn_v = t["hnT"].rearrange("(c p) t -> p c t", p=128)
    wv = t["w"].rearrange("(c p) n -> p c n", p=128)
    y_v = t["yT"].rearrange("(c p) t -> p c t", p=128)

    def project(i, tt, wh, r_wh):
        hn, r_hn = hnrot.next()
        ph.dma("sp", hn[:], hn_v[:, :, tt * 512:(tt + 1) * 512], writes=[r_hn])
        tsl = slice(tt * 512, (tt + 1) * 512)
        for hd in range(2):
            for which in range(2):
                ps, r_ps = projrot.next()
                c0 = which * 128 + hd * 64
                for c in range(8):
                    ph.op("pe", lambda e, ps=ps, c=c, c0=c0, hn=hn, wh=wh: e.matmul(ps[0:64, :], wh[:, c, c0:c0 + 64], hn[:, c, :], start=(c == 0), stop=(c == 7)),
                          reads=[r_wh, r_hn], writes=[r_ps])
                sq, r_sq = sqrot.next()
                ph.op("act", lambda e, sq=sq, ps=ps: e.activation(out=sq[0:64, :], in_=ps[0:64, :], func=AF.Square), reads=[r_ps], writes=[r_sq])
                pn, r_pn = nrmrot.next()
                ph.op("pe", lambda e, pn=pn, sq=sq: e.matmul(pn[0:64, :], ones32[0:64, 0:64], sq[0:64, :], start=True, stop=True), reads=[r_sq, r_c], writes=[r_pn])
                tmp, r_tmp = tmprot.next()
                ph.op("act", lambda e, tmp=tmp, pn=pn: e.activation(out=tmp[0:64, :], in_=pn[0:64, :], func=AF.Ln, scale=1.0 / 64, bias=epsc[0:64, 0:1]),
                      reads=[r_pn, r_c], writes=[r_tmp])
                rstd, r_rstd = tmprot.next()
                ph.op("act", lambda e, tmp=tmp, rstd=rstd: e.activation(out=rstd[0:64, :], in_=tmp[0:64, :], func=AF.Exp, scale=-0.5),
                      reads=[r_tmp], writes=[r_rstd])
                if which == 0:
                    dst, r_dst = qa[hd][tt % 2][0:64, :], r_qa[hd][tt % 2]
                else:
                    dst, r_dst = ka[hd][0:64, tsl], r_ka[hd][tt]
                ph.op("dve", lambda e, ps=ps, rstd=rstd, dst=dst, which=which: e.scalar_tensor_tensor(
                    out=dst, in0=ps[0:64, :], scalar=gqk[:, which:which + 1], in1=rstd[0:64, :], op0=ALU.mult, op1=ALU.mult),
                    reads=[r_ps, r_rstd, r_c], writes=[r_dst])
        ps, r_ps = projrot.next()
        for c in range(8):
            ph.op("pe", lambda e, ps=ps, c=c, hn=hn, wh=wh: e.matmul(ps[:], wh[:, c, 256:384], hn[:, c, :], start=(c == 0), stop=(c == 7)),
                  reads=[r_wh, r_hn], writes=[r_ps])
        vt, r_vt = vtrot.next()
        ph.op("act", lambda e, vt=vt, ps=ps: e.activation(out=vt[:], in_=ps[:], func=AF.Copy), reads=[r_ps], writes=[r_vt])
        tp, r_tp = trps[cnt["tr"] % 2]; cnt["tr"] += 1
        for b in range(4):
            ph.op("pe", lambda e, tp=tp, vt=vt, b=b: e.transpose(tp[:, b * 128:(b + 1) * 128], vt[:, b * 128:(b + 1) * 128], ident[:]),
                  reads=[r_vt, r_c], writes=[r_tp])
        ph.op("act", lambda e, tp=tp, tt=tt: e.activation(out=vtm[:, tt * 4:(tt + 1) * 4, :], in_=tp.rearrange("p (b n) -> p b n", b=4), func=AF.Copy),
              reads=[r_tp], writes=[r_vtm[tt]])

    def attend(i, tt):
        nkb = 4 * tt + 4
        o1 = None
        for hd in range(2):
            q, r_q = qa[hd][tt % 2], r_qa[hd][tt % 2]
            U, r_U = urot.next()
            zd, r_zd = zD.next()
            zp, r_zp = zP.next()
            for kb in range(nkb):
                diag = kb >= 4 * tt
                sc, r_sc = scrot.next()
                ph.op("pe", lambda e, sc=sc, kb=kb, q=q, hd=hd, diag=diag: e.matmul(
                    sc[:], ka[hd][0:67, kb * 128:(kb + 1) * 128], q[0:67, :], start=True, stop=(not diag)),
                    reads=[r_ka[hd][kb // 4], r_kaug, r_q], writes=[r_sc])
                if diag:
                    ph.op("pe", lambda e, sc=sc, d=kb - 4 * tt: e.matmul(sc[:], ident[:], maskb[:, d, :], start=False, stop=True),
                          reads=[r_c], writes=[r_sc])
                pt, r_pt = ptrot.next()
                cbias = -sl[i] * 128.0 * (4 * tt - kb)
                ph.op("act", lambda e, pt=pt, sc=sc, cbias=cbias: e.activation(out=pt[:], in_=sc[:], func=AF.Exp, bias=cbias),
                      reads=[r_sc], writes=[r_pt])
                ph.op("pe", lambda e, U=U, pt=pt, kb=kb, nkb=nkb: e.matmul(U[:], vtm[:, kb, :], pt[:], start=(kb == 0), stop=(kb == nkb - 1)),
                      reads=[r_vtm[kb // 4], r_pt], writes=[r_U])
                eng, z, r_z = ("dve", zd, r_zd) if kb % 2 == 0 else ("pool", zp, r_zp)
                if kb < 2:
                    ph.op(eng, lambda e, z=z, pt=pt: e.tensor_copy(out=z[:], in_=pt[:]), reads=[r_pt], writes=[r_z])
                else:
                    ph.op(eng, lambda e, z=z, pt=pt: e.tensor_tensor(out=z[:], in0=z[:], in1=pt[:], op=ALU.add), reads=[r_pt, r_z], writes=[r_z])
            pn, r_pn = nrmrot.next()
            ph.op("pe", lambda e, pn=pn, zd=zd: e.matmul(pn[:], ones32[:], zd[:], start=True, stop=False), reads=[r_zd, r_c], writes=[r_pn])
            ph.op("pe", lambda e, pn=pn, zp=zp: e.matmul(pn[:], ones32[:], zp[:], start=False, stop=True), reads=[r_zp, r_c], writes=[r_pn])
            rz, r_rz = tmprot.next()
            ph.op("dve", lambda e, rz=rz, pn=pn: e.reciprocal(out=rz[:], in_=pn[:]), reads=[r_pn], writes=[r_rz])
            if hd == 0:
                o1, r_o1 = o1rot.next()
                ph.op("dve", lambda e, o1=o1, U=U, rz=rz: e.tensor_tensor(out=o1[:], in0=U[:], in1=rz[:], op=ALU.mult), reads=[r_U, r_rz], writes=[r_o1])
            else:
                ph.op("dve", lambda e, U=U, rz=rz: e.tensor_tensor(out=rz[:], in0=U[:], in1=rz[:], op=ALU.mult), reads=[r_U, r_rz], writes=[r_rz])
                ph.op("dve", lambda e, o1=o1, rz=rz: e.scalar_tensor_tensor(out=o1[:], in0=rz[:], scalar=lamw[:, 2:3], in1=o1[:], op0=ALU.mult, op1=ALU.add),
                      reads=[r_rz, r_o1, r_c], writes=[r_o1])
        sq, r_sq = sqrot.next()
        ph.op("act", lambda e, sq=sq, o1=o1: e.activation(out=sq[:], in_=o1[:], func=AF.Square), reads=[r_o1], writes=[r_sq])
        pn, r_pn = nrmrot.next()
        ph.op("pe", lambda e, pn=pn, sq=sq: e.matmul(pn[:], ones32[:], sq[:], start=True, stop=True), reads=[r_sq, r_c], writes=[r_pn])
        tmp, r_tmp = tmprot.next()
        ph.op("act", lambda e, tmp=tmp, pn=pn: e.activation(out=tmp[:], in_=pn[:], func=AF.Ln, scale=1.0 / 128, bias=epsc[:, 0:1]),
              reads=[r_pn, r_c], writes=[r_tmp])
        rstd, r_rstd = tmprot.next()
        ph.op("act", lambda e, tmp=tmp, rstd=rstd: e.activation(out=rstd[:], in_=tmp[:], func=AF.Exp, scale=-0.5), reads=[r_tmp], writes=[r_rstd])
        ot, r_ot = outrot.next()
        ph.op("dve", lambda e, ot=ot, o1=o1, rstd=rstd: e.scalar_tensor_tensor(out=ot[:], in0=o1[:], scalar=gsub[:, 0:1], in1=rstd[:], op0=ALU.mult, op1=ALU.mult),
              reads=[r_o1, r_rstd, r_c], writes=[r_ot])
        ph.dma("sp", y_v[:, i, tt * 512:(tt + 1) * 512], ot[:], reads=[r_ot])

    for i in range(8):
        wh, r_wh = whrot.next()
        ph.dma("pool", wh[:, :, 0:128], wv[:, :, i * 128:(i + 1) * 128], writes=[r_wh])
        ph.dma("pool", wh[:, :, 128:256], wv[:, :, 1024 + i * 128:1024 + (i + 1) * 128], writes=[r_wh])
        ph.dma("pool", wh[:, :, 256:384], wv[:, :, 2048 + i * 128:2048 + (i + 1) * 128], writes=[r_wh])
        for hd in range(2):
            ph.dma("sp", ka[hd][64:67, :], t["kaug"][:, i, :], writes=[r_kaug] if hd == 1 else [r_kaug])
            for par in range(2):
                ph.dma("sp", qa[hd][par][64:67, :], t["qaug"][:, i, :], writes=[r_qa[hd][par]])
        pend = None
        for tt in range(NT):
            project(i, tt, wh, r_wh)
            if pend is not None:
                attend(i, pend)
            pend = tt
        attend(i, pend)
    ph.end()


def build_program(ntok):
    nc = bass.Bass("TRN2", target_bir_lowering=False)

    def din(name, shape, dt=F32):
        return nc.dram_tensor(name, list(shape), dt, kind="ExternalInput").ap()

    def dint(name, shape, dt=F32):
        return nc.dram_tensor(name, list(shape), dt).ap()

    xT = din("xT", [D, ntok])
    pT = din("pT", [DEPTH, 256, ntok])
    vecs = din("vecs", [DEPTH + 1, 128, 32])
    a_w = din("a_w", [2, 2, D, 3072])
    a_wo = din("a_wo", [2, 2048, D])
    b_w = din("b_w", [D, 7168])
    b_wo = din("b_wo", [D, D])
    c_w = din("c_w", [D, 3072])
    c_wo = din("c_wo", [D, D])
    f_wi = din("f_wi", [DEPTH, D, 2 * HID])
    f_wo = din("f_wo", [DEPTH, HID, D])
    p_wp = din("p_wp", [DEPTH, 256, D])
    p_wg = din("p_wg", [DEPTH, D, D])
    rc = dict(dq=din("r_dq", [2, 128, 2, 512]), dkt=din("r_dkt", [2, 128, 2, 512]), dkm=din("r_dkm", [2, 128, 2]),
              mask=din("r_mask", [128, 128]))
    ident = din("ident", [128, 128], BF16)
    b_gq = din("b_gq", [128, 1]); b_gk = din("b_gk", [128, 1]); b_bias = din("b_bias", [128, 24, 256], BF16)
    c_gq = din("c_gq", [64, 1]); c_gk = din("c_gk", [64, 1]); c_lam = din("c_lam", [64, 4]); c_gsub = din("c_gsub", [128, 1])
    c_qaug = din("c_qaug", [3, 8, 512], BF16); c_kaug = din("c_kaug", [3, 8, ntok], BF16); c_maskb = din("c_maskb", [128, 4, 512], BF16)
    outT = nc.dram_tensor("outT", [D, ntok], F32, kind="ExternalOutput").ap()
    hA = dint("hA", [D, ntok]); hB = dint("hB", [D, ntok])
    hnT = dint("hnT", [D, ntok], BF16)
    yT = dint("yT", [2048, ntok], BF16)

    ctx = Ctx(nc)
    phase_token_local(ctx, dict(hT=xT, vecs=vecs[0], hnT_out=hnT), 0, ntok, True, False)
    h_in = xT
    for i in range(DEPTH):
        kind, j = i % 3, i // 3
        if kind == 0:
            for r in range(2):
                phase_retention(ctx, dict(hnT=hnT, w=a_w[j, r], dq=rc["dq"][r], dkt=rc["dkt"][r], dkm=rc["dkm"][r], mask=rc["mask"],
                                          ident=ident, yT=yT[r * 1024:(r + 1) * 1024, :]),
                                [ret_gamma(2 * r), ret_gamma(2 * r + 1)], ntok)
            kout, wout = 2048, a_wo[j]
        elif kind == 1:
            phase_dilated(ctx, dict(hnT=hnT, w=b_w, gq=b_gq, gk=b_gk, bias=b_bias, ident=ident, yT=yT[0:1024, :]), ntok)
            kout, wout = 1024, b_wo
        else:
            phase_diff(ctx, dict(hnT=hnT, w=c_w, gq=c_gq, gk=c_gk, lam=c_lam, gsub=c_gsub, qaug=c_qaug, kaug=c_kaug, maskb=c_maskb,
                                 ident=ident, yT=yT[0:1024, :]), ntok, i)
            kout, wout = 1024, c_wo
        last = (i == DEPTH - 1)
        h_out = outT if last else (hA if i % 2 == 0 else hB)
        phase_token_local(ctx, dict(hT=h_in, yT=yT[0:kout, :], pT=pT[i], wout=wout, win=f_wi[i], wo=f_wo[i], wg=p_wg[i], wp=p_wp[i],
                                    vecs=vecs[i + 1], hT_out=h_out, hnT_out=hnT), kout, ntok, False, last)
        h_in = h_out
    ctx.finish()
    return nc


def _colvec(v):
    return np.ascontiguousarray(np.asarray(v, np.float32).reshape(-1, 128).T)


def prepare_inputs(inp, nb, ntok):
    f32 = np.float32
    g = {k: np.asarray(v) for k, v in inp.items()}
    vecs = np.zeros((DEPTH + 1, 128, 32), f32)
    vecs[0, :, 24:32] = _colvec(g["mix_norm"][0])
    for i in range(DEPTH):
        vecs[i + 1, :, 0:8] = _colvec(g["ffn_norm"][i])
        vecs[i + 1, :, 8:16] = _colvec(g["ple_gate_norm"][i])
        vecs[i + 1, :, 16:24] = _colvec(g["ple_norm"][i])
        if i + 1 < DEPTH:
            vecs[i + 1, :, 24:32] = _colvec(g["mix_norm"][i + 1])
    a_w = np.zeros((2, 2, D, 3072), f32)
    for j in range(2):
        w = g["a_w_in"][j]
        for r in range(2):
            parts = []
            for h in (2 * r, 2 * r + 1):
                parts += [w[:, h * 256:(h + 1) * 256], w[:, 1024 + h * 256:1024 + (h + 1) * 256],
                          w[:, 2048 + h * 512:2048 + (h + 1) * 512], w[:, 4096 + h * 512:4096 + (h + 1) * 512]]
            a_w[j, r] = np.concatenate(parts, axis=1)
    rcs = [ret_consts(r) for r in range(2)]
    dc = dil_consts()
    fc = diff_consts(ntok)
    shared = dict(
        vecs=vecs, a_w=a_w, a_wo=g["a_w_out"].astype(f32), b_w=g["b_w_in"][0], b_wo=g["b_w_out"][0], c_w=g["c_w_in"][0], c_wo=g["c_w_out"][0],
        f_wi=g["ffn_w_in"], f_wo=g["ffn_w_out"], p_wp=g["ple_w_proj"], p_wg=g["ple_w_gate"],
        r_dq=np.stack([c["dq"] for c in rcs]), r_dkt=np.stack([c["dkt"] for c in rcs]), r_dkm=np.stack([c["dkm"] for c in rcs]),
        r_mask=rcs[0]["mask"], ident=rcs[0]["ident"],
        b_gq=g["b_q_norm"][0].reshape(128, 1), b_gk=g["b_k_norm"][0].reshape(128, 1), b_bias=dc["bias"],
        c_gq=g["c_q_norm"][0].reshape(64, 1), c_gk=g["c_k_norm"][0].reshape(64, 1),
        c_lam=np.ascontiguousarray(np.stack([g["c_lambda_q1"][0], g["c_lambda_k1"][0], g["c_lambda_q2"][0], g["c_lambda_k2"][0]], axis=1).astype(f32)),
        c_gsub=g["c_subln"][0].reshape(128, 1), c_qaug=fc["qaug"], c_kaug=fc["kaug"], c_maskb=fc["maskb"])
    shared = {k: np.ascontiguousarray(v) for k, v in shared.items()}
    in_maps = []
    for b in range(nb):
        m = dict(shared)
        m["xT"] = np.ascontiguousarray(g["x"][b, :ntok].T)
        m["pT"] = np.ascontiguousarray(np.transpose(g["p"][:, b, :ntok, :], (0, 2, 1)))
        in_maps.append(m)
    return in_maps


def run_model(inp, nb, ntok):
    nc = build_program(ntok)
    in_maps = prepare_inputs(inp, nb, ntok)
    res = run_bass_kernel_spmd(nc, in_maps, core_ids=list(range(nb)))
    out = np.stack([np.ascontiguousarray(res.results[b]["outT"].T) for b in range(nb)], axis=0)
    return out.astype(np.float32)


def kernel(**inputs):
    return run_model(inputs, B, S)
```

```python
import contextlib
import math
import numpy as np
import concourse.bass as bass
import concourse.mybir as mybir
from concourse.bass_utils import run_bass_kernel_spmd

F32 = mybir.dt.float32
BF16 = mybir.dt.bfloat16
ALU = mybir.AluOpType
AF = mybir.ActivationFunctionType

D = 1024
S = 8192
B = 4
DEPTH = 4
HID = 2816
EPS = 1e-6

ENGS = ("pe", "act", "dve", "pool", "sp")
NDMASEM = 8
SAME_ENGINE_SYNC = {"pe": False, "act": True, "dve": True, "pool": True, "sp": False}


class Res:
    __slots__ = ("w", "rs")

    def __init__(self):
        self.w = None
        self.rs = []


class Op:
    __slots__ = ("eng", "fn", "deps", "dma", "need_inc", "sem", "val", "dslot")

    def __init__(self, eng, fn, dma):
        self.eng = eng
        self.fn = fn
        self.dma = dma
        self.deps = []
        self.need_inc = False
        self.sem = None
        self.val = None
        self.dslot = 0


class Ctx:
    def __init__(self, nc):
        self.nc = nc
        self.stack = contextlib.ExitStack()
        st = self.stack
        self.esem = {e: st.enter_context(nc.semaphore(f"s_{e}")) for e in ENGS}
        self.dsem = {e: [st.enter_context(nc.semaphore(f"d_{e}{i}")) for i in range(NDMASEM)] for e in ENGS}
        self.ecount = {e: 0 for e in ENGS}
        self.dcount = {e: 0 for e in ENGS}
        self.seen = {e: {} for e in ENGS}
        self.barrier = []
        self.nuid = 0

    def all_signals(self):
        sig = []
        for e in ENGS:
            if self.ecount[e] > 0:
                sig.append((self.esem[e], self.ecount[e]))
            nd = self.dcount[e]
            for i in range(min(nd, NDMASEM)):
                n_i = (nd - 1 - i) // NDMASEM + 1
                sig.append((self.dsem[e][i], 16 * n_i))
        return sig

    def finish(self):
        nc = self.nc
        sig = self.all_signals()
        ctx = self
        with nc.Block() as block:
            @block.sync
            def _(eng):
                for (k, v) in sig:
                    if ctx.seen["sp"].get(id(k), 0) < v:
                        eng.wait_ge(k, v)
        self.stack.close()


class Phase:
    def __init__(self, ctx):
        self.ctx = ctx
        self.nc = ctx.nc
        self.q = {e: [] for e in ENGS}
        self.stack = contextlib.ExitStack()

    def sbuf(self, shape, dtype):
        self.ctx.nuid += 1
        return self.stack.enter_context(self.nc.sbuf_tensor(f"sb{self.ctx.nuid}", list(shape), dtype))

    def psum(self, shape, dtype=F32):
        self.ctx.nuid += 1
        return self.stack.enter_context(self.nc.psum_tensor(f"ps{self.ctx.nuid}", list(shape), dtype))

    def op(self, eng, fn, reads=(), writes=(), dma=False):
        o = Op(eng, fn, dma)
        deps = []
        for r in reads:
            if r.w is not None:
                deps.append(r.w)
        for w in writes:
            if w.w is not None:
                deps.append(w.w)
            deps.extend(w.rs)
        seen = set()
        for d in deps:
            if id(d) in seen or d is o:
                continue
            seen.add(id(d))
            if d.eng == eng and not d.dma and not SAME_ENGINE_SYNC[eng]:
                continue
            o.deps.append(d)
            d.need_inc = True
        for r in reads:
            r.rs.append(o)
        for w in writes:
            w.w = o
            w.rs = []
        self.q[eng].append(o)
        return o

    def dma(self, eng, out, in_, reads=(), writes=()):
        return self.op(eng, lambda e: e.dma_start(out=out, in_=in_), reads, writes, dma=True)

    def end(self):
        ctx = self.ctx
        nc = self.nc
        for e in ENGS:
            last = None
            for o in self.q[e]:
                if not o.dma:
                    last = o
            if last is not None:
                last.need_inc = True
        for e in ENGS:
            for o in self.q[e]:
                if o.dma:
                    nd = ctx.dcount[e]
                    o.sem = ctx.dsem[e][nd % NDMASEM]
                    o.val = 16 * (nd // NDMASEM + 1)
                    o.dslot = nd
                    ctx.dcount[e] = nd + 1
                elif o.need_inc:
                    ctx.ecount[e] += 1
                    o.sem = ctx.esem[e]
                    o.val = ctx.ecount[e]
        barrier = ctx.barrier
        ph = self

        def replay(e, eng):
            seen = ctx.seen[e]
            for (k, v) in barrier:
                if seen.get(id(k), 0) < v:
                    eng.wait_ge(k, v)
                    seen[id(k)] = v
            for o in ph.q[e]:
                need = {}
                for d in o.deps:
                    kk = id(d.sem)
                    if kk not in need or need[kk][1] < d.val:
                        need[kk] = (d.sem, d.val)
                if o.dma and o.dslot >= NDMASEM:
                    kk = id(o.sem)
                    v = o.val - 16
                    if kk not in need or need[kk][1] < v:
                        need[kk] = (o.sem, v)
                for kk, (k, v) in need.items():
                    if seen.get(kk, 0) < v:
                        eng.wait_ge(k, v)
                        seen[kk] = v
                ins = o.fn(eng)
                if o.dma:
                    ins.then_inc(o.sem, 16)
                elif o.need_inc:
                    ins.then_inc(o.sem, 1)

        with nc.Block() as block:
            @block.tensor
            def _(eng):
                replay("pe", eng)

            @block.scalar
            def _(eng):
                replay("act", eng)

            @block.vector
            def _(eng):
                replay("dve", eng)

            @block.gpsimd
            def _(eng):
                replay("pool", eng)

            @block.sync
            def _(eng):
                replay("sp", eng)
        ctx.barrier = ctx.all_signals()
        self.stack.close()


class Rot:
    def __init__(self, ph, n, shape, dtype, psum=False):
        self.bufs = [(ph.psum(shape, dtype) if psum else ph.sbuf(shape, dtype), Res()) for _ in range(n)]
        self.i = 0

    def next(self):
        b = self.bufs[self.i % len(self.bufs)]
        self.i += 1
        return b


def emit_rmsnorm_fm(ph, src, r_src, nchunk, T0, TN, gain, r_gain, dst, r_dst, env, dim, gain_col0=0):
    ones, r_ones, mh, r_mh = env["ones"], env["r_ones"], env["mh"], env["r_mh"]
    pst, r_pst = env["psrot"].next()
    for c in range(nchunk):
        sq, r_sq = env["sqrot"].next()
        ph.op("act", lambda e, c=c, sq=sq: e.activation(out=sq[:, :TN], in_=src[:, c, T0:T0 + TN], func=AF.Square),
              reads=[r_src], writes=[r_sq])
        ph.op("pe", lambda e, c=c, sq=sq, pst=pst: e.matmul(pst[:, :TN], ones[:], sq[:, :TN], start=(c == 0), stop=(c == nchunk - 1)),
              reads=[r_ones, r_sq], writes=[r_pst])
    tmp, r_tmp = env["tmprot"].next()
    ph.op("act", lambda e, tmp=tmp, pst=pst: e.activation(out=tmp[:, :TN], in_=pst[:, :TN], func=AF.Ln, scale=1.0 / dim, bias=env["epsc"][:, 0:1]),
          reads=[r_pst, r_mh], writes=[r_tmp])
    rstd, r_rstd = env["tmprot"].next()
    ph.op("act", lambda e, tmp=tmp, rstd=rstd: e.activation(out=rstd[:, :TN], in_=tmp[:, :TN], func=AF.Exp, scale=-0.5),
          reads=[r_tmp], writes=[r_rstd])
    for c in range(nchunk):
        ph.op("dve", lambda e, c=c, rstd=rstd: e.scalar_tensor_tensor(
            out=dst[:, c, T0:T0 + TN], in0=src[:, c, T0:T0 + TN], scalar=gain[:, gain_col0 + c:gain_col0 + c + 1], in1=rstd[:, :TN],
            op0=ALU.mult, op1=ALU.mult), reads=[r_src, r_gain, r_rstd], writes=[r_dst])


def make_env(ph):
    env = {}
    env["ones"] = ph.sbuf([128, 128], F32)
    env["r_ones"] = Res()
    env["mh"] = ph.sbuf([128, 512], F32)
    env["r_mh"] = Res()
    ph.op("pool", lambda e: e.memset(env["ones"][:], 1.0), writes=[env["r_ones"]])
    ph.op("pool", lambda e: e.memset(env["mh"][:], -0.5), writes=[env["r_mh"]])
    env["epsc"] = ph.sbuf([128, 1], F32)
    ph.op("pool", lambda e: e.memset(env["epsc"][:], EPS), writes=[env["r_mh"]])
    env["sqrot"] = Rot(ph, 3, [128, 512], F32)
    env["tmprot"] = Rot(ph, 4, [128, 512], F32)
    return env


TT = 1024
ST = 512


def phase_token_local(ctx, t, kout, ntok, first, last):
    ph = Phase(ctx)
    env = make_env(ph)
    psrot = Rot(ph, 8, [128, 512], F32, psum=True)
    env["psrot"] = psrot
    kc_out = kout // 128
    vec = ph.sbuf([128, 32], F32)
    r_vec = Res()
    ph.dma("sp", vec[:], t["vecs"], writes=[r_vec])
    h = ph.sbuf([128, 8, TT], F32)
    r_h = Res()
    hn = ph.sbuf([128, 8, TT], BF16)
    r_hn = Res()
    if not first:
        yg = ph.sbuf([128, 22, TT], BF16)
        r_yg = Res()
        pt = ph.sbuf([128, 2, TT], BF16)
        r_pt = Res()
        et = ph.sbuf([128, 8, TT], F32)
        r_et = Res()
        wrot = Rot(ph, 4, [128, 4096], BF16)
        sarot = Rot(ph, 3, [128, 512], F32)

    hT_in = t["hT"].rearrange("(c p) t -> p c t", p=128)
    for tt in range(ntok // TT):
        tsl = slice(tt * TT, (tt + 1) * TT)
        ph.dma("sp", h[:], hT_in[:, :, tsl], writes=[r_h])
        if first:
            for st in range(TT // ST):
                emit_rmsnorm_fm(ph, h, r_h, 8, st * ST, ST, vec, r_vec, hn, r_hn, env, D, gain_col0=24)
            ph.dma("sp", t["hnT_out"].rearrange("(c p) t -> p c t", p=128)[:, :, tsl], hn[:], reads=[r_hn])
            continue
        ph.dma("sp", yg[:, :kc_out, :], t["yT"].rearrange("(c p) t -> p c t", p=128)[:, :, tsl], writes=[r_yg])
        ph.dma("pool", pt[:], t["pT"].rearrange("(c p) t -> p c t", p=128)[:, :, tsl], writes=[r_pt])
        ncol = 4096 // kc_out
        wv = t["wout"].rearrange("(c p) n -> p c n", p=128)
        for blk in range(D // ncol):
            wb, r_wb = wrot.next()
            wbv = wb[:, :kc_out * ncol].rearrange("p (c n) -> p c n", c=kc_out)
            ph.dma("pool", wbv, wv[:, :, blk * ncol:(blk + 1) * ncol], writes=[r_wb])
            for st in range(TT // ST):
                for fi in range(ncol // 128):
                    f = blk * (ncol // 128) + fi
                    ps, r_ps = psrot.next()
                    for k in range(kc_out):
                        ph.op("pe", lambda e, ps=ps, wbv=wbv, fi=fi, k=k, st=st: e.matmul(
                            ps[:], wbv[:, k, fi * 128:(fi + 1) * 128], yg[:, k, st * ST:(st + 1) * ST],
                            start=(k == 0), stop=(k == kc_out - 1)), reads=[r_wb, r_yg], writes=[r_ps])
                    ph.op("dve", lambda e, ps=ps, f=f, st=st: e.tensor_tensor(
                        out=h[:, f, st * ST:(st + 1) * ST], in0=h[:, f, st * ST:(st + 1) * ST], in1=ps[:], op=ALU.add),
                        reads=[r_ps, r_h], writes=[r_h])
        for st in range(TT // ST):
            emit_rmsnorm_fm(ph, h, r_h, 8, st * ST, ST, vec, r_vec, hn, r_hn, env, D, gain_col0=0)
        wv = t["win"].rearrange("(c p) n -> p c n", p=128)
        for blk in range(HID // 256):
            wb, r_wb = wrot.next()
            wbv = wb[:].rearrange("p (c n) -> p c n", c=8)
            ph.dma("pool", wbv[:, :, 0:256], wv[:, :, blk * 256:(blk + 1) * 256], writes=[r_wb])
            ph.dma("pool", wbv[:, :, 256:512], wv[:, :, HID + blk * 256:HID + (blk + 1) * 256], writes=[r_wb])
            for st in range(TT // ST):
                for jj in range(2):
                    j = blk * 2 + jj
                    pa, r_pa = psrot.next()
                    pb, r_pb = psrot.next()
                    for c in range(8):
                        ph.op("pe", lambda e, pa=pa, wbv=wbv, jj=jj, c=c, st=st: e.matmul(
                            pa[:], wbv[:, c, jj * 128:(jj + 1) * 128], hn[:, c, st * ST:(st + 1) * ST],
                            start=(c == 0), stop=(c == 7)), reads=[r_wb, r_hn], writes=[r_pa])
                    for c in range(8):
                        ph.op("pe", lambda e, pb=pb, wbv=wbv, jj=jj, c=c, st=st: e.matmul(
                            pb[:], wbv[:, c, 256 + jj * 128:256 + (jj + 1) * 128], hn[:, c, st * ST:(st + 1) * ST],
                            start=(c == 0), stop=(c == 7)), reads=[r_wb, r_hn], writes=[r_pb])
                    sa, r_sa = sarot.next()
                    ph.op("act", lambda e, sa=sa, pa=pa: e.activation(out=sa[:], in_=pa[:], func=AF.Silu), reads=[r_pa], writes=[r_sa])
                    ph.op("dve", lambda e, sa=sa, pb=pb, j=j, st=st: e.tensor_tensor(
                        out=yg[:, j, st * ST:(st + 1) * ST], in0=sa[:], in1=pb[:], op=ALU.mult),
                        reads=[r_sa, r_pb], writes=[r_yg])
        wv = t["wo"].rearrange("(c p) n -> p c n", p=128)
        for f in range(8):
            wb, r_wb = wrot.next()
            wbv = wb[:, :22 * 128].rearrange("p (c n) -> p c n", c=22)
            ph.dma("pool", wbv, wv[:, :, f * 128:(f + 1) * 128], writes=[r_wb])
            for st in range(TT // ST):
                ps, r_ps = psrot.next()
                for k in range(22):
                    ph.op("pe", lambda e, ps=ps, wbv=wbv, k=k, st=st: e.matmul(
                        ps[:], wbv[:, k, :], yg[:, k, st * ST:(st + 1) * ST], start=(k == 0), stop=(k == 21)),
                        reads=[r_wb, r_yg], writes=[r_ps])
                ph.op("dve", lambda e, ps=ps, f=f, st=st: e.tensor_tensor(
                    out=h[:, f, st * ST:(st + 1) * ST], in0=h[:, f, st * ST:(st + 1) * ST], in1=ps[:], op=ALU.add),
                    reads=[r_ps, r_h], writes=[r_h])
        for st in range(TT // ST):
            emit_rmsnorm_fm(ph, h, r_h, 8, st * ST, ST, vec, r_vec, hn, r_hn, env, D, gain_col0=8)
        wb, r_wb = wrot.next()
        wbv = wb[:, :2048].rearrange("p (c n) -> p c n", c=2)
        ph.dma("pool", wbv, t["wp"].rearrange("(c p) n -> p c n", p=128), writes=[r_wb])
        for st in range(TT // ST):
            for f in range(8):
                ps, r_ps = psrot.next()
                for k in range(2):
                    ph.op("pe", lambda e, ps=ps, wbv=wbv, k=k, f=f, st=st: e.matmul(
                        ps[:], wbv[:, k, f * 128:(f + 1) * 128], pt[:, k, st * ST:(st + 1) * ST], start=(k == 0), stop=(k == 1)),
                        reads=[r_wb, r_pt], writes=[r_ps])
                ph.op("act", lambda e, ps=ps, f=f, st=st: e.activation(out=et[:, f, st * ST:(st + 1) * ST], in_=ps[:], func=AF.Copy),
                      reads=[r_ps], writes=[r_et])
            emit_rmsnorm_fm(ph, et, r_et, 8, st * ST, ST, vec, r_vec, et, r_et, env, D, gain_col0=16)
        wv = t["wg"].rearrange("(c p) n -> p c n", p=128)
        for blk in range(2):
            wb, r_wb = wrot.next()
            wbv = wb[:].rearrange("p (c n) -> p c n", c=8)
            ph.dma("pool", wbv, wv[:, :, blk * 512:(blk + 1) * 512], writes=[r_wb])
            for st in range(TT // ST):
                for fi in range(4):
                    f = blk * 4 + fi
                    ps, r_ps = psrot.next()
                    for c in range(8):
                        ph.op("pe", lambda e, ps=ps, wbv=wbv, fi=fi, c=c, st=st: e.matmul(
                            ps[:], wbv[:, c, fi * 128:(fi + 1) * 128], hn[:, c, st * ST:(st + 1) * ST],
                            start=(c == 0), stop=(c == 7)), reads=[r_wb, r_hn], writes=[r_ps])
                    sa, r_sa = sarot.next()
                    ph.op("act", lambda e, sa=sa, ps=ps: e.activation(out=sa[:], in_=ps[:], func=AF.Tanh, scale=0.5),
                          reads=[r_ps], writes=[r_sa])
                    ph.op("dve", lambda e, sa=sa: e.tensor_scalar(out=sa[:], in0=sa[:], scalar1=0.5, scalar2=0.5, op0=ALU.mult, op1=ALU.add),
                          reads=[r_sa], writes=[r_sa])
                    ph.op("dve", lambda e, sa=sa, f=f, st=st: e.tensor_tensor(
                        out=sa[:], in0=sa[:], in1=et[:, f, st * ST:(st + 1) * ST], op=ALU.mult), reads=[r_sa, r_et], writes=[r_sa])
                    ph.op("dve", lambda e, sa=sa, f=f, st=st: e.tensor_tensor(
                        out=h[:, f, st * ST:(st + 1) * ST], in0=h[:, f, st * ST:(st + 1) * ST], in1=sa[:], op=ALU.add),
                        reads=[r_sa, r_h], writes=[r_h])
        ph.dma("sp", t["hT_out"].rearrange("(c p) t -> p c t", p=128)[:, :, tsl], h[:], reads=[r_h])
        if not last:
            for st in range(TT // ST):
                emit_rmsnorm_fm(ph, h, r_h, 8, st * ST, ST, vec, r_vec, hn, r_hn, env, D, gain_col0=24)
            ph.dma("sp", t["hnT_out"].rearrange("(c p) t -> p c t", p=128)[:, :, tsl], hn[:], reads=[r_hn])
    ph.end()


def ret_gamma(h):
    return float(np.float32(1.0) - np.float32(2.0) ** np.float32(-5.0 - h))


def ret_consts(r):
    import ml_dtypes
    pos = np.arange(512) % 128
    dq = np.zeros((128, 2, 512), np.float32)
    dkt = np.zeros((128, 2, 512), np.float32)
    dkm = np.zeros((128, 2), np.float32)
    for hh in range(2):
        lg = np.log(np.float64(ret_gamma(2 * r + hh)))
        dq[:, hh, :] = np.exp((pos + 1.0) * lg)[None, :]
        dkt[:, hh, :] = (np.exp(-(pos + 1.0) * lg) * (256.0 ** -0.5))[None, :]
        dkm[:, hh] = np.exp((127.0 - np.arange(128)) * lg) * (256.0 ** -0.5)
    m = np.arange(128)
    mask = (m[:, None] <= m[None, :]).astype(np.float32)
    ident = np.eye(128, dtype=np.float32).astype(ml_dtypes.bfloat16)
    return dict(dq=dq, dkt=dkt, dkm=dkm, mask=mask, ident=ident)


def phase_retention(ctx, t, r_role_gammas, ntok):
    ph = Phase(ctx)
    NT = ntok // 512
    gC = [g ** 128 for g in r_role_gammas]
    w = ph.sbuf([128, 8, 3072], BF16)
    r_w = Res()
    wv = t["w"].rearrange("(c p) n -> p c n", p=128)
    for hh in range(2):
        for part in range(3):
            sl = slice(hh * 1536 + part * 512, hh * 1536 + (part + 1) * 512)
            ph.dma("pool", w[:, :, sl], wv[:, :, sl], writes=[r_w])
    dq = ph.sbuf([128, 2, 512], F32); r_c = Res()
    dkt = ph.sbuf([128, 2, 512], F32)
    dkm = ph.sbuf([128, 2], F32)
    mask = ph.sbuf([128, 128], F32)
    ident = ph.sbuf([128, 128], BF16)
    mh = ph.sbuf([128, 1], F32)
    ph.dma("sp", dq[:], t["dq"], writes=[r_c])
    ph.dma("sp", dkt[:], t["dkt"], writes=[r_c])
    ph.dma("sp", dkm[:], t["dkm"], writes=[r_c])
    ph.dma("sp", mask[:], t["mask"], writes=[r_c])
    ph.dma("sp", ident[:], t["ident"], writes=[r_c])
    epsc = ph.sbuf([128, 1], F32)
    ph.op("pool", lambda e: e.memset(epsc[:], EPS), writes=[r_c])

    hnrot = Rot(ph, 2, [128, 8, 512], BF16)
    qrot = Rot(ph, 2, [128, 2, 512], BF16)
    ktrot = Rot(ph, 2, [128, 2, 512], BF16)
    kmrot = Rot(ph, 2, [128, 4, 256], BF16)
    vrot = Rot(ph, 2, [128, 4, 512], BF16)
    sgrot = Rot(ph, 2, [128, 4, 512], BF16)
    R32 = [ph.sbuf([128, 2, 512], F32) for _ in range(2)]
    r_R32 = [Res(), Res()]
    Rb = [ph.sbuf([128, 2, 512], BF16) for _ in range(2)]
    r_Rb = [Res(), Res()]
    attrot = Rot(ph, 2, [128, 128], BF16)
    ygrot = Rot(ph, 2, [128, 512], BF16)
    ygtrot = Rot(ph, 2, [128, 4, 512], BF16)
    junk = ph.sbuf([128, 512], BF16); r_junk = Res()
    ssrot = Rot(ph, 4, [128, 1], F32)
    projrot = Rot(ph, 2, [128, 512], F32, psum=True)
    attp = ph.psum([128, 512], F32)
    r_attp = Res()
    attps = [(attp[:, i * 128:(i + 1) * 128], r_attp) for i in range(4)]
    yrot = Rot(ph, 2, [128, 512], F32, psum=True)
    rprot = Rot(ph, 2, [128, 512], F32, psum=True)
    trp = ph.psum([128, 1024], BF16)
    r_trp = Res()
    trps = [(trp[:, i * 512:(i + 1) * 512], r_trp) for i in range(2)]
    cnt = {"att": 0, "tr": 0}

    hn_v = t["hnT"].rearrange("(c p) t -> p c t", p=128)
    y_v = t["yT"].rearrange("(c p) t -> p c t", p=128)

    def stage_a(tt, hh, hn, r_hn, res):
        q, r_q = qrot.next(); kt, r_kt = ktrot.next(); km, r_km = kmrot.next(); v, r_v = vrot.next(); sg, r_sg = sgrot.next()
        base = hh * 1536
        for dc in range(2):
            ps, r_ps = projrot.next()
            for c in range(8):
                ph.op("pe", lambda e, ps=ps, c=c, dc=dc: e.matmul(ps[:], w[:, c, base + dc * 128: base + (dc + 1) * 128], hn[:, c, :],
                                                                 start=(c == 0), stop=(c == 7)), reads=[r_w, r_hn], writes=[r_ps])
            ph.op("dve", lambda e, ps=ps, dc=dc: e.tensor_tensor(out=q[:, dc, :], in0=ps[:], in1=dq[:, hh, :], op=ALU.mult),
                  reads=[r_ps, r_c], writes=[r_q])
            yield
        for dc in range(2):
            ps, r_ps = projrot.next()
            for c in range(8):
                ph.op("pe", lambda e, ps=ps, c=c, dc=dc: e.matmul(ps[:], w[:, c, base + 256 + dc * 128: base + 256 + (dc + 1) * 128], hn[:, c, :],
                                                                 start=(c == 0), stop=(c == 7)), reads=[r_w, r_hn], writes=[r_ps])
            ph.op("dve", lambda e, ps=ps, dc=dc: e.tensor_tensor(out=kt[:, dc, :], in0=ps[:], in1=dkt[:, hh, :], op=ALU.mult),
                  reads=[r_ps, r_c], writes=[r_kt])
            yield
        for ci in range(4):
            ps, r_ps = projrot.next()
            for c in range(8):
                ph.op("pe", lambda e, ps=ps, c=c, ci=ci: e.matmul(ps[:, :256], hn[:, c, ci * 128:(ci + 1) * 128], w[:, c, base + 256: base + 512],
                                                                 start=(c == 0), stop=(c == 7)), reads=[r_w, r_hn], writes=[r_ps])
            ph.op("act", lambda e, ps=ps, ci=ci: e.activation(out=km[:, ci, :], in_=ps[:, :256], func=AF.Identity, scale=dkm[:, hh:hh + 1]),
                  reads=[r_ps, r_c], writes=[r_km])
            yield
            ps, r_ps = projrot.next()
            for c in range(8):
                ph.op("pe", lambda e, ps=ps, c=c, ci=ci: e.matmul(ps[:], hn[:, c, ci * 128:(ci + 1) * 128], w[:, c, base + 512: base + 1024],
                                                                 start=(c == 0), stop=(c == 7)), reads=[r_w, r_hn], writes=[r_ps])
            ph.op("act", lambda e, ps=ps, ci=ci: e.activation(out=v[:, ci, :], in_=ps[:], func=AF.Copy), reads=[r_ps], writes=[r_v])
            yield
            ps, r_ps = projrot.next()
            for c in range(8):
                ph.op("pe", lambda e, ps=ps, c=c, ci=ci: e.matmul(ps[:], hn[:, c, ci * 128:(ci + 1) * 128], w[:, c, base + 1024: base + 1536],
                                                                 start=(c == 0), stop=(c == 7)), reads=[r_w, r_hn], writes=[r_ps])
            ph.op("act", lambda e, ps=ps, ci=ci: e.activation(out=sg[:, ci, :], in_=ps[:], func=AF.Silu), reads=[r_ps], writes=[r_sg])
            yield
        res["bufs"] = (q, r_q, kt, r_kt, km, r_km, v, r_v, sg, r_sg)

    def stage_b(tt, hh, bufs, gen):
        q, r_q, kt, r_kt, km, r_km, v, r_v, sg, r_sg = bufs

        def pull(n):
            if gen is not None:
                for _ in range(n):
                    next(gen, None)
        ygt, r_ygt = ygtrot.next()
        for ci in range(4):
            first = (tt == 0 and ci == 0)
            csl = slice(ci * 128, (ci + 1) * 128)
            ap_, r_ap = attps[cnt["att"] % 4]; cnt["att"] += 1
            for dc in range(2):
                ph.op("pe", lambda e, ap_=ap_, dc=dc, csl=csl: e.matmul(ap_, kt[:, dc, csl], q[:, dc, csl], start=(dc == 0), stop=(dc == 1)),
                      reads=[r_kt, r_q], writes=[r_ap])
            pull(1)
            at, r_at = attrot.next()
            ph.op("dve", lambda e, at=at, ap_=ap_: e.tensor_tensor(out=at[:], in0=ap_, in1=mask[:], op=ALU.mult),
                  reads=[r_ap, r_c], writes=[r_at])
            yp, r_yp = yrot.next()
            ph.op("pe", lambda e, yp=yp, at=at, ci=ci, first=first: e.matmul(yp[:], at[:], v[:, ci, :], start=True, stop=first),
                  reads=[r_at, r_v], writes=[r_yp])
            if not first:
                for dc in range(2):
                    ph.op("pe", lambda e, yp=yp, dc=dc, csl=csl: e.matmul(yp[:], q[:, dc, csl], Rb[hh][:, dc, :], start=False, stop=(dc == 1)),
                          reads=[r_q, r_Rb[hh]], writes=[r_yp])
            for dc in range(2):
                rp, r_rp = rprot.next()
                ph.op("pe", lambda e, rp=rp, dc=dc, ci=ci: e.matmul(rp[:], km[:, ci, dc * 128:(dc + 1) * 128], v[:, ci, :], start=True, stop=True),
                      reads=[r_km, r_v], writes=[r_rp])
                if first:
                    ph.op("dve", lambda e, rp=rp, dc=dc: e.tensor_copy(out=R32[hh][:, dc, :], in_=rp[:]), reads=[r_rp], writes=[r_R32[hh]])
                else:
                    ph.op("dve", lambda e, rp=rp, dc=dc: e.scalar_tensor_tensor(
                        out=R32[hh][:, dc, :], in0=R32[hh][:, dc, :], scalar=gC[hh], in1=rp[:], op0=ALU.mult, op1=ALU.add),
                        reads=[r_rp, r_R32[hh]], writes=[r_R32[hh]])
            ph.op("act", lambda e: e.activation(out=Rb[hh][:], in_=R32[hh][:], func=AF.Copy), reads=[r_R32[hh]], writes=[r_Rb[hh]])
            pull(1)
            ss, r_ss = ssrot.next()
            ph.op("act", lambda e, yp=yp, ss=ss: e.activation(out=junk[:], in_=yp[:], func=AF.Square, accum_out=ss[:]),
                  reads=[r_yp], writes=[r_junk, r_ss])
            ph.op("act", lambda e, ss=ss: e.activation(out=ss[:], in_=ss[:], func=AF.Ln, scale=1.0 / 512, bias=epsc[:, 0:1]),
                  reads=[r_ss, r_c], writes=[r_ss])
            ph.op("act", lambda e, ss=ss: e.activation(out=ss[:], in_=ss[:], func=AF.Exp, scale=-0.5), reads=[r_ss], writes=[r_ss])
            yg, r_yg = ygrot.next()
            ph.op("dve", lambda e, yg=yg, yp=yp, ss=ss, ci=ci: e.scalar_tensor_tensor(
                out=yg[:], in0=yp[:], scalar=ss[:, 0:1], in1=sg[:, ci, :], op0=ALU.mult, op1=ALU.mult),
                reads=[r_yp, r_ss, r_sg], writes=[r_yg])
            pull(2)
            tp, r_tp = trps[cnt["tr"] % 2]; cnt["tr"] += 1
            for ec in range(4):
                ph.op("pe", lambda e, tp=tp, yg=yg, ec=ec: e.transpose(tp[:, ec * 128:(ec + 1) * 128], yg[:, ec * 128:(ec + 1) * 128], ident[:]),
                      reads=[r_yg, r_c], writes=[r_tp])
            ph.op("act", lambda e, tp=tp, ygt=ygt, csl=csl: e.activation(
                out=ygt[:, :, csl], in_=tp.rearrange("p (c n) -> p c n", c=4), func=AF.Copy), reads=[r_tp], writes=[r_ygt])
        ph.dma("sp", y_v[:, hh * 4:(hh + 1) * 4, tt * 512:(tt + 1) * 512], ygt[:], reads=[r_ygt])

    pend = None
    for tt in range(NT):
        hn, r_hn = hnrot.next()
        ph.dma("sp", hn[:], hn_v[:, :, tt * 512:(tt + 1) * 512], writes=[r_hn])
        for hh in range(2):
            res = {}
            gen = stage_a(tt, hh, hn, r_hn, res)
            if pend is not None:
                stage_b(*pend, gen)
            for _ in gen:
                pass
            pend = (tt, hh, res["bufs"])
    stage_b(*pend, None)
    ph.end()


DIL_GROUPS = ((128, 1), (512, 4), (2048, 16))
NEGB = -1e30


def alibi_slopes8():
    ratio = 2.0 ** (-8.0 / 8)
    return [ratio ** (h + 1) for h in range(8)]


def dil_consts():
    import ml_dtypes
    sl = alibi_slopes8()
    bias = np.zeros((128, 24, 256), np.float32)
    kj = np.arange(128)[:, None]
    qi = np.arange(128)[None, :]
    for g, (win, dil) in enumerate(DIL_GROUPS):
        for h in range(8):
            for part, off in ((0, 0), (1, 128)):
                rel = off + qi - kj
                ok = (rel >= 0) & (rel <= win // dil)
                bias[:, g * 8 + h, part * 128:(part + 1) * 128] = np.where(ok, -(sl[h] * dil) * rel, NEGB)
    return dict(bias=bias.astype(ml_dtypes.bfloat16), ident=np.eye(128, dtype=np.float32).astype(ml_dtypes.bfloat16))


def phase_dilated(ctx, t, ntok):
    ph = Phase(ctx)
    NT = ntok // 512
    bias = ph.sbuf([128, 24, 256], BF16); r_c = Res()
    ident = ph.sbuf([128, 128], BF16)
    onesb = ph.sbuf([128, 128], BF16)
    ones32 = ph.sbuf([128, 128], F32)
    mh = ph.sbuf([128, 512], F32)
    gqk = ph.sbuf([128, 2], F32)
    ph.dma("sp", bias[:], t["bias"], writes=[r_c])
    ph.dma("sp", ident[:], t["ident"], writes=[r_c])
    ph.dma("sp", gqk[:, 0:1], t["gq"], writes=[r_c])
    ph.dma("sp", gqk[:, 1:2], t["gk"], writes=[r_c])
    ph.op("pool", lambda e: e.memset(onesb[:], 1.0), writes=[r_c])
    ph.op("pool", lambda e: e.memset(ones32[:], 1.0), writes=[r_c])
    epsc = ph.sbuf([128, 1], F32)
    ph.op("pool", lambda e: e.memset(epsc[:], EPS), writes=[r_c])
    ph.op("dve", lambda e: e.tensor_scalar(out=gqk[:, 0:1], in0=gqk[:, 0:1], scalar1=128.0 ** -0.5, scalar2=None, op0=ALU.mult),
          reads=[r_c], writes=[r_c])

    whrot = Rot(ph, 2, [128, 8, 7, 128], BF16)
    hnrot = Rot(ph, 2, [128, 8, 512], BF16)
    qd = ph.sbuf([128, ntok], BF16); r_qd = Res()
    kd = ph.sbuf([128, ntok], BF16); r_kd = Res()
    vd = ph.sbuf([128, ntok], BF16); r_vd = Res()
    uz = ph.sbuf([128, 2, ntok], F32); r_uz = Res()
    ob, r_ob = qd, r_qd
    sqrot = Rot(ph, 2, [128, 512], F32)
    tmprot = Rot(ph, 4, [128, 512], F32)
    ptrot = Rot(ph, 4, [128, 256], BF16)
    vtrot = Rot(ph, 4, [128, 128], BF16)
    projrot = Rot(ph, 2, [128, 512], F32, psum=True)
    nrmrot = Rot(ph, 1, [128, 512], F32, psum=True)
    scrot = Rot(ph, 2, [128, 512], F32, psum=True)
    trp = ph.psum([128, 1024], BF16)
    r_trp = Res()
    trps = [(trp[:, i * 128:(i + 1) * 128], r_trp) for i in range(4)]
    uzrot = Rot(ph, 2, [128, 512], F32, psum=True)
    cnt = {"tr": 0}

    hn_v = t["hnT"].rearrange("(c p) t -> p c t", p=128)
    wv = t["w"].rearrange("(c p) n -> p c n", p=128)
    y_v = t["yT"].rearrange("(c p) t -> p c t", p=128)

    def dview(buf, dil, l0, nl):
        if dil == 1:
            return buf[:, l0:l0 + nl]
        return buf[:].rearrange("p (r l) -> p r l", r=dil)[:, :, l0:l0 + nl].rearrange("p r l -> p l r")

    def pview(ps, dil):
        if dil == 1:
            return ps[:]
        return ps[:].rearrange("p (l r) -> p l r", r=dil)

    for h in range(8):
        wh, r_wh = whrot.next()
        for i in range(7):
            col = (i * 1024 + h * 128) if i < 6 else (6144 + h * 128)
            ph.dma("pool", wh[:, :, i, :], wv[:, :, col:col + 128], writes=[r_wh])
        ph.op("pool", lambda e: e.memset(uz[:], 0.0), writes=[r_uz])
        for g, (win, dil) in enumerate(DIL_GROUPS):
            L = ntok // dil
            nblk = L // 128
            for tt in range(NT):
                hn, r_hn = hnrot.next()
                ph.dma("sp", hn[:], hn_v[:, :, tt * 512:(tt + 1) * 512], writes=[r_hn])
                l0, nl = tt * 512 // dil, 512 // dil
                for which, dst, r_dst in ((0, qd, r_qd), (1, kd, r_kd)):
                    ps, r_ps = projrot.next()
                    wi = which * 3 + g
                    for c in range(8):
                        ph.op("pe", lambda e, ps=ps, c=c, wi=wi, hn=hn, wh=wh: e.matmul(ps[:], wh[:, c, wi, :], hn[:, c, :], start=(c == 0), stop=(c == 7)),
                              reads=[r_wh, r_hn], writes=[r_ps])
                    sq, r_sq = sqrot.next()
                    ph.op("act", lambda e, sq=sq, ps=ps: e.activation(out=sq[:], in_=ps[:], func=AF.Square), reads=[r_ps], writes=[r_sq])
                    pn, r_pn = nrmrot.next()
                    ph.op("pe", lambda e, pn=pn, sq=sq: e.matmul(pn[:], ones32[:], sq[:], start=True, stop=True), reads=[r_sq, r_c], writes=[r_pn])
                    tmp, r_tmp = tmprot.next()
                    ph.op("act", lambda e, tmp=tmp, pn=pn: e.activation(out=tmp[:], in_=pn[:], func=AF.Ln, scale=1.0 / 128, bias=epsc[:, 0:1]),
                          reads=[r_pn, r_c], writes=[r_tmp])
                    rstd, r_rstd = tmprot.next()
                    ph.op("act", lambda e, tmp=tmp, rstd=rstd: e.activation(out=rstd[:], in_=tmp[:], func=AF.Exp, scale=-0.5),
                          reads=[r_tmp], writes=[r_rstd])
                    ph.op("dve", lambda e, ps=ps, rstd=rstd, dst=dst, which=which, dil=dil, l0=l0, nl=nl: e.scalar_tensor_tensor(
                        out=dview(dst, dil, l0, nl), in0=pview(ps, dil), scalar=gqk[:, which:which + 1], in1=pview(rstd, dil),
                        op0=ALU.mult, op1=ALU.mult), reads=[r_ps, r_rstd, r_c], writes=[r_dst])
                ps, r_ps = projrot.next()
                for c in range(8):
                    ph.op("pe", lambda e, ps=ps, c=c, hn=hn, wh=wh: e.matmul(ps[:], wh[:, c, 6, :], hn[:, c, :], start=(c == 0), stop=(c == 7)),
                          reads=[r_wh, r_hn], writes=[r_ps])
                ph.op("act", lambda e, ps=ps, dil=dil, l0=l0, nl=nl: e.activation(out=dview(vd, dil, l0, nl), in_=pview(ps, dil), func=AF.Copy),
                      reads=[r_ps], writes=[r_vd])
            uzv = uz[:].rearrange("p a (l r) -> p a r l", r=dil) if dil > 1 else None
            for r in range(dil):
                base = r * L
                cur = None
                pend = None
                for step in range(nblk + 1):
                    newblk = None
                    if step < nblk:
                        j = step
                        nq = 256 if j < nblk - 1 else 128
                        ksl = slice(base + j * 128, base + (j + 1) * 128)
                        sc, r_sc = scrot.next()
                        ph.op("pe", lambda e, sc=sc, ksl=ksl, nq=nq, base=base, j=j: e.matmul(
                            sc[:, :nq], kd[:, ksl], qd[:, base + j * 128: base + j * 128 + nq], start=True, stop=False),
                            reads=[r_kd, r_qd], writes=[r_sc])
                        ph.op("pe", lambda e, sc=sc, nq=nq, g=g, h=h: e.matmul(sc[:, :nq], ident[:], bias[:, g * 8 + h, :nq], start=False, stop=True),
                              reads=[r_c], writes=[r_sc])
                        pt, r_pt = ptrot.next()
                        ph.op("act", lambda e, pt=pt, sc=sc, nq=nq: e.activation(out=pt[:, :nq], in_=sc[:, :nq], func=AF.Exp), reads=[r_sc], writes=[r_pt])
                        tp, r_tp = trps[cnt["tr"] % 4]; cnt["tr"] += 1
                        ph.op("pe", lambda e, tp=tp, ksl=ksl: e.transpose(tp, vd[:, ksl], ident[:]), reads=[r_vd, r_c], writes=[r_tp])
                        vt, r_vt = vtrot.next()
                        ph.op("dve", lambda e, vt=vt, tp=tp: e.tensor_copy(out=vt[:], in_=tp), reads=[r_tp], writes=[r_vt])
                        newblk = (j, pt, r_pt, vt, r_vt)
                    if pend is not None:
                        j, pt, r_pt, vt, r_vt = pend
                        if cur is None:
                            cur = uzrot.next()
                        cu, r_cu = cur
                        ph.op("pe", lambda e, cu=cu, vt=vt, pt=pt, j=j: e.matmul(cu[:, 0:128], vt[:], pt[:, 0:128], start=(j == 0), stop=True, skip_group_check=True),
                              reads=[r_vt, r_pt], writes=[r_cu])
                        ph.op("pe", lambda e, cu=cu, pt=pt, j=j: e.matmul(cu[:, 128:256], onesb[:], pt[:, 0:128], start=False, stop=True, skip_group_check=True),
                              reads=[r_pt, r_c], writes=[r_cu])
                        nxt = None
                        if j < nblk - 1:
                            nxt = uzrot.next()
                            nu, r_nu = nxt
                            ph.op("pe", lambda e, nu=nu, vt=vt, pt=pt: e.matmul(nu[:, 0:128], vt[:], pt[:, 128:256], start=True, stop=False, skip_group_check=True),
                                  reads=[r_vt, r_pt], writes=[r_nu])
                            ph.op("pe", lambda e, nu=nu, pt=pt: e.matmul(nu[:, 128:256], onesb[:], pt[:, 128:256], start=False, stop=False, skip_group_check=True),
                                  reads=[r_pt, r_c], writes=[r_nu])
                        if dil == 1:
                            dstv = uz[:, :, j * 128:(j + 1) * 128]
                        else:
                            dstv = uzv[:, :, r, j * 128:(j + 1) * 128]
                        ph.op("dve", lambda e, dstv=dstv, cu=cu: e.tensor_tensor(
                            out=dstv, in0=dstv, in1=cu[:, 0:256].rearrange("p (a n) -> p a n", a=2), op=ALU.add),
                            reads=[r_cu, r_uz], writes=[r_uz])
                        cur = nxt
                    pend = newblk
        for tt in range(NT):
            tsl = slice(tt * 512, (tt + 1) * 512)
            rz, r_rz = tmprot.next()
            ph.op("dve", lambda e, rz=rz, tsl=tsl: e.reciprocal(out=rz[:], in_=uz[:, 1, tsl]), reads=[r_uz], writes=[r_rz])
            ph.op("dve", lambda e, rz=rz, tsl=tsl: e.tensor_tensor(out=ob[:, tsl], in0=uz[:, 0, tsl], in1=rz[:], op=ALU.mult),
                  reads=[r_uz, r_rz], writes=[r_ob])
        ph.dma("sp", y_v[:, h, :], ob[:], reads=[r_ob])
    ph.end()


def diff_lambda_init(layer_idx):
    return 0.8 - 0.6 * math.exp(-0.3 * layer_idx)


def diff_consts(ntok):
    import ml_dtypes
    sl = alibi_slopes8()
    col = np.arange(512)
    qaug = np.zeros((3, 8, 512), np.float32)
    kaug = np.zeros((3, 8, ntok), np.float32)
    kj = np.arange(ntok) % 128
    for i in range(8):
        qaug[0, i] = -sl[i] * 128.0 * (col // 128)
        qaug[1, i] = -sl[i] * (col % 128)
        qaug[2, i] = 1.0
        kaug[0, i] = 1.0
        kaug[1, i] = 1.0
        kaug[2, i] = sl[i] * kj
    maskb = np.zeros((128, 4, 512), np.float32)
    kk = np.arange(128)[:, None]
    for d in range(4):
        maskb[:, d, :] = np.where(128 * d + kk <= col[None, :], 0.0, NEGB)
    bf = ml_dtypes.bfloat16
    return dict(qaug=qaug.astype(bf), kaug=kaug.astype(bf), maskb=maskb.astype(bf), ident=np.eye(128, dtype=np.float32).astype(bf))


def phase_diff(ctx, t, ntok, layer_idx):
    ph = Phase(ctx)
    NT = ntok // 512
    lam_init = diff_lambda_init(layer_idx)
    sl = alibi_slopes8()
    r_c = Res()
    maskb = ph.sbuf([128, 4, 512], BF16)
    ident = ph.sbuf([128, 128], BF16)
    ones32 = ph.sbuf([128, 128], F32)
    mh = ph.sbuf([128, 512], F32)
    gqk = ph.sbuf([64, 2], F32)
    lamv = ph.sbuf([64, 4], F32)
    gsub = ph.sbuf([128, 1], F32)
    lamw = ph.sbuf([128, 4], F32)
    ph.dma("sp", maskb[:], t["maskb"], writes=[r_c])
    ph.dma("sp", ident[:], t["ident"], writes=[r_c])
    ph.dma("sp", gqk[:, 0:1], t["gq"], writes=[r_c])
    ph.dma("sp", gqk[:, 1:2], t["gk"], writes=[r_c])
    ph.dma("sp", lamv[:], t["lam"], writes=[r_c])
    ph.dma("sp", gsub[:], t["gsub"], writes=[r_c])
    ph.op("pool", lambda e: e.memset(ones32[:], 1.0), writes=[r_c])
    epsc = ph.sbuf([128, 1], F32)
    ph.op("pool", lambda e: e.memset(epsc[:], EPS), writes=[r_c])
    ph.op("dve", lambda e: e.tensor_scalar(out=gqk[:, 0:1], in0=gqk[:, 0:1], scalar1=64.0 ** -0.5, scalar2=None, op0=ALU.mult),
          reads=[r_c], writes=[r_c])
    ph.op("dve", lambda e: e.tensor_scalar(out=gsub[:], in0=gsub[:], scalar1=1.0 - lam_init, scalar2=None, op0=ALU.mult),
          reads=[r_c], writes=[r_c])
    projrot = Rot(ph, 2, [128, 512], F32, psum=True)
    nrmrot = Rot(ph, 1, [128, 512], F32, psum=True)
    scrot = Rot(ph, 3, [128, 512], F32, psum=True)
    urot = Rot(ph, 1, [128, 512], F32, psum=True)
    trp = ph.psum([128, 1024], BF16)
    r_trp = Res()
    trps = [(trp[:, i * 512:(i + 1) * 512], r_trp) for i in range(2)]
    ph.op("dve", lambda e: e.tensor_tensor(out=lamv[:, 0:1], in0=lamv[:, 0:1], in1=lamv[:, 1:2], op=ALU.mult), reads=[r_c], writes=[r_c])
    ph.op("dve", lambda e: e.tensor_tensor(out=lamv[:, 1:2], in0=lamv[:, 2:3], in1=lamv[:, 3:4], op=ALU.mult), reads=[r_c], writes=[r_c])
    pn, r_pn = nrmrot.next()
    ph.op("pe", lambda e: e.matmul(pn[:, 0:2], ones32[0:64, :], lamv[:, 0:2], start=True, stop=True), reads=[r_c], writes=[r_pn])
    ph.op("act", lambda e: e.activation(out=lamw[:, 0:2], in_=pn[:, 0:2], func=AF.Exp), reads=[r_pn], writes=[r_c])
    ph.op("dve", lambda e: e.tensor_tensor(out=lamw[:, 2:3], in0=lamw[:, 1:2], in1=lamw[:, 0:1], op=ALU.subtract), reads=[r_c], writes=[r_c])
    ph.op("dve", lambda e: e.tensor_scalar(out=lamw[:, 2:3], in0=lamw[:, 2:3], scalar1=-lam_init, scalar2=None, op0=ALU.add),
          reads=[r_c], writes=[r_c])

    whrot = Rot(ph, 2, [128, 8, 384], BF16)
    hnrot = Rot(ph, 2, [128, 8, 512], BF16)
    ka = [ph.sbuf([128, ntok], BF16) for _ in range(2)]
    r_ka = [[Res() for _ in range(NT)] for _ in range(2)]
    r_kaug = Res()
    vtm = ph.sbuf([128, ntok // 128, 128], BF16)
    r_vtm = [Res() for _ in range(NT)]
    qa = [[ph.sbuf([128, 512], BF16) for _ in range(2)] for _ in range(2)]
    r_qa = [[Res() for _ in range(2)] for _ in range(2)]
    vtrot = Rot(ph, 2, [128, 512], BF16)
    sqrot = Rot(ph, 2, [128, 512], F32)
    tmprot = Rot(ph, 4, [128, 512], F32)
    ptrot = Rot(ph, 6, [128, 512], BF16)
    zD = Rot(ph, 2, [128, 512], F32)
    zP = Rot(ph, 2, [128, 512], F32)
    o1rot = Rot(ph, 2, [128, 512], F32)
    outrot = Rot(ph, 2, [128, 512], BF16)
    cnt = {"tr": 0}

    hn_v = t["hnT"].rearrange("(c p) t -> p c t", p=128)
    wv = t["w"].rearrange("(c p) n -> p c n", p=128)
    y_v = t["yT"].rearrange("(c p) t -> p c t", p=128)

    def project(i, tt, wh, r_wh):
        hn, r_hn = hnrot.next()
        ph.dma("sp", hn[:], hn_v[:, :, tt * 512:(tt + 1) * 512], writes=[r_hn])
        tsl = slice(tt * 512, (tt + 1) * 512)
        for hd in range(2):
            for which in range(2):
                ps, r_ps = projrot.next()
                c0 = which * 128 + hd * 64
                for c in range(8):
                    ph.op("pe", lambda e, ps=ps, c=c, c0=c0, hn=hn, wh=wh: e.matmul(ps[0:64, :], wh[:, c, c0:c0 + 64], hn[:, c, :], start=(c == 0), stop=(c == 7)),
                          reads=[r_wh, r_hn], writes=[r_ps])
                sq, r_sq = sqrot.next()
                ph.op("act", lambda e, sq=sq, ps=ps: e.activation(out=sq[0:64, :], in_=ps[0:64, :], func=AF.Square), reads=[r_ps], writes=[r_sq])
                pn, r_pn = nrmrot.next()
                ph.op("pe", lambda e, pn=pn, sq=sq: e.matmul(pn[0:64, :], ones32[0:64, 0:64], sq[0:64, :], start=True, stop=True), reads=[r_sq, r_c], writes=[r_pn])
                tmp, r_tmp = tmprot.next()
                ph.op("act", lambda e, tmp=tmp, pn=pn: e.activation(out=tmp[0:64, :], in_=pn[0:64, :], func=AF.Ln, scale=1.0 / 64, bias=epsc[0:64, 0:1]),
                      reads=[r_pn, r_c], writes=[r_tmp])
                rstd, r_rstd = tmprot.next()
                ph.op("act", lambda e, tmp=tmp, rstd=rstd: e.activation(out=rstd[0:64, :], in_=tmp[0:64, :], func=AF.Exp, scale=-0.5),
                      reads=[r_tmp], writes=[r_rstd])
                if which == 0:
                    dst, r_dst = qa[hd][tt % 2][0:64, :], r_qa[hd][tt % 2]
                else:
                    dst, r_dst = ka[hd][0:64, tsl], r_ka[hd][tt]
                ph.op("dve", lambda e, ps=ps, rstd=rstd, dst=dst, which=which: e.scalar_tensor_tensor(
                    out=dst, in0=ps[0:64, :], scalar=gqk[:, which:which + 1], in1=rstd[0:64, :], op0=ALU.mult, op1=ALU.mult),
                    reads=[r_ps, r_rstd, r_c], writes=[r_dst])
        ps, r_ps = projrot.next()
        for c in range(8):
            ph.op("pe", lambda e, ps=ps, c=c, hn=hn, wh=wh: e.matmul(ps[:], wh[:, c, 256:384], hn[:, c, :], start=(c == 0), stop=(c == 7)),
                  reads=[r_wh, r_hn], writes=[r_ps])
        vt, r_vt = vtrot.next()
        ph.op("act", lambda e, vt=vt, ps=ps: e.activation(out=vt[:], in_=ps[:], func=AF.Copy), reads=[r_ps], writes=[r_vt])
        tp, r_tp = trps[cnt["tr"] % 2]; cnt["tr"] += 1
        for b in range(4):
            ph.op("pe", lambda e, tp=tp, vt=vt, b=b: e.transpose(tp[:, b * 128:(b + 1) * 128], vt[:, b * 128:(b + 1) * 128], ident[:]),
                  reads=[r_vt, r_c], writes=[r_tp])
        ph.op("act", lambda e, tp=tp, tt=tt: e.activation(out=vtm[:, tt * 4:(tt + 1) * 4, :], in_=tp.rearrange("p (b n) -> p b n", b=4), func=AF.Copy),
              reads=[r_tp], writes=[r_vtm[tt]])

    def attend(i, tt):
        nkb = 4 * tt + 4
        o1 = None
        for hd in range(2):
            q, r_q = qa[hd][tt % 2], r_qa[hd][tt % 2]
            U, r_U = urot.next()
            zd, r_zd = zD.next()
            zp, r_zp = zP.next()
            LAG = 2
            pts = {}
            for step in range(nkb + LAG):
                kb = step
                if kb < nkb:
                    diag = kb >= 4 * tt
                    sc, r_sc = scrot.next()
                    ph.op("pe", lambda e, sc=sc, kb=kb, q=q, hd=hd, diag=diag: e.matmul(
                        sc[:], ka[hd][0:67, kb * 128:(kb + 1) * 128], q[0:67, :], start=True, stop=(not diag)),
                        reads=[r_ka[hd][kb // 4], r_kaug, r_q], writes=[r_sc])
                    if diag:
                        ph.op("pe", lambda e, sc=sc, d=kb - 4 * tt: e.matmul(sc[:], ident[:], maskb[:, d, :], start=False, stop=True),
                              reads=[r_c], writes=[r_sc])
                    pt, r_pt = ptrot.next()
                    pts[kb] = (pt, r_pt)
                    cbias = -sl[i] * 128.0 * (4 * tt - kb)
                    ph.op("act", lambda e, pt=pt, sc=sc, cbias=cbias: e.activation(out=pt[:], in_=sc[:], func=AF.Exp, bias=cbias),
                          reads=[r_sc], writes=[r_pt])
                    eng, z, r_z = ("dve", zd, r_zd) if kb % 2 == 0 else ("pool", zp, r_zp)
                    if kb < 2:
                        ph.op(eng, lambda e, z=z, pt=pt: e.tensor_copy(out=z[:], in_=pt[:]), reads=[r_pt], writes=[r_z])
                    else:
                        ph.op(eng, lambda e, z=z, pt=pt: e.tensor_tensor(out=z[:], in0=z[:], in1=pt[:], op=ALU.add), reads=[r_pt, r_z], writes=[r_z])
                kv = step - LAG
                if kv >= 0:
                    pt, r_pt = pts.pop(kv)
                    ph.op("pe", lambda e, U=U, pt=pt, kv=kv, nkb=nkb: e.matmul(U[:], vtm[:, kv, :], pt[:], start=(kv == 0), stop=(kv == nkb - 1)),
                          reads=[r_vtm[kv // 4], r_pt], writes=[r_U])
            pn, r_pn = nrmrot.next()
            ph.op("pe", lambda e, pn=pn, zd=zd: e.matmul(pn[:], ones32[:], zd[:], start=True, stop=False), reads=[r_zd, r_c], writes=[r_pn])
            ph.op("pe", lambda e, pn=pn, zp=zp: e.matmul(pn[:], ones32[:], zp[:], start=False, stop=True), reads=[r_zp, r_c], writes=[r_pn])
            rz, r_rz = tmprot.next()
            ph.op("dve", lambda e, rz=rz, pn=pn: e.reciprocal(out=rz[:], in_=pn[:]), reads=[r_pn], writes=[r_rz])
            if hd == 0:
                o1, r_o1 = o1rot.next()
                ph.op("dve", lambda e, o1=o1, U=U, rz=rz: e.tensor_tensor(out=o1[:], in0=U[:], in1=rz[:], op=ALU.mult), reads=[r_U, r_rz], writes=[r_o1])
            else:
                ph.op("dve", lambda e, U=U, rz=rz: e.tensor_tensor(out=rz[:], in0=U[:], in1=rz[:], op=ALU.mult), reads=[r_U, r_rz], writes=[r_rz])
                ph.op("dve", lambda e, o1=o1, rz=rz: e.scalar_tensor_tensor(out=o1[:], in0=rz[:], scalar=lamw[:, 2:3], in1=o1[:], op0=ALU.mult, op1=ALU.add),
                      reads=[r_rz, r_o1, r_c], writes=[r_o1])
        sq, r_sq = sqrot.next()
        ph.op("act", lambda e, sq=sq, o1=o1: e.activation(out=sq[:], in_=o1[:], func=AF.Square), reads=[r_o1], writes=[r_sq])
        pn, r_pn = nrmrot.next()
        ph.op("pe", lambda e, pn=pn, sq=sq: e.matmul(pn[:], ones32[:], sq[:], start=True, stop=True), reads=[r_sq, r_c], writes=[r_pn])
        tmp, r_tmp = tmprot.next()
        ph.op("act", lambda e, tmp=tmp, pn=pn: e.activation(out=tmp[:], in_=pn[:], func=AF.Ln, scale=1.0 / 128, bias=epsc[:, 0:1]),
              reads=[r_pn, r_c], writes=[r_tmp])
        rstd, r_rstd = tmprot.next()
        ph.op("act", lambda e, tmp=tmp, rstd=rstd: e.activation(out=rstd[:], in_=tmp[:], func=AF.Exp, scale=-0.5), reads=[r_tmp], writes=[r_rstd])
        ot, r_ot = outrot.next()
        ph.op("dve", lambda e, ot=ot, o1=o1, rstd=rstd: e.scalar_tensor_tensor(out=ot[:], in0=o1[:], scalar=gsub[:, 0:1], in1=rstd[:], op0=ALU.mult, op1=ALU.mult),
              reads=[r_o1, r_rstd, r_c], writes=[r_ot])
        ph.dma("sp", y_v[:, i, tt * 512:(tt + 1) * 512], ot[:], reads=[r_ot])

    for i in range(8):
        wh, r_wh = whrot.next()
        ph.dma("pool", wh[:, :, 0:128], wv[:, :, i * 128:(i + 1) * 128], writes=[r_wh])
        ph.dma("pool", wh[:, :, 128:256], wv[:, :, 1024 + i * 128:1024 + (i + 1) * 128], writes=[r_wh])
        ph.dma("pool", wh[:, :, 256:384], wv[:, :, 2048 + i * 128:2048 + (i + 1) * 128], writes=[r_wh])
        for hd in range(2):
            ph.dma("sp", ka[hd][64:67, :], t["kaug"][:, i, :], writes=[r_kaug] if hd == 1 else [r_kaug])
            for par in range(2):
                ph.dma("sp", qa[hd][par][64:67, :], t["qaug"][:, i, :], writes=[r_qa[hd][par]])
        pend = None
        for tt in range(NT):
            project(i, tt, wh, r_wh)
            if pend is not None:
                attend(i, pend)
            pend = tt
        attend(i, pend)
    ph.end()


def build_program(ntok):
    nc = bass.Bass("TRN2", target_bir_lowering=False)

    def din(name, shape, dt=F32):
        return nc.dram_tensor(name, list(shape), dt, kind="ExternalInput").ap()

    def dint(name, shape, dt=F32):
        return nc.dram_tensor(name, list(shape), dt).ap()

    xT = din("xT", [D, ntok])
    pT = din("pT", [DEPTH, 256, ntok])
    vecs = din("vecs", [DEPTH + 1, 128, 32])
    a_w = din("a_w", [2, 2, D, 3072])
    a_wo = din("a_wo", [2, 2048, D])
    b_w = din("b_w", [D, 7168])
    b_wo = din("b_wo", [D, D])
    c_w = din("c_w", [D, 3072])
    c_wo = din("c_wo", [D, D])
    f_wi = din("f_wi", [DEPTH, D, 2 * HID])
    f_wo = din("f_wo", [DEPTH, HID, D])
    p_wp = din("p_wp", [DEPTH, 256, D])
    p_wg = din("p_wg", [DEPTH, D, D])
    rc = dict(dq=din("r_dq", [2, 128, 2, 512]), dkt=din("r_dkt", [2, 128, 2, 512]), dkm=din("r_dkm", [2, 128, 2]),
              mask=din("r_mask", [128, 128]))
    ident = din("ident", [128, 128], BF16)
    b_gq = din("b_gq", [128, 1]); b_gk = din("b_gk", [128, 1]); b_bias = din("b_bias", [128, 24, 256], BF16)
    c_gq = din("c_gq", [64, 1]); c_gk = din("c_gk", [64, 1]); c_lam = din("c_lam", [64, 4]); c_gsub = din("c_gsub", [128, 1])
    c_qaug = din("c_qaug", [3, 8, 512], BF16); c_kaug = din("c_kaug", [3, 8, ntok], BF16); c_maskb = din("c_maskb", [128, 4, 512], BF16)
    outT = nc.dram_tensor("outT", [D, ntok], F32, kind="ExternalOutput").ap()
    hA = dint("hA", [D, ntok]); hB = dint("hB", [D, ntok])
    hnT = dint("hnT", [D, ntok], BF16)
    yT = dint("yT", [2048, ntok], BF16)

    ctx = Ctx(nc)
    phase_token_local(ctx, dict(hT=xT, vecs=vecs[0], hnT_out=hnT), 0, ntok, True, False)
    h_in = xT
    for i in range(DEPTH):
        kind, j = i % 3, i // 3
        if kind == 0:
            for r in range(2):
                phase_retention(ctx, dict(hnT=hnT, w=a_w[j, r], dq=rc["dq"][r], dkt=rc["dkt"][r], dkm=rc["dkm"][r], mask=rc["mask"],
                                          ident=ident, yT=yT[r * 1024:(r + 1) * 1024, :]),
                                [ret_gamma(2 * r), ret_gamma(2 * r + 1)], ntok)
            kout, wout = 2048, a_wo[j]
        elif kind == 1:
            phase_dilated(ctx, dict(hnT=hnT, w=b_w, gq=b_gq, gk=b_gk, bias=b_bias, ident=ident, yT=yT[0:1024, :]), ntok)
            kout, wout = 1024, b_wo
        else:
            phase_diff(ctx, dict(hnT=hnT, w=c_w, gq=c_gq, gk=c_gk, lam=c_lam, gsub=c_gsub, qaug=c_qaug, kaug=c_kaug, maskb=c_maskb,
                                 ident=ident, yT=yT[0:1024, :]), ntok, i)
            kout, wout = 1024, c_wo
        last = (i == DEPTH - 1)
        h_out = outT if last else (hA if i % 2 == 0 else hB)
        phase_token_local(ctx, dict(hT=h_in, yT=yT[0:kout, :], pT=pT[i], wout=wout, win=f_wi[i], wo=f_wo[i], wg=p_wg[i], wp=p_wp[i],
                                    vecs=vecs[i + 1], hT_out=h_out, hnT_out=hnT), kout, ntok, False, last)
        h_in = h_out
    ctx.finish()
    return nc


def _colvec(v):
    return np.ascontiguousarray(np.asarray(v, np.float32).reshape(-1, 128).T)


def prepare_inputs(inp, nb, ntok):
    f32 = np.float32
    g = {k: np.asarray(v) for k, v in inp.items()}
    vecs = np.zeros((DEPTH + 1, 128, 32), f32)
    vecs[0, :, 24:32] = _colvec(g["mix_norm"][0])
    for i in range(DEPTH):
        vecs[i + 1, :, 0:8] = _colvec(g["ffn_norm"][i])
        vecs[i + 1, :, 8:16] = _colvec(g["ple_gate_norm"][i])
        vecs[i + 1, :, 16:24] = _colvec(g["ple_norm"][i])
        if i + 1 < DEPTH:
            vecs[i + 1, :, 24:32] = _colvec(g["mix_norm"][i + 1])
    a_w = np.zeros((2, 2, D, 3072), f32)
    for j in range(2):
        w = g["a_w_in"][j]
        for r in range(2):
            parts = []
            for h in (2 * r, 2 * r + 1):
                parts += [w[:, h * 256:(h + 1) * 256], w[:, 1024 + h * 256:1024 + (h + 1) * 256],
                          w[:, 2048 + h * 512:2048 + (h + 1) * 512], w[:, 4096 + h * 512:4096 + (h + 1) * 512]]
            a_w[j, r] = np.concatenate(parts, axis=1)
    rcs = [ret_consts(r) for r in range(2)]
    dc = dil_consts()
    fc = diff_consts(ntok)
    shared = dict(
        vecs=vecs, a_w=a_w, a_wo=g["a_w_out"].astype(f32), b_w=g["b_w_in"][0], b_wo=g["b_w_out"][0], c_w=g["c_w_in"][0], c_wo=g["c_w_out"][0],
        f_wi=g["ffn_w_in"], f_wo=g["ffn_w_out"], p_wp=g["ple_w_proj"], p_wg=g["ple_w_gate"],
        r_dq=np.stack([c["dq"] for c in rcs]), r_dkt=np.stack([c["dkt"] for c in rcs]), r_dkm=np.stack([c["dkm"] for c in rcs]),
        r_mask=rcs[0]["mask"], ident=rcs[0]["ident"],
        b_gq=g["b_q_norm"][0].reshape(128, 1), b_gk=g["b_k_norm"][0].reshape(128, 1), b_bias=dc["bias"],
        c_gq=g["c_q_norm"][0].reshape(64, 1), c_gk=g["c_k_norm"][0].reshape(64, 1),
        c_lam=np.ascontiguousarray(np.stack([g["c_lambda_q1"][0], g["c_lambda_k1"][0], g["c_lambda_q2"][0], g["c_lambda_k2"][0]], axis=1).astype(f32)),
        c_gsub=g["c_subln"][0].reshape(128, 1), c_qaug=fc["qaug"], c_kaug=fc["kaug"], c_maskb=fc["maskb"])
    shared = {k: np.ascontiguousarray(v) for k, v in shared.items()}
    in_maps = []
    for b in range(nb):
        m = dict(shared)
        m["xT"] = np.ascontiguousarray(g["x"][b, :ntok].T)
        m["pT"] = np.ascontiguousarray(np.transpose(g["p"][:, b, :ntok, :], (0, 2, 1)))
        in_maps.append(m)
    return in_maps


def run_model(inp, nb, ntok):
    nc = build_program(ntok)
    in_maps = prepare_inputs(inp, nb, ntok)
    res = run_bass_kernel_spmd(nc, in_maps, core_ids=list(range(nb)))
    out = np.stack([np.ascontiguousarray(res.results[b]["outT"].T) for b in range(nb)], axis=0)
    return out.astype(np.float32)


def kernel(**inputs):
    return run_model(inputs, B, S)
```

```python
import contextlib
import math
import numpy as np
import concourse.bass as bass
import concourse.mybir as mybir
from concourse.bass_utils import run_bass_kernel_spmd

F32 = mybir.dt.float32
BF16 = mybir.dt.bfloat16
ALU = mybir.AluOpType
AF = mybir.ActivationFunctionType

D = 1024
S = 8192
B = 4
DEPTH = 4
HID = 2816
EPS = 1e-6

ENGS = ("pe", "act", "dve", "pool", "sp")
NDMASEM = 8
SAME_ENGINE_SYNC = {"pe": False, "act": True, "dve": True, "pool": True, "sp": False}


class Res:
    __slots__ = ("w", "rs")

    def __init__(self):
        self.w = None
        self.rs = []


class Op:
    __slots__ = ("eng", "fn", "deps", "dma", "need_inc", "sem", "val", "dslot")

    def __init__(self, eng, fn, dma):
        self.eng = eng
        self.fn = fn
        self.dma = dma
        self.deps = []
        self.need_inc = False
        self.sem = None
        self.val = None
        self.dslot = 0


class Ctx:
    def __init__(self, nc):
        self.nc = nc
        self.stack = contextlib.ExitStack()
        st = self.stack
        self.esem = {e: st.enter_context(nc.semaphore(f"s_{e}")) for e in ENGS}
        self.dsem = {e: [st.enter_context(nc.semaphore(f"d_{e}{i}")) for i in range(NDMASEM)] for e in ENGS}
        self.ecount = {e: 0 for e in ENGS}
        self.dcount = {e: 0 for e in ENGS}
        self.seen = {e: {} for e in ENGS}
        self.barrier = []
        self.nuid = 0

    def all_signals(self):
        sig = []
        for e in ENGS:
            if self.ecount[e] > 0:
                sig.append((self.esem[e], self.ecount[e]))
            nd = self.dcount[e]
            for i in range(min(nd, NDMASEM)):
                n_i = (nd - 1 - i) // NDMASEM + 1
                sig.append((self.dsem[e][i], 16 * n_i))
        return sig

    def finish(self):
        nc = self.nc
        sig = self.all_signals()
        ctx = self
        with nc.Block() as block:
            @block.sync
            def _(eng):
                for (k, v) in sig:
                    if ctx.seen["sp"].get(id(k), 0) < v:
                        eng.wait_ge(k, v)
        self.stack.close()


class Phase:
    def __init__(self, ctx):
        self.ctx = ctx
        self.nc = ctx.nc
        self.q = {e: [] for e in ENGS}
        self.stack = contextlib.ExitStack()

    def sbuf(self, shape, dtype):
        self.ctx.nuid += 1
        return self.stack.enter_context(self.nc.sbuf_tensor(f"sb{self.ctx.nuid}", list(shape), dtype))

    def psum(self, shape, dtype=F32):
        self.ctx.nuid += 1
        return self.stack.enter_context(self.nc.psum_tensor(f"ps{self.ctx.nuid}", list(shape), dtype))

    def op(self, eng, fn, reads=(), writes=(), dma=False):
        o = Op(eng, fn, dma)
        deps = []
        for r in reads:
            if r.w is not None:
                deps.append(r.w)
        for w in writes:
            if w.w is not None:
                deps.append(w.w)
            deps.extend(w.rs)
        seen = set()
        for d in deps:
            if id(d) in seen or d is o:
                continue
            seen.add(id(d))
            if d.eng == eng and not d.dma and not SAME_ENGINE_SYNC[eng]:
                continue
            o.deps.append(d)
            d.need_inc = True
        for r in reads:
            r.rs.append(o)
        for w in writes:
            w.w = o
            w.rs = []
        self.q[eng].append(o)
        return o

    def dma(self, eng, out, in_, reads=(), writes=()):
        return self.op(eng, lambda e: e.dma_start(out=out, in_=in_), reads, writes, dma=True)

    def end(self):
        ctx = self.ctx
        nc = self.nc
        for e in ENGS:
            last = None
            for o in self.q[e]:
                if not o.dma:
                    last = o
            if last is not None:
                last.need_inc = True
        for e in ENGS:
            for o in self.q[e]:
                if o.dma:
                    nd = ctx.dcount[e]
                    o.sem = ctx.dsem[e][nd % NDMASEM]
                    o.val = 16 * (nd // NDMASEM + 1)
                    o.dslot = nd
                    ctx.dcount[e] = nd + 1
                elif o.need_inc:
                    ctx.ecount[e] += 1
                    o.sem = ctx.esem[e]
                    o.val = ctx.ecount[e]
        barrier = ctx.barrier
        ph = self

        def replay(e, eng):
            seen = ctx.seen[e]
            for (k, v) in barrier:
                if seen.get(id(k), 0) < v:
                    eng.wait_ge(k, v)
                    seen[id(k)] = v
            for o in ph.q[e]:
                need = {}
                for d in o.deps:
                    kk = id(d.sem)
                    if kk not in need or need[kk][1] < d.val:
                        need[kk] = (d.sem, d.val)
                if o.dma and o.dslot >= NDMASEM:
                    kk = id(o.sem)
                    v = o.val - 16
                    if kk not in need or need[kk][1] < v:
                        need[kk] = (o.sem, v)
                for kk, (k, v) in need.items():
                    if seen.get(kk, 0) < v:
                        eng.wait_ge(k, v)
                        seen[kk] = v
                ins = o.fn(eng)
                if o.dma:
                    ins.then_inc(o.sem, 16)
                elif o.need_inc:
                    ins.then_inc(o.sem, 1)

        with nc.Block() as block:
            @block.tensor
            def _(eng):
                replay("pe", eng)

            @block.scalar
            def _(eng):
                replay("act", eng)

            @block.vector
            def _(eng):
                replay("dve", eng)

            @block.gpsimd
            def _(eng):
                replay("pool", eng)

            @block.sync
            def _(eng):
                replay("sp", eng)
        ctx.barrier = ctx.all_signals()
        self.stack.close()


class Rot:
    def __init__(self, ph, n, shape, dtype, psum=False):
        self.bufs = [(ph.psum(shape, dtype) if psum else ph.sbuf(shape, dtype), Res()) for _ in range(n)]
        self.i = 0

    def next(self):
        b = self.bufs[self.i % len(self.bufs)]
        self.i += 1
        return b


def emit_rmsnorm_fm(ph, src, r_src, nchunk, T0, TN, gain, r_gain, dst, r_dst, env, dim, gain_col0=0):
    ones, r_ones, mh, r_mh = env["ones"], env["r_ones"], env["mh"], env["r_mh"]
    pst, r_pst = env["psrot"].next()
    for c in range(nchunk):
        sq, r_sq = env["sqrot"].next()
        ph.op("act", lambda e, c=c, sq=sq: e.activation(out=sq[:, :TN], in_=src[:, c, T0:T0 + TN], func=AF.Square),
              reads=[r_src], writes=[r_sq])
        ph.op("pe", lambda e, c=c, sq=sq, pst=pst: e.matmul(pst[:, :TN], ones[:], sq[:, :TN], start=(c == 0), stop=(c == nchunk - 1)),
              reads=[r_ones, r_sq], writes=[r_pst])
    tmp, r_tmp = env["tmprot"].next()
    ph.op("act", lambda e, tmp=tmp, pst=pst: e.activation(out=tmp[:, :TN], in_=pst[:, :TN], func=AF.Ln, scale=1.0 / dim, bias=env["epsc"][:, 0:1]),
          reads=[r_pst, r_mh], writes=[r_tmp])
    rstd, r_rstd = env["tmprot"].next()
    ph.op("act", lambda e, tmp=tmp, rstd=rstd: e.activation(out=rstd[:, :TN], in_=tmp[:, :TN], func=AF.Exp, scale=-0.5),
          reads=[r_tmp], writes=[r_rstd])
    for c in range(nchunk):
        ph.op("dve", lambda e, c=c, rstd=rstd: e.scalar_tensor_tensor(
            out=dst[:, c, T0:T0 + TN], in0=src[:, c, T0:T0 + TN], scalar=gain[:, gain_col0 + c:gain_col0 + c + 1], in1=rstd[:, :TN],
            op0=ALU.mult, op1=ALU.mult), reads=[r_src, r_gain, r_rstd], writes=[r_dst])


def make_env(ph):
    env = {}
    env["ones"] = ph.sbuf([128, 128], F32)
    env["r_ones"] = Res()
    env["mh"] = ph.sbuf([128, 512], F32)
    env["r_mh"] = Res()
    ph.op("pool", lambda e: e.memset(env["ones"][:], 1.0), writes=[env["r_ones"]])
    ph.op("pool", lambda e: e.memset(env["mh"][:], -0.5), writes=[env["r_mh"]])
    env["epsc"] = ph.sbuf([128, 1], F32)
    ph.op("pool", lambda e: e.memset(env["epsc"][:], EPS), writes=[env["r_mh"]])
    env["sqrot"] = Rot(ph, 3, [128, 512], F32)
    env["tmprot"] = Rot(ph, 4, [128, 512], F32)
    return env


TT = 1024
ST = 512


def phase_token_local(ctx, t, kout, ntok, first, last):
    ph = Phase(ctx)
    env = make_env(ph)
    psrot = Rot(ph, 8, [128, 512], F32, psum=True)
    env["psrot"] = psrot
    kc_out = kout // 128
    vec = ph.sbuf([128, 32], F32)
    r_vec = Res()
    ph.dma("sp", vec[:], t["vecs"], writes=[r_vec])
    h = ph.sbuf([128, 8, TT], F32)
    r_h = Res()
    hn = ph.sbuf([128, 8, TT], BF16)
    r_hn = Res()
    if not first:
        yg = ph.sbuf([128, 22, TT], BF16)
        r_yg = Res()
        pt = ph.sbuf([128, 2, TT], BF16)
        r_pt = Res()
        et = ph.sbuf([128, 8, TT], F32)
        r_et = Res()
        wrot = Rot(ph, 4, [128, 4096], BF16)
        sarot = Rot(ph, 3, [128, 512], F32)

    hT_in = t["hT"].rearrange("(c p) t -> p c t", p=128)
    for tt in range(ntok // TT):
        tsl = slice(tt * TT, (tt + 1) * TT)
        ph.dma("sp", h[:], hT_in[:, :, tsl], writes=[r_h])
        if first:
            for st in range(TT // ST):
                emit_rmsnorm_fm(ph, h, r_h, 8, st * ST, ST, vec, r_vec, hn, r_hn, env, D, gain_col0=24)
            ph.dma("sp", t["hnT_out"].rearrange("(c p) t -> p c t", p=128)[:, :, tsl], hn[:], reads=[r_hn])
            continue
        ph.dma("sp", yg[:, :kc_out, :], t["yT"].rearrange("(c p) t -> p c t", p=128)[:, :, tsl], writes=[r_yg])
        ph.dma("pool", pt[:], t["pT"].rearrange("(c p) t -> p c t", p=128)[:, :, tsl], writes=[r_pt])
        ncol = 4096 // kc_out
        wv = t["wout"].rearrange("(c p) n -> p c n", p=128)
        for blk in range(D // ncol):
            wb, r_wb = wrot.next()
            wbv = wb[:, :kc_out * ncol].rearrange("p (c n) -> p c n", c=kc_out)
            ph.dma("pool", wbv, wv[:, :, blk * ncol:(blk + 1) * ncol], writes=[r_wb])
            for st in range(TT // ST):
                for fi in range(ncol // 128):
                    f = blk * (ncol // 128) + fi
                    ps, r_ps = psrot.next()
                    for k in range(kc_out):
                        ph.op("pe", lambda e, ps=ps, wbv=wbv, fi=fi, k=k, st=st: e.matmul(
                            ps[:], wbv[:, k, fi * 128:(fi + 1) * 128], yg[:, k, st * ST:(st + 1) * ST],
                            start=(k == 0), stop=(k == kc_out - 1)), reads=[r_wb, r_yg], writes=[r_ps])
                    ph.op("dve", lambda e, ps=ps, f=f, st=st: e.tensor_tensor(
                        out=h[:, f, st * ST:(st + 1) * ST], in0=h[:, f, st * ST:(st + 1) * ST], in1=ps[:], op=ALU.add),
                        reads=[r_ps, r_h], writes=[r_h])
        for st in range(TT // ST):
            emit_rmsnorm_fm(ph, h, r_h, 8, st * ST, ST, vec, r_vec, hn, r_hn, env, D, gain_col0=0)
        wv = t["win"].rearrange("(c p) n -> p c n", p=128)
        for blk in range(HID // 256):
            wb, r_wb = wrot.next()
            wbv = wb[:].rearrange("p (c n) -> p c n", c=8)
            ph.dma("pool", wbv[:, :, 0:256], wv[:, :, blk * 256:(blk + 1) * 256], writes=[r_wb])
            ph.dma("pool", wbv[:, :, 256:512], wv[:, :, HID + blk * 256:HID + (blk + 1) * 256], writes=[r_wb])
            for st in range(TT // ST):
                for jj in range(2):
                    j = blk * 2 + jj
                    pa, r_pa = psrot.next()
                    pb, r_pb = psrot.next()
                    for c in range(8):
                        ph.op("pe", lambda e, pa=pa, wbv=wbv, jj=jj, c=c, st=st: e.matmul(
                            pa[:], wbv[:, c, jj * 128:(jj + 1) * 128], hn[:, c, st * ST:(st + 1) * ST],
                            start=(c == 0), stop=(c == 7)), reads=[r_wb, r_hn], writes=[r_pa])
                    for c in range(8):
                        ph.op("pe", lambda e, pb=pb, wbv=wbv, jj=jj, c=c, st=st: e.matmul(
                            pb[:], wbv[:, c, 256 + jj * 128:256 + (jj + 1) * 128], hn[:, c, st * ST:(st + 1) * ST],
                            start=(c == 0), stop=(c == 7)), reads=[r_wb, r_hn], writes=[r_pb])
                    sa, r_sa = sarot.next()
                    ph.op("act", lambda e, sa=sa, pa=pa: e.activation(out=sa[:], in_=pa[:], func=AF.Silu), reads=[r_pa], writes=[r_sa])
                    ph.op("dve", lambda e, sa=sa, pb=pb, j=j, st=st: e.tensor_tensor(
                        out=yg[:, j, st * ST:(st + 1) * ST], in0=sa[:], in1=pb[:], op=ALU.mult),
                        reads=[r_sa, r_pb], writes=[r_yg])
        wv = t["wo"].rearrange("(c p) n -> p c n", p=128)
        for f in range(8):
            wb, r_wb = wrot.next()
            wbv = wb[:, :22 * 128].rearrange("p (c n) -> p c n", c=22)
            ph.dma("pool", wbv, wv[:, :, f * 128:(f + 1) * 128], writes=[r_wb])
            for st in range(TT // ST):
                ps, r_ps = psrot.next()
                for k in range(22):
                    ph.op("pe", lambda e, ps=ps, wbv=wbv, k=k, st=st: e.matmul(
                        ps[:], wbv[:, k, :], yg[:, k, st * ST:(st + 1) * ST], start=(k == 0), stop=(k == 21)),
                        reads=[r_wb, r_yg], writes=[r_ps])
                ph.op("dve", lambda e, ps=ps, f=f, st=st: e.tensor_tensor(
                    out=h[:, f, st * ST:(st + 1) * ST], in0=h[:, f, st * ST:(st + 1) * ST], in1=ps[:], op=ALU.add),
                    reads=[r_ps, r_h], writes=[r_h])
        for st in range(TT // ST):
            emit_rmsnorm_fm(ph, h, r_h, 8, st * ST, ST, vec, r_vec, hn, r_hn, env, D, gain_col0=8)
        wb, r_wb = wrot.next()
        wbv = wb[:, :2048].rearrange("p (c n) -> p c n", c=2)
        ph.dma("pool", wbv, t["wp"].rearrange("(c p) n -> p c n", p=128), writes=[r_wb])
        for st in range(TT // ST):
            for f in range(8):
                ps, r_ps = psrot.next()
                for k in range(2):
                    ph.op("pe", lambda e, ps=ps, wbv=wbv, k=k, f=f, st=st: e.matmul(
                        ps[:], wbv[:, k, f * 128:(f + 1) * 128], pt[:, k, st * ST:(st + 1) * ST], start=(k == 0), stop=(k == 1)),
                        reads=[r_wb, r_pt], writes=[r_ps])
                ph.op("act", lambda e, ps=ps, f=f, st=st: e.activation(out=et[:, f, st * ST:(st + 1) * ST], in_=ps[:], func=AF.Copy),
                      reads=[r_ps], writes=[r_et])
            emit_rmsnorm_fm(ph, et, r_et, 8, st * ST, ST, vec, r_vec, et, r_et, env, D, gain_col0=16)
        wv = t["wg"].rearrange("(c p) n -> p c n", p=128)
        for blk in range(2):
            wb, r_wb = wrot.next()
            wbv = wb[:].rearrange("p (c n) -> p c n", c=8)
            ph.dma("pool", wbv, wv[:, :, blk * 512:(blk + 1) * 512], writes=[r_wb])
            for st in range(TT // ST):
                for fi in range(4):
                    f = blk * 4 + fi
                    ps, r_ps = psrot.next()
                    for c in range(8):
                        ph.op("pe", lambda e, ps=ps, wbv=wbv, fi=fi, c=c, st=st: e.matmul(
                            ps[:], wbv[:, c, fi * 128:(fi + 1) * 128], hn[:, c, st * ST:(st + 1) * ST],
                            start=(c == 0), stop=(c == 7)), reads=[r_wb, r_hn], writes=[r_ps])
                    sa, r_sa = sarot.next()
                    ph.op("act", lambda e, sa=sa, ps=ps: e.activation(out=sa[:], in_=ps[:], func=AF.Tanh, scale=0.5),
                          reads=[r_ps], writes=[r_sa])
                    ph.op("dve", lambda e, sa=sa: e.tensor_scalar(out=sa[:], in0=sa[:], scalar1=0.5, scalar2=0.5, op0=ALU.mult, op1=ALU.add),
                          reads=[r_sa], writes=[r_sa])
                    ph.op("dve", lambda e, sa=sa, f=f, st=st: e.tensor_tensor(
                        out=sa[:], in0=sa[:], in1=et[:, f, st * ST:(st + 1) * ST], op=ALU.mult), reads=[r_sa, r_et], writes=[r_sa])
                    ph.op("dve", lambda e, sa=sa, f=f, st=st: e.tensor_tensor(
                        out=h[:, f, st * ST:(st + 1) * ST], in0=h[:, f, st * ST:(st + 1) * ST], in1=sa[:], op=ALU.add),
                        reads=[r_sa, r_h], writes=[r_h])
        ph.dma("sp", t["hT_out"].rearrange("(c p) t -> p c t", p=128)[:, :, tsl], h[:], reads=[r_h])
        if not last:
            for st in range(TT // ST):
                emit_rmsnorm_fm(ph, h, r_h, 8, st * ST, ST, vec, r_vec, hn, r_hn, env, D, gain_col0=24)
            ph.dma("sp", t["hnT_out"].rearrange("(c p) t -> p c t", p=128)[:, :, tsl], hn[:], reads=[r_hn])
    ph.end()


def ret_gamma(h):
    return float(np.float32(1.0) - np.float32(2.0) ** np.float32(-5.0 - h))


def ret_consts(r):
    import ml_dtypes
    pos = np.arange(512) % 128
    dq = np.zeros((128, 2, 512), np.float32)
    dkt = np.zeros((128, 2, 512), np.float32)
    dkm = np.zeros((128, 2), np.float32)
    for hh in range(2):
        lg = np.log(np.float64(ret_gamma(2 * r + hh)))
        dq[:, hh, :] = np.exp((pos + 1.0) * lg)[None, :]
        dkt[:, hh, :] = (np.exp(-(pos + 1.0) * lg) * (256.0 ** -0.5))[None, :]
        dkm[:, hh] = np.exp((127.0 - np.arange(128)) * lg) * (256.0 ** -0.5)
    m = np.arange(128)
    mask = (m[:, None] <= m[None, :]).astype(np.float32)
    ident = np.eye(128, dtype=np.float32).astype(ml_dtypes.bfloat16)
    return dict(dq=dq, dkt=dkt, dkm=dkm, mask=mask, ident=ident)


def phase_retention(ctx, t, r_role_gammas, ntok):
    ph = Phase(ctx)
    NT = ntok // 512
    gC = [g ** 128 for g in r_role_gammas]
    w = ph.sbuf([128, 8, 3072], BF16)
    r_w = Res()
    wv = t["w"].rearrange("(c p) n -> p c n", p=128)
    for hh in range(2):
        for part in range(3):
            sl = slice(hh * 1536 + part * 512, hh * 1536 + (part + 1) * 512)
            ph.dma("pool", w[:, :, sl], wv[:, :, sl], writes=[r_w])
    dq = ph.sbuf([128, 2, 512], F32); r_c = Res()
    dkt = ph.sbuf([128, 2, 512], F32)
    dkm = ph.sbuf([128, 2], F32)
    mask = ph.sbuf([128, 128], F32)
    ident = ph.sbuf([128, 128], BF16)
    mh = ph.sbuf([128, 1], F32)
    ph.dma("sp", dq[:], t["dq"], writes=[r_c])
    ph.dma("sp", dkt[:], t["dkt"], writes=[r_c])
    ph.dma("sp", dkm[:], t["dkm"], writes=[r_c])
    ph.dma("sp", mask[:], t["mask"], writes=[r_c])
    ph.dma("sp", ident[:], t["ident"], writes=[r_c])
    epsc = ph.sbuf([128, 1], F32)
    ph.op("pool", lambda e: e.memset(epsc[:], EPS), writes=[r_c])

    hnrot = Rot(ph, 2, [128, 8, 512], BF16)
    qrot = Rot(ph, 2, [128, 2, 512], BF16)
    ktrot = Rot(ph, 2, [128, 2, 512], BF16)
    kmrot = Rot(ph, 2, [128, 4, 256], BF16)
    vrot = Rot(ph, 2, [128, 4, 512], BF16)
    sgrot = Rot(ph, 2, [128, 4, 512], BF16)
    R32 = [ph.sbuf([128, 2, 512], F32) for _ in range(2)]
    r_R32 = [Res(), Res()]
    Rb = [ph.sbuf([128, 2, 512], BF16) for _ in range(2)]
    r_Rb = [Res(), Res()]
    attrot = Rot(ph, 2, [128, 128], BF16)
    ygrot = Rot(ph, 2, [128, 512], BF16)
    ygtrot = Rot(ph, 2, [128, 4, 512], BF16)
    junk = ph.sbuf([128, 512], BF16); r_junk = Res()
    ssrot = Rot(ph, 4, [128, 1], F32)
    projrot = Rot(ph, 2, [128, 512], F32, psum=True)
    attp = ph.psum([128, 512], F32)
    r_attp = Res()
    attps = [(attp[:, i * 128:(i + 1) * 128], r_attp) for i in range(4)]
    yrot = Rot(ph, 2, [128, 512], F32, psum=True)
    rprot = Rot(ph, 2, [128, 512], F32, psum=True)
    trp = ph.psum([128, 1024], BF16)
    r_trp = Res()
    trps = [(trp[:, i * 512:(i + 1) * 512], r_trp) for i in range(2)]
    cnt = {"att": 0, "tr": 0}

    hn_v = t["hnT"].rearrange("(c p) t -> p c t", p=128)
    y_v = t["yT"].rearrange("(c p) t -> p c t", p=128)

    def stage_a(tt, hh, hn, r_hn, res):
        q, r_q = qrot.next(); kt, r_kt = ktrot.next(); km, r_km = kmrot.next(); v, r_v = vrot.next(); sg, r_sg = sgrot.next()
        base = hh * 1536
        for dc in range(2):
            ps, r_ps = projrot.next()
            for c in range(8):
                ph.op("pe", lambda e, ps=ps, c=c, dc=dc: e.matmul(ps[:], w[:, c, base + dc * 128: base + (dc + 1) * 128], hn[:, c, :],
                                                                 start=(c == 0), stop=(c == 7)), reads=[r_w, r_hn], writes=[r_ps])
            ph.op("dve", lambda e, ps=ps, dc=dc: e.tensor_tensor(out=q[:, dc, :], in0=ps[:], in1=dq[:, hh, :], op=ALU.mult),
                  reads=[r_ps, r_c], writes=[r_q])
            yield
        for dc in range(2):
            ps, r_ps = projrot.next()
            for c in range(8):
                ph.op("pe", lambda e, ps=ps, c=c, dc=dc: e.matmul(ps[:], w[:, c, base + 256 + dc * 128: base + 256 + (dc + 1) * 128], hn[:, c, :],
                                                                 start=(c == 0), stop=(c == 7)), reads=[r_w, r_hn], writes=[r_ps])
            ph.op("dve", lambda e, ps=ps, dc=dc: e.tensor_tensor(out=kt[:, dc, :], in0=ps[:], in1=dkt[:, hh, :], op=ALU.mult),
                  reads=[r_ps, r_c], writes=[r_kt])
            yield
        for ci in range(4):
            ps, r_ps = projrot.next()
            for c in range(8):
                ph.op("pe", lambda e, ps=ps, c=c, ci=ci: e.matmul(ps[:, :256], hn[:, c, ci * 128:(ci + 1) * 128], w[:, c, base + 256: base + 512],
                                                                 start=(c == 0), stop=(c == 7)), reads=[r_w, r_hn], writes=[r_ps])
            ph.op("act", lambda e, ps=ps, ci=ci: e.activation(out=km[:, ci, :], in_=ps[:, :256], func=AF.Identity, scale=dkm[:, hh:hh + 1]),
                  reads=[r_ps, r_c], writes=[r_km])
            yield
            ps, r_ps = projrot.next()
            for c in range(8):
                ph.op("pe", lambda e, ps=ps, c=c, ci=ci: e.matmul(ps[:], hn[:, c, ci * 128:(ci + 1) * 128], w[:, c, base + 512: base + 1024],
                                                                 start=(c == 0), stop=(c == 7)), reads=[r_w, r_hn], writes=[r_ps])
            ph.op("act", lambda e, ps=ps, ci=ci: e.activation(out=v[:, ci, :], in_=ps[:], func=AF.Copy), reads=[r_ps], writes=[r_v])
            yield
            ps, r_ps = projrot.next()
            for c in range(8):
                ph.op("pe", lambda e, ps=ps, c=c, ci=ci: e.matmul(ps[:], hn[:, c, ci * 128:(ci + 1) * 128], w[:, c, base + 1024: base + 1536],
                                                                 start=(c == 0), stop=(c == 7)), reads=[r_w, r_hn], writes=[r_ps])
            ph.op("act", lambda e, ps=ps, ci=ci: e.activation(out=sg[:, ci, :], in_=ps[:], func=AF.Silu), reads=[r_ps], writes=[r_sg])
            yield
        res["bufs"] = (q, r_q, kt, r_kt, km, r_km, v, r_v, sg, r_sg)

    def stage_b(tt, hh, bufs, gen):
        q, r_q, kt, r_kt, km, r_km, v, r_v, sg, r_sg = bufs

        def pull(n):
            if gen is not None:
                for _ in range(n):
                    next(gen, None)
        ygt, r_ygt = ygtrot.next()
        for ci in range(4):
            first = (tt == 0 and ci == 0)
            csl = slice(ci * 128, (ci + 1) * 128)
            ap_, r_ap = attps[cnt["att"] % 4]; cnt["att"] += 1
            for dc in range(2):
                ph.op("pe", lambda e, ap_=ap_, dc=dc, csl=csl: e.matmul(ap_, kt[:, dc, csl], q[:, dc, csl], start=(dc == 0), stop=(dc == 1)),
                      reads=[r_kt, r_q], writes=[r_ap])
            pull(1)
            at, r_at = attrot.next()
            ph.op("dve", lambda e, at=at, ap_=ap_: e.tensor_tensor(out=at[:], in0=ap_, in1=mask[:], op=ALU.mult),
                  reads=[r_ap, r_c], writes=[r_at])
            yp, r_yp = yrot.next()
            ph.op("pe", lambda e, yp=yp, at=at, ci=ci, first=first: e.matmul(yp[:], at[:], v[:, ci, :], start=True, stop=first),
                  reads=[r_at, r_v], writes=[r_yp])
            if not first:
                for dc in range(2):
                    ph.op("pe", lambda e, yp=yp, dc=dc, csl=csl: e.matmul(yp[:], q[:, dc, csl], Rb[hh][:, dc, :], start=False, stop=(dc == 1)),
                          reads=[r_q, r_Rb[hh]], writes=[r_yp])
            for dc in range(2):
                rp, r_rp = rprot.next()
                ph.op("pe", lambda e, rp=rp, dc=dc, ci=ci: e.matmul(rp[:], km[:, ci, dc * 128:(dc + 1) * 128], v[:, ci, :], start=True, stop=True),
                      reads=[r_km, r_v], writes=[r_rp])
                if first:
                    ph.op("dve", lambda e, rp=rp, dc=dc: e.tensor_copy(out=R32[hh][:, dc, :], in_=rp[:]), reads=[r_rp], writes=[r_R32[hh]])
                else:
                    ph.op("dve", lambda e, rp=rp, dc=dc: e.scalar_tensor_tensor(
                        out=R32[hh][:, dc, :], in0=R32[hh][:, dc, :], scalar=gC[hh], in1=rp[:], op0=ALU.mult, op1=ALU.add),
                        reads=[r_rp, r_R32[hh]], writes=[r_R32[hh]])
            ph.op("act", lambda e: e.activation(out=Rb[hh][:], in_=R32[hh][:], func=AF.Copy), reads=[r_R32[hh]], writes=[r_Rb[hh]])
            pull(1)
            ss, r_ss = ssrot.next()
            ph.op("act", lambda e, yp=yp, ss=ss: e.activation(out=junk[:], in_=yp[:], func=AF.Square, accum_out=ss[:]),
                  reads=[r_yp], writes=[r_junk, r_ss])
            ph.op("act", lambda e, ss=ss: e.activation(out=ss[:], in_=ss[:], func=AF.Ln, scale=1.0 / 512, bias=epsc[:, 0:1]),
                  reads=[r_ss, r_c], writes=[r_ss])
            ph.op("act", lambda e, ss=ss: e.activation(out=ss[:], in_=ss[:], func=AF.Exp, scale=-0.5), reads=[r_ss], writes=[r_ss])
            yg, r_yg = ygrot.next()
            ph.op("dve", lambda e, yg=yg, yp=yp, ss=ss, ci=ci: e.scalar_tensor_tensor(
                out=yg[:], in0=yp[:], scalar=ss[:, 0:1], in1=sg[:, ci, :], op0=ALU.mult, op1=ALU.mult),
                reads=[r_yp, r_ss, r_sg], writes=[r_yg])
            pull(2)
            tp, r_tp = trps[cnt["tr"] % 2]; cnt["tr"] += 1
            for ec in range(4):
                ph.op("pe", lambda e, tp=tp, yg=yg, ec=ec: e.transpose(tp[:, ec * 128:(ec + 1) * 128], yg[:, ec * 128:(ec + 1) * 128], ident[:]),
                      reads=[r_yg, r_c], writes=[r_tp])
            ph.op("act", lambda e, tp=tp, ygt=ygt, csl=csl: e.activation(
                out=ygt[:, :, csl], in_=tp.rearrange("p (c n) -> p c n", c=4), func=AF.Copy), reads=[r_tp], writes=[r_ygt])
        ph.dma("sp", y_v[:, hh * 4:(hh + 1) * 4, tt * 512:(tt + 1) * 512], ygt[:], reads=[r_ygt])

    pend = None
    for tt in range(NT):
        hn, r_hn = hnrot.next()
        ph.dma("sp", hn[:], hn_v[:, :, tt * 512:(tt + 1) * 512], writes=[r_hn])
        for hh in range(2):
            res = {}
            gen = stage_a(tt, hh, hn, r_hn, res)
            if pend is not None:
                stage_b(*pend, gen)
            for _ in gen:
                pass
            pend = (tt, hh, res["bufs"])
    stage_b(*pend, None)
    ph.end()


DIL_GROUPS = ((128, 1), (512, 4), (2048, 16))
NEGB = -1e30


def alibi_slopes8():
    ratio = 2.0 ** (-8.0 / 8)
    return [ratio ** (h + 1) for h in range(8)]


def dil_consts():
    import ml_dtypes
    sl = alibi_slopes8()
    bias = np.zeros((128, 24, 256), np.float32)
    kj = np.arange(128)[:, None]
    qi = np.arange(128)[None, :]
    for g, (win, dil) in enumerate(DIL_GROUPS):
        for h in range(8):
            for part, off in ((0, 0), (1, 128)):
                rel = off + qi - kj
                ok = (rel >= 0) & (rel <= win // dil)
                bias[:, g * 8 + h, part * 128:(part + 1) * 128] = np.where(ok, -(sl[h] * dil) * rel, NEGB)
    return dict(bias=bias.astype(ml_dtypes.bfloat16), ident=np.eye(128, dtype=np.float32).astype(ml_dtypes.bfloat16))


def phase_dilated(ctx, t, ntok):
    ph = Phase(ctx)
    NT = ntok // 512
    bias = ph.sbuf([128, 24, 256], BF16); r_c = Res()
    ident = ph.sbuf([128, 128], BF16)
    onesb = ph.sbuf([128, 128], BF16)
    ones32 = ph.sbuf([128, 128], F32)
    mh = ph.sbuf([128, 512], F32)
    gqk = ph.sbuf([128, 2], F32)
    ph.dma("sp", bias[:], t["bias"], writes=[r_c])
    ph.dma("sp", ident[:], t["ident"], writes=[r_c])
    ph.dma("sp", gqk[:, 0:1], t["gq"], writes=[r_c])
    ph.dma("sp", gqk[:, 1:2], t["gk"], writes=[r_c])
    ph.op("pool", lambda e: e.memset(onesb[:], 1.0), writes=[r_c])
    ph.op("pool", lambda e: e.memset(ones32[:], 1.0), writes=[r_c])
    epsc = ph.sbuf([128, 1], F32)
    ph.op("pool", lambda e: e.memset(epsc[:], EPS), writes=[r_c])
    ph.op("dve", lambda e: e.tensor_scalar(out=gqk[:, 0:1], in0=gqk[:, 0:1], scalar1=128.0 ** -0.5, scalar2=None, op0=ALU.mult),
          reads=[r_c], writes=[r_c])

    whrot = Rot(ph, 2, [128, 8, 7, 128], BF16)
    hnrot = Rot(ph, 2, [128, 8, 512], BF16)
    qd = ph.sbuf([128, ntok], BF16); r_qd = Res()
    kd = ph.sbuf([128, ntok], BF16); r_kd = Res()
    vd = ph.sbuf([128, ntok], BF16); r_vd = Res()
    uz = ph.sbuf([128, 2, ntok], F32); r_uz = Res()
    ob, r_ob = qd, r_qd
    sqrot = Rot(ph, 2, [128, 512], F32)
    tmprot = Rot(ph, 4, [128, 512], F32)
    ptrot = Rot(ph, 4, [128, 256], BF16)
    vtrot = Rot(ph, 4, [128, 128], BF16)
    projrot = Rot(ph, 2, [128, 512], F32, psum=True)
    nrmrot = Rot(ph, 1, [128, 512], F32, psum=True)
    scrot = Rot(ph, 2, [128, 512], F32, psum=True)
    trp = ph.psum([128, 1024], BF16)
    r_trp = Res()
    trps = [(trp[:, i * 128:(i + 1) * 128], r_trp) for i in range(4)]
    uzrot = Rot(ph, 2, [128, 512], F32, psum=True)
    cnt = {"tr": 0}
    projrot.bufs = projrot.bufs + scrot.bufs

    hn_v = t["hnT"].rearrange("(c p) t -> p c t", p=128)
    wv = t["w"].rearrange("(c p) n -> p c n", p=128)
    y_v = t["yT"].rearrange("(c p) t -> p c t", p=128)

    def dview(buf, dil, l0, nl):
        if dil == 1:
            return buf[:, l0:l0 + nl]
        return buf[:].rearrange("p (r l) -> p r l", r=dil)[:, :, l0:l0 + nl].rearrange("p r l -> p l r")

    def pview(ps, dil):
        if dil == 1:
            return ps[:]
        return ps[:].rearrange("p (l r) -> p l r", r=dil)

    for h in range(8):
        wh, r_wh = whrot.next()
        for i in range(7):
            col = (i * 1024 + h * 128) if i < 6 else (6144 + h * 128)
            ph.dma("pool", wh[:, :, i, :], wv[:, :, col:col + 128], writes=[r_wh])
        ph.op("pool", lambda e: e.memset(uz[:], 0.0), writes=[r_uz])
        for g, (win, dil) in enumerate(DIL_GROUPS):
            L = ntok // dil
            nblk = L // 128
            for tt in range(NT):
                hn, r_hn = hnrot.next()
                ph.dma("sp", hn[:], hn_v[:, :, tt * 512:(tt + 1) * 512], writes=[r_hn])
                l0, nl = tt * 512 // dil, 512 // dil
                for which, dst, r_dst in ((0, qd, r_qd), (1, kd, r_kd)):
                    ps, r_ps = projrot.next()
                    wi = which * 3 + g
                    for c in range(8):
                        ph.op("pe", lambda e, ps=ps, c=c, wi=wi, hn=hn, wh=wh: e.matmul(ps[:], wh[:, c, wi, :], hn[:, c, :], start=(c == 0), stop=(c == 7)),
                              reads=[r_wh, r_hn], writes=[r_ps])
                    sq, r_sq = sqrot.next()
                    ph.op("act", lambda e, sq=sq, ps=ps: e.activation(out=sq[:], in_=ps[:], func=AF.Square), reads=[r_ps], writes=[r_sq])
                    pn, r_pn = nrmrot.next()
                    ph.op("pe", lambda e, pn=pn, sq=sq: e.matmul(pn[:], ones32[:], sq[:], start=True, stop=True), reads=[r_sq, r_c], writes=[r_pn])
                    tmp, r_tmp = tmprot.next()
                    ph.op("act", lambda e, tmp=tmp, pn=pn: e.activation(out=tmp[:], in_=pn[:], func=AF.Ln, scale=1.0 / 128, bias=epsc[:, 0:1]),
                          reads=[r_pn, r_c], writes=[r_tmp])
                    rstd, r_rstd = tmprot.next()
                    ph.op("act", lambda e, tmp=tmp, rstd=rstd: e.activation(out=rstd[:], in_=tmp[:], func=AF.Exp, scale=-0.5),
                          reads=[r_tmp], writes=[r_rstd])
                    ph.op("dve", lambda e, ps=ps, rstd=rstd, dst=dst, which=which, dil=dil, l0=l0, nl=nl: e.scalar_tensor_tensor(
                        out=dview(dst, dil, l0, nl), in0=pview(ps, dil), scalar=gqk[:, which:which + 1], in1=pview(rstd, dil),
                        op0=ALU.mult, op1=ALU.mult), reads=[r_ps, r_rstd, r_c], writes=[r_dst])
                ps, r_ps = projrot.next()
                for c in range(8):
                    ph.op("pe", lambda e, ps=ps, c=c, hn=hn, wh=wh: e.matmul(ps[:], wh[:, c, 6, :], hn[:, c, :], start=(c == 0), stop=(c == 7)),
                          reads=[r_wh, r_hn], writes=[r_ps])
                ph.op("act", lambda e, ps=ps, dil=dil, l0=l0, nl=nl: e.activation(out=dview(vd, dil, l0, nl), in_=pview(ps, dil), func=AF.Copy),
                      reads=[r_ps], writes=[r_vd])
            uzv = uz[:].rearrange("p a (l r) -> p a r l", r=dil) if dil > 1 else None
            for r in range(dil):
                base = r * L
                cur = None
                pend = None
                for step in range(nblk + 1):
                    newblk = None
                    if step < nblk:
                        j = step
                        nq = 256 if j < nblk - 1 else 128
                        ksl = slice(base + j * 128, base + (j + 1) * 128)
                        sc, r_sc = scrot.next()
                        ph.op("pe", lambda e, sc=sc, ksl=ksl, nq=nq, base=base, j=j: e.matmul(
                            sc[:, :nq], kd[:, ksl], qd[:, base + j * 128: base + j * 128 + nq], start=True, stop=False),
                            reads=[r_kd, r_qd], writes=[r_sc])
                        ph.op("pe", lambda e, sc=sc, nq=nq, g=g, h=h: e.matmul(sc[:, :nq], ident[:], bias[:, g * 8 + h, :nq], start=False, stop=True),
                              reads=[r_c], writes=[r_sc])
                        pt, r_pt = ptrot.next()
                        ph.op("act", lambda e, pt=pt, sc=sc, nq=nq: e.activation(out=pt[:, :nq], in_=sc[:, :nq], func=AF.Exp), reads=[r_sc], writes=[r_pt])
                        tp, r_tp = trps[cnt["tr"] % 4]; cnt["tr"] += 1
                        ph.op("pe", lambda e, tp=tp, ksl=ksl: e.transpose(tp, vd[:, ksl], ident[:]), reads=[r_vd, r_c], writes=[r_tp])
                        vt, r_vt = vtrot.next()
                        ph.op("dve", lambda e, vt=vt, tp=tp: e.tensor_copy(out=vt[:], in_=tp), reads=[r_tp], writes=[r_vt])
                        newblk = (j, pt, r_pt, vt, r_vt)
                    if pend is not None:
                        j, pt, r_pt, vt, r_vt = pend
                        if cur is None:
                            cur = uzrot.next()
                        cu, r_cu = cur
                        ph.op("pe", lambda e, cu=cu, vt=vt, pt=pt, j=j: e.matmul(cu[:, 0:128], vt[:], pt[:, 0:128], start=(j == 0), stop=True, skip_group_check=True),
                              reads=[r_vt, r_pt], writes=[r_cu])
                        ph.op("pe", lambda e, cu=cu, pt=pt, j=j: e.matmul(cu[:, 128:256], onesb[:], pt[:, 0:128], start=False, stop=True, skip_group_check=True),
                              reads=[r_pt, r_c], writes=[r_cu])
                        nxt = None
                        if j < nblk - 1:
                            nxt = uzrot.next()
                            nu, r_nu = nxt
                            ph.op("pe", lambda e, nu=nu, vt=vt, pt=pt: e.matmul(nu[:, 0:128], vt[:], pt[:, 128:256], start=True, stop=False, skip_group_check=True),
                                  reads=[r_vt, r_pt], writes=[r_nu])
                            ph.op("pe", lambda e, nu=nu, pt=pt: e.matmul(nu[:, 128:256], onesb[:], pt[:, 128:256], start=False, stop=False, skip_group_check=True),
                                  reads=[r_pt, r_c], writes=[r_nu])
                        if dil == 1:
                            dstv = uz[:, :, j * 128:(j + 1) * 128]
                        else:
                            dstv = uzv[:, :, r, j * 128:(j + 1) * 128]
                        ph.op("dve", lambda e, dstv=dstv, cu=cu: e.tensor_tensor(
                            out=dstv, in0=dstv, in1=cu[:, 0:256].rearrange("p (a n) -> p a n", a=2), op=ALU.add),
                            reads=[r_cu, r_uz], writes=[r_uz])
                        cur = nxt
                    pend = newblk
        for tt in range(NT):
            tsl = slice(tt * 512, (tt + 1) * 512)
            rz, r_rz = tmprot.next()
            ph.op("dve", lambda e, rz=rz, tsl=tsl: e.reciprocal(out=rz[:], in_=uz[:, 1, tsl]), reads=[r_uz], writes=[r_rz])
            ph.op("dve", lambda e, rz=rz, tsl=tsl: e.tensor_tensor(out=ob[:, tsl], in0=uz[:, 0, tsl], in1=rz[:], op=ALU.mult),
                  reads=[r_uz, r_rz], writes=[r_ob])
        ph.dma("sp", y_v[:, h, :], ob[:], reads=[r_ob])
    ph.end()


def diff_lambda_init(layer_idx):
    return 0.8 - 0.6 * math.exp(-0.3 * layer_idx)


def diff_consts(ntok):
    import ml_dtypes
    sl = alibi_slopes8()
    col = np.arange(512)
    qaug = np.zeros((3, 8, 512), np.float32)
    kaug = np.zeros((3, 8, ntok), np.float32)
    kj = np.arange(ntok) % 128
    for i in range(8):
        qaug[0, i] = -sl[i] * 128.0 * (col // 128)
        qaug[1, i] = -sl[i] * (col % 128)
        qaug[2, i] = 1.0
        kaug[0, i] = 1.0
        kaug[1, i] = 1.0
        kaug[2, i] = sl[i] * kj
    maskb = np.zeros((128, 4, 512), np.float32)
    kk = np.arange(128)[:, None]
    for d in range(4):
        maskb[:, d, :] = np.where(128 * d + kk <= col[None, :], 0.0, NEGB)
    bf = ml_dtypes.bfloat16
    return dict(qaug=qaug.astype(bf), kaug=kaug.astype(bf), maskb=maskb.astype(bf), ident=np.eye(128, dtype=np.float32).astype(bf))


def phase_diff(ctx, t, ntok, layer_idx):
    ph = Phase(ctx)
    NT = ntok // 512
    lam_init = diff_lambda_init(layer_idx)
    sl = alibi_slopes8()
    r_c = Res()
    maskb = ph.sbuf([128, 4, 512], BF16)
    ident = ph.sbuf([128, 128], BF16)
    ones32 = ph.sbuf([128, 128], F32)
    mh = ph.sbuf([128, 512], F32)
    gqk = ph.sbuf([64, 2], F32)
    lamv = ph.sbuf([64, 4], F32)
    gsub = ph.sbuf([128, 1], F32)
    lamw = ph.sbuf([128, 4], F32)
    ph.dma("sp", maskb[:], t["maskb"], writes=[r_c])
    ph.dma("sp", ident[:], t["ident"], writes=[r_c])
    ph.dma("sp", gqk[:, 0:1], t["gq"], writes=[r_c])
    ph.dma("sp", gqk[:, 1:2], t["gk"], writes=[r_c])
    ph.dma("sp", lamv[:], t["lam"], writes=[r_c])
    ph.dma("sp", gsub[:], t["gsub"], writes=[r_c])
    ph.op("pool", lambda e: e.memset(ones32[:], 1.0), writes=[r_c])
    epsc = ph.sbuf([128, 1], F32)
    ph.op("pool", lambda e: e.memset(epsc[:], EPS), writes=[r_c])
    ph.op("dve", lambda e: e.tensor_scalar(out=gqk[:, 0:1], in0=gqk[:, 0:1], scalar1=64.0 ** -0.5, scalar2=None, op0=ALU.mult),
          reads=[r_c], writes=[r_c])
    ph.op("dve", lambda e: e.tensor_scalar(out=gsub[:], in0=gsub[:], scalar1=1.0 - lam_init, scalar2=None, op0=ALU.mult),
          reads=[r_c], writes=[r_c])
    projrot = Rot(ph, 2, [128, 512], F32, psum=True)
    nrmrot = Rot(ph, 1, [128, 512], F32, psum=True)
    scrot = Rot(ph, 3, [128, 512], F32, psum=True)
    urot = Rot(ph, 1, [128, 512], F32, psum=True)
    trp = ph.psum([128, 1024], BF16)
    r_trp = Res()
    trps = [(trp[:, i * 512:(i + 1) * 512], r_trp) for i in range(2)]
    projrot.bufs = projrot.bufs + scrot.bufs
    ph.op("dve", lambda e: e.tensor_tensor(out=lamv[:, 0:1], in0=lamv[:, 0:1], in1=lamv[:, 1:2], op=ALU.mult), reads=[r_c], writes=[r_c])
    ph.op("dve", lambda e: e.tensor_tensor(out=lamv[:, 1:2], in0=lamv[:, 2:3], in1=lamv[:, 3:4], op=ALU.mult), reads=[r_c], writes=[r_c])
    pn, r_pn = nrmrot.next()
    ph.op("pe", lambda e: e.matmul(pn[:, 0:2], ones32[0:64, :], lamv[:, 0:2], start=True, stop=True), reads=[r_c], writes=[r_pn])
    ph.op("act", lambda e: e.activation(out=lamw[:, 0:2], in_=pn[:, 0:2], func=AF.Exp), reads=[r_pn], writes=[r_c])
    ph.op("dve", lambda e: e.tensor_tensor(out=lamw[:, 2:3], in0=lamw[:, 1:2], in1=lamw[:, 0:1], op=ALU.subtract), reads=[r_c], writes=[r_c])
    ph.op("dve", lambda e: e.tensor_scalar(out=lamw[:, 2:3], in0=lamw[:, 2:3], scalar1=-lam_init, scalar2=None, op0=ALU.add),
          reads=[r_c], writes=[r_c])

    whrot = Rot(ph, 2, [128, 8, 384], BF16)
    hnrot = Rot(ph, 2, [128, 8, 512], BF16)
    ka = [ph.sbuf([128, ntok], BF16) for _ in range(2)]
    r_ka = [[Res() for _ in range(NT)] for _ in range(2)]
    r_kaug = Res()
    vtm = ph.sbuf([128, ntok // 128, 128], BF16)
    r_vtm = [Res() for _ in range(NT)]
    qa = [[ph.sbuf([128, 512], BF16) for _ in range(2)] for _ in range(2)]
    r_qa = [[Res() for _ in range(2)] for _ in range(2)]
    vtrot = Rot(ph, 2, [128, 512], BF16)
    sqrot = Rot(ph, 2, [128, 512], F32)
    tmprot = Rot(ph, 4, [128, 512], F32)
    ptrot = Rot(ph, 6, [128, 512], BF16)
    zD = Rot(ph, 2, [128, 512], F32)
    zP = Rot(ph, 2, [128, 512], F32)
    o1rot = Rot(ph, 2, [128, 512], F32)
    outrot = Rot(ph, 2, [128, 512], BF16)
    cnt = {"tr": 0}

    hn_v = t["hnT"].rearrange("(c p) t -> p c t", p=128)
    wv = t["w"].rearrange("(c p) n -> p c n", p=128)
    y_v = t["yT"].rearrange("(c p) t -> p c t", p=128)

    def project(i, tt, wh, r_wh):
        hn, r_hn = hnrot.next()
        ph.dma("sp", hn[:], hn_v[:, :, tt * 512:(tt + 1) * 512], writes=[r_hn])
        tsl = slice(tt * 512, (tt + 1) * 512)
        for hd in range(2):
            for which in range(2):
                ps, r_ps = projrot.next()
                c0 = which * 128 + hd * 64
                for c in range(8):
                    ph.op("pe", lambda e, ps=ps, c=c, c0=c0, hn=hn, wh=wh: e.matmul(ps[0:64, :], wh[:, c, c0:c0 + 64], hn[:, c, :], start=(c == 0), stop=(c == 7)),
                          reads=[r_wh, r_hn], writes=[r_ps])
                sq, r_sq = sqrot.next()
                ph.op("act", lambda e, sq=sq, ps=ps: e.activation(out=sq[0:64, :], in_=ps[0:64, :], func=AF.Square), reads=[r_ps], writes=[r_sq])
                pn, r_pn = nrmrot.next()
                ph.op("pe", lambda e, pn=pn, sq=sq: e.matmul(pn[0:64, :], ones32[0:64, 0:64], sq[0:64, :], start=True, stop=True), reads=[r_sq, r_c], writes=[r_pn])
                tmp, r_tmp = tmprot.next()
                ph.op("act", lambda e, tmp=tmp, pn=pn: e.activation(out=tmp[0:64, :], in_=pn[0:64, :], func=AF.Ln, scale=1.0 / 64, bias=epsc[0:64, 0:1]),
                      reads=[r_pn, r_c], writes=[r_tmp])
                rstd, r_rstd = tmprot.next()
                ph.op("act", lambda e, tmp=tmp, rstd=rstd: e.activation(out=rstd[0:64, :], in_=tmp[0:64, :], func=AF.Exp, scale=-0.5),
                      reads=[r_tmp], writes=[r_rstd])
                if which == 0:
                    dst, r_dst = qa[hd][tt % 2][0:64, :], r_qa[hd][tt % 2]
                else:
                    dst, r_dst = ka[hd][0:64, tsl], r_ka[hd][tt]
                ph.op("dve", lambda e, ps=ps, rstd=rstd, dst=dst, which=which: e.scalar_tensor_tensor(
                    out=dst, in0=ps[0:64, :], scalar=gqk[:, which:which + 1], in1=rstd[0:64, :], op0=ALU.mult, op1=ALU.mult),
                    reads=[r_ps, r_rstd, r_c], writes=[r_dst])
        ps, r_ps = projrot.next()
        for c in range(8):
            ph.op("pe", lambda e, ps=ps, c=c, hn=hn, wh=wh: e.matmul(ps[:], wh[:, c, 256:384], hn[:, c, :], start=(c == 0), stop=(c == 7)),
                  reads=[r_wh, r_hn], writes=[r_ps])
        vt, r_vt = vtrot.next()
        ph.op("act", lambda e, vt=vt, ps=ps: e.activation(out=vt[:], in_=ps[:], func=AF.Copy), reads=[r_ps], writes=[r_vt])
        tp, r_tp = trps[cnt["tr"] % 2]; cnt["tr"] += 1
        for b in range(4):
            ph.op("pe", lambda e, tp=tp, vt=vt, b=b: e.transpose(tp[:, b * 128:(b + 1) * 128], vt[:, b * 128:(b + 1) * 128], ident[:]),
                  reads=[r_vt, r_c], writes=[r_tp])
        ph.op("act", lambda e, tp=tp, tt=tt: e.activation(out=vtm[:, tt * 4:(tt + 1) * 4, :], in_=tp.rearrange("p (b n) -> p b n", b=4), func=AF.Copy),
              reads=[r_tp], writes=[r_vtm[tt]])

    def attend(i, tt):
        nkb = 4 * tt + 4
        o1 = None
        for hd in range(2):
            q, r_q = qa[hd][tt % 2], r_qa[hd][tt % 2]
            U, r_U = urot.next()
            zd, r_zd = zD.next()
            zp, r_zp = zP.next()
            LAG = 2
            pts = {}
            kbs = [kb for kb in range(nkb) if sl[i] * (128.0 * (4 * tt - kb) - 127.0) <= 200.0]
            nk = len(kbs)
            for step in range(nk + LAG):
                if step < nk:
                    kb = kbs[step]
                    diag = kb >= 4 * tt
                    sc, r_sc = scrot.next()
                    ph.op("pe", lambda e, sc=sc, kb=kb, q=q, hd=hd, diag=diag: e.matmul(
                        sc[:], ka[hd][0:67, kb * 128:(kb + 1) * 128], q[0:67, :], start=True, stop=(not diag)),
                        reads=[r_ka[hd][kb // 4], r_kaug, r_q], writes=[r_sc])
                    if diag:
                        ph.op("pe", lambda e, sc=sc, d=kb - 4 * tt: e.matmul(sc[:], ident[:], maskb[:, d, :], start=False, stop=True),
                              reads=[r_c], writes=[r_sc])
                    pt, r_pt = ptrot.next()
                    pts[step] = (pt, r_pt, kb)
                    cbias = -sl[i] * 128.0 * (4 * tt - kb)
                    ph.op("act", lambda e, pt=pt, sc=sc, cbias=cbias: e.activation(out=pt[:], in_=sc[:], func=AF.Exp, bias=cbias),
                          reads=[r_sc], writes=[r_pt])
                    eng, z, r_z = ("dve", zd, r_zd) if step % 2 == 0 else ("pool", zp, r_zp)
                    if step < 2:
                        ph.op(eng, lambda e, z=z, pt=pt: e.tensor_copy(out=z[:], in_=pt[:]), reads=[r_pt], writes=[r_z])
                    else:
                        ph.op(eng, lambda e, z=z, pt=pt: e.tensor_tensor(out=z[:], in0=z[:], in1=pt[:], op=ALU.add), reads=[r_pt, r_z], writes=[r_z])
                sv = step - LAG
                if sv >= 0:
                    pt, r_pt, kv = pts.pop(sv)
                    ph.op("pe", lambda e, U=U, pt=pt, kv=kv, sv=sv, nk=nk: e.matmul(U[:], vtm[:, kv, :], pt[:], start=(sv == 0), stop=(sv == nk - 1)),
                          reads=[r_vtm[kv // 4], r_pt], writes=[r_U])
            pn, r_pn = nrmrot.next()
            ph.op("pe", lambda e, pn=pn, zd=zd: e.matmul(pn[:], ones32[:], zd[:], start=True, stop=False), reads=[r_zd, r_c], writes=[r_pn])
            ph.op("pe", lambda e, pn=pn, zp=zp: e.matmul(pn[:], ones32[:], zp[:], start=False, stop=True), reads=[r_zp, r_c], writes=[r_pn])
            rz, r_rz = tmprot.next()
            ph.op("dve", lambda e, rz=rz, pn=pn: e.reciprocal(out=rz[:], in_=pn[:]), reads=[r_pn], writes=[r_rz])
            if hd == 0:
                o1, r_o1 = o1rot.next()
                ph.op("dve", lambda e, o1=o1, U=U, rz=rz: e.tensor_tensor(out=o1[:], in0=U[:], in1=rz[:], op=ALU.mult), reads=[r_U, r_rz], writes=[r_o1])
            else:
                ph.op("dve", lambda e, U=U, rz=rz: e.tensor_tensor(out=rz[:], in0=U[:], in1=rz[:], op=ALU.mult), reads=[r_U, r_rz], writes=[r_rz])
                ph.op("dve", lambda e, o1=o1, rz=rz: e.scalar_tensor_tensor(out=o1[:], in0=rz[:], scalar=lamw[:, 2:3], in1=o1[:], op0=ALU.mult, op1=ALU.add),
                      reads=[r_rz, r_o1, r_c], writes=[r_o1])
        sq, r_sq = sqrot.next()
        ph.op("act", lambda e, sq=sq, o1=o1: e.activation(out=sq[:], in_=o1[:], func=AF.Square), reads=[r_o1], writes=[r_sq])
        pn, r_pn = nrmrot.next()
        ph.op("pe", lambda e, pn=pn, sq=sq: e.matmul(pn[:], ones32[:], sq[:], start=True, stop=True), reads=[r_sq, r_c], writes=[r_pn])
        tmp, r_tmp = tmprot.next()
        ph.op("act", lambda e, tmp=tmp, pn=pn: e.activation(out=tmp[:], in_=pn[:], func=AF.Ln, scale=1.0 / 128, bias=epsc[:, 0:1]),
              reads=[r_pn, r_c], writes=[r_tmp])
        rstd, r_rstd = tmprot.next()
        ph.op("act", lambda e, tmp=tmp, rstd=rstd: e.activation(out=rstd[:], in_=tmp[:], func=AF.Exp, scale=-0.5), reads=[r_tmp], writes=[r_rstd])
        ot, r_ot = outrot.next()
        ph.op("dve", lambda e, ot=ot, o1=o1, rstd=rstd: e.scalar_tensor_tensor(out=ot[:], in0=o1[:], scalar=gsub[:, 0:1], in1=rstd[:], op0=ALU.mult, op1=ALU.mult),
              reads=[r_o1, r_rstd, r_c], writes=[r_ot])
        ph.dma("sp", y_v[:, i, tt * 512:(tt + 1) * 512], ot[:], reads=[r_ot])

    for i in range(8):
        wh, r_wh = whrot.next()
        ph.dma("pool", wh[:, :, 0:128], wv[:, :, i * 128:(i + 1) * 128], writes=[r_wh])
        ph.dma("pool", wh[:, :, 128:256], wv[:, :, 1024 + i * 128:1024 + (i + 1) * 128], writes=[r_wh])
        ph.dma("pool", wh[:, :, 256:384], wv[:, :, 2048 + i * 128:2048 + (i + 1) * 128], writes=[r_wh])
        for hd in range(2):
            ph.dma("sp", ka[hd][64:67, :], t["kaug"][:, i, :], writes=[r_kaug] if hd == 1 else [r_kaug])
            for par in range(2):
                ph.dma("sp", qa[hd][par][64:67, :], t["qaug"][:, i, :], writes=[r_qa[hd][par]])
        pend = None
        for tt in range(NT):
            project(i, tt, wh, r_wh)
            if pend is not None:
                attend(i, pend)
            pend = tt
        attend(i, pend)
    ph.end()


def build_program(ntok):
    nc = bass.Bass("TRN2", target_bir_lowering=False)

    def din(name, shape, dt=F32):
        return nc.dram_tensor(name, list(shape), dt, kind="ExternalInput").ap()

    def dint(name, shape, dt=F32):
        return nc.dram_tensor(name, list(shape), dt).ap()

    xT = din("xT", [D, ntok])
    pT = din("pT", [DEPTH, 256, ntok])
    vecs = din("vecs", [DEPTH + 1, 128, 32])
    a_w = din("a_w", [2, 2, D, 3072])
    a_wo = din("a_wo", [2, 2048, D])
    b_w = din("b_w", [D, 7168])
    b_wo = din("b_wo", [D, D])
    c_w = din("c_w", [D, 3072])
    c_wo = din("c_wo", [D, D])
    f_wi = din("f_wi", [DEPTH, D, 2 * HID])
    f_wo = din("f_wo", [DEPTH, HID, D])
    p_wp = din("p_wp", [DEPTH, 256, D])
    p_wg = din("p_wg", [DEPTH, D, D])
    rc = dict(dq=din("r_dq", [2, 128, 2, 512]), dkt=din("r_dkt", [2, 128, 2, 512]), dkm=din("r_dkm", [2, 128, 2]),
              mask=din("r_mask", [128, 128]))
    ident = din("ident", [128, 128], BF16)
    b_gq = din("b_gq", [128, 1]); b_gk = din("b_gk", [128, 1]); b_bias = din("b_bias", [128, 24, 256], BF16)
    c_gq = din("c_gq", [64, 1]); c_gk = din("c_gk", [64, 1]); c_lam = din("c_lam", [64, 4]); c_gsub = din("c_gsub", [128, 1])
    c_qaug = din("c_qaug", [3, 8, 512], BF16); c_kaug = din("c_kaug", [3, 8, ntok], BF16); c_maskb = din("c_maskb", [128, 4, 512], BF16)
    outT = nc.dram_tensor("outT", [D, ntok], F32, kind="ExternalOutput").ap()
    hA = dint("hA", [D, ntok]); hB = dint("hB", [D, ntok])
    hnT = dint("hnT", [D, ntok], BF16)
    yT = dint("yT", [2048, ntok], BF16)

    ctx = Ctx(nc)
    phase_token_local(ctx, dict(hT=xT, vecs=vecs[0], hnT_out=hnT), 0, ntok, True, False)
    h_in = xT
    for i in range(DEPTH):
        kind, j = i % 3, i // 3
        if kind == 0:
            for r in range(2):
                phase_retention(ctx, dict(hnT=hnT, w=a_w[j, r], dq=rc["dq"][r], dkt=rc["dkt"][r], dkm=rc["dkm"][r], mask=rc["mask"],
                                          ident=ident, yT=yT[r * 1024:(r + 1) * 1024, :]),
                                [ret_gamma(2 * r), ret_gamma(2 * r + 1)], ntok)
            kout, wout = 2048, a_wo[j]
        elif kind == 1:
            phase_dilated(ctx, dict(hnT=hnT, w=b_w, gq=b_gq, gk=b_gk, bias=b_bias, ident=ident, yT=yT[0:1024, :]), ntok)
            kout, wout = 1024, b_wo
        else:
            phase_diff(ctx, dict(hnT=hnT, w=c_w, gq=c_gq, gk=c_gk, lam=c_lam, gsub=c_gsub, qaug=c_qaug, kaug=c_kaug, maskb=c_maskb,
                                 ident=ident, yT=yT[0:1024, :]), ntok, i)
            kout, wout = 1024, c_wo
        last = (i == DEPTH - 1)
        h_out = outT if last else (hA if i % 2 == 0 else hB)
        phase_token_local(ctx, dict(hT=h_in, yT=yT[0:kout, :], pT=pT[i], wout=wout, win=f_wi[i], wo=f_wo[i], wg=p_wg[i], wp=p_wp[i],
                                    vecs=vecs[i + 1], hT_out=h_out, hnT_out=hnT), kout, ntok, False, last)
        h_in = h_out
    ctx.finish()
    return nc


def _colvec(v):
    return np.ascontiguousarray(np.asarray(v, np.float32).reshape(-1, 128).T)


def prepare_inputs(inp, nb, ntok):
    f32 = np.float32
    g = {k: np.asarray(v) for k, v in inp.items()}
    vecs = np.zeros((DEPTH + 1, 128, 32), f32)
    vecs[0, :, 24:32] = _colvec(g["mix_norm"][0])
    for i in range(DEPTH):
        vecs[i + 1, :, 0:8] = _colvec(g["ffn_norm"][i])
        vecs[i + 1, :, 8:16] = _colvec(g["ple_gate_norm"][i])
        vecs[i + 1, :, 16:24] = _colvec(g["ple_norm"][i])
        if i + 1 < DEPTH:
            vecs[i + 1, :, 24:32] = _colvec(g["mix_norm"][i + 1])
    a_w = np.zeros((2, 2, D, 3072), f32)
    for j in range(2):
        w = g["a_w_in"][j]
        for r in range(2):
            parts = []
            for h in (2 * r, 2 * r + 1):
                parts += [w[:, h * 256:(h + 1) * 256], w[:, 1024 + h * 256:1024 + (h + 1) * 256],
                          w[:, 2048 + h * 512:2048 + (h + 1) * 512], w[:, 4096 + h * 512:4096 + (h + 1) * 512]]
            a_w[j, r] = np.concatenate(parts, axis=1)
    rcs = [ret_consts(r) for r in range(2)]
    dc = dil_consts()
    fc = diff_consts(ntok)
    shared = dict(
        vecs=vecs, a_w=a_w, a_wo=g["a_w_out"].astype(f32), b_w=g["b_w_in"][0], b_wo=g["b_w_out"][0], c_w=g["c_w_in"][0], c_wo=g["c_w_out"][0],
        f_wi=g["ffn_w_in"], f_wo=g["ffn_w_out"], p_wp=g["ple_w_proj"], p_wg=g["ple_w_gate"],
        r_dq=np.stack([c["dq"] for c in rcs]), r_dkt=np.stack([c["dkt"] for c in rcs]), r_dkm=np.stack([c["dkm"] for c in rcs]),
        r_mask=rcs[0]["mask"], ident=rcs[0]["ident"],
        b_gq=g["b_q_norm"][0].reshape(128, 1), b_gk=g["b_k_norm"][0].reshape(128, 1), b_bias=dc["bias"],
        c_gq=g["c_q_norm"][0].reshape(64, 1), c_gk=g["c_k_norm"][0].reshape(64, 1),
        c_lam=np.ascontiguousarray(np.stack([g["c_lambda_q1"][0], g["c_lambda_k1"][0], g["c_lambda_q2"][0], g["c_lambda_k2"][0]], axis=1).astype(f32)),
        c_gsub=g["c_subln"][0].reshape(128, 1), c_qaug=fc["qaug"], c_kaug=fc["kaug"], c_maskb=fc["maskb"])
    shared = {k: np.ascontiguousarray(v) for k, v in shared.items()}
    in_maps = []
    for b in range(nb):
        m = dict(shared)
        m["xT"] = np.ascontiguousarray(g["x"][b, :ntok].T)
        m["pT"] = np.ascontiguousarray(np.transpose(g["p"][:, b, :ntok, :], (0, 2, 1)))
        in_maps.append(m)
    return in_maps


def run_model(inp, nb, ntok):
    nc = build_program(ntok)
    in_maps = prepare_inputs(inp, nb, ntok)
    res = run_bass_kernel_spmd(nc, in_maps, core_ids=list(range(nb)))
    out = np.stack([np.ascontiguousarray(res.results[b]["outT"].T) for b in range(nb)], axis=0)
    return out.astype(np.float32)


def kernel(**inputs):
    return run_model(inputs, B, S)
```

```python
import contextlib
import math
import numpy as np
import concourse.bass as bass
import concourse.mybir as mybir
from concourse.bass_utils import run_bass_kernel_spmd

F32 = mybir.dt.float32
BF16 = mybir.dt.bfloat16
ALU = mybir.AluOpType
AF = mybir.ActivationFunctionType

D = 1024
S = 8192
B = 4
DEPTH = 4
HID = 2816
EPS = 1e-6

ENGS = ("pe", "act", "dve", "pool", "sp")
NDMASEM = 8
SAME_ENGINE_SYNC = {"pe": False, "act": True, "dve": True, "pool": True, "sp": False}


class Res:
    __slots__ = ("w", "rs")

    def __init__(self):
        self.w = None
        self.rs = []


class Op:
    __slots__ = ("eng", "fn", "deps", "dma", "need_inc", "sem", "val", "dslot")

    def __init__(self, eng, fn, dma):
        self.eng = eng
        self.fn = fn
        self.dma = dma
        self.deps = []
        self.need_inc = False
        self.sem = None
        self.val = None
        self.dslot = 0


class Ctx:
    def __init__(self, nc):
        self.nc = nc
        self.stack = contextlib.ExitStack()
        st = self.stack
        self.esem = {e: st.enter_context(nc.semaphore(f"s_{e}")) for e in ENGS}
        self.dsem = {e: [st.enter_context(nc.semaphore(f"d_{e}{i}")) for i in range(NDMASEM)] for e in ENGS}
        self.ecount = {e: 0 for e in ENGS}
        self.dcount = {e: 0 for e in ENGS}
        self.seen = {e: {} for e in ENGS}
        self.barrier = []
        self.nuid = 0

    def all_signals(self):
        sig = []
        for e in ENGS:
            if self.ecount[e] > 0:
                sig.append((self.esem[e], self.ecount[e]))
            nd = self.dcount[e]
            for i in range(min(nd, NDMASEM)):
                n_i = (nd - 1 - i) // NDMASEM + 1
                sig.append((self.dsem[e][i], 16 * n_i))
        return sig

    def finish(self):
        nc = self.nc
        sig = self.all_signals()
        ctx = self
        with nc.Block() as block:
            @block.sync
            def _(eng):
                for (k, v) in sig:
                    if ctx.seen["sp"].get(id(k), 0) < v:
                        eng.wait_ge(k, v)
        self.stack.close()


class Phase:
    def __init__(self, ctx):
        self.ctx = ctx
        self.nc = ctx.nc
        self.q = {e: [] for e in ENGS}
        self.stack = contextlib.ExitStack()

    def sbuf(self, shape, dtype):
        self.ctx.nuid += 1
        return self.stack.enter_context(self.nc.sbuf_tensor(f"sb{self.ctx.nuid}", list(shape), dtype))

    def psum(self, shape, dtype=F32):
        self.ctx.nuid += 1
        return self.stack.enter_context(self.nc.psum_tensor(f"ps{self.ctx.nuid}", list(shape), dtype))

    def op(self, eng, fn, reads=(), writes=(), dma=False):
        o = Op(eng, fn, dma)
        deps = []
        for r in reads:
            if r.w is not None:
                deps.append(r.w)
        for w in writes:
            if w.w is not None:
                deps.append(w.w)
            deps.extend(w.rs)
        seen = set()
        for d in deps:
            if id(d) in seen or d is o:
                continue
            seen.add(id(d))
            if d.eng == eng and not d.dma and not SAME_ENGINE_SYNC[eng]:
                continue
            o.deps.append(d)
            d.need_inc = True
        for r in reads:
            r.rs.append(o)
        for w in writes:
            w.w = o
            w.rs = []
        self.q[eng].append(o)
        return o

    def dma(self, eng, out, in_, reads=(), writes=()):
        return self.op(eng, lambda e: e.dma_start(out=out, in_=in_), reads, writes, dma=True)

    def end(self):
        ctx = self.ctx
        nc = self.nc
        for e in ENGS:
            last = None
            for o in self.q[e]:
                if not o.dma:
                    last = o
            if last is not None:
                last.need_inc = True
        for e in ENGS:
            for o in self.q[e]:
                if o.dma:
                    nd = ctx.dcount[e]
                    o.sem = ctx.dsem[e][nd % NDMASEM]
                    o.val = 16 * (nd // NDMASEM + 1)
                    o.dslot = nd
                    ctx.dcount[e] = nd + 1
                elif o.need_inc:
                    ctx.ecount[e] += 1
                    o.sem = ctx.esem[e]
                    o.val = ctx.ecount[e]
        barrier = ctx.barrier
        ph = self

        def replay(e, eng):
            seen = ctx.seen[e]
            for (k, v) in barrier:
                if seen.get(id(k), 0) < v:
                    eng.wait_ge(k, v)
                    seen[id(k)] = v
            for o in ph.q[e]:
                need = {}
                for d in o.deps:
                    kk = id(d.sem)
                    if kk not in need or need[kk][1] < d.val:
                        need[kk] = (d.sem, d.val)
                if o.dma and o.dslot >= NDMASEM:
                    kk = id(o.sem)
                    v = o.val - 16
                    if kk not in need or need[kk][1] < v:
                        need[kk] = (o.sem, v)
                for kk, (k, v) in need.items():
                    if seen.get(kk, 0) < v:
                        eng.wait_ge(k, v)
                        seen[kk] = v
                ins = o.fn(eng)
                if o.dma:
                    ins.then_inc(o.sem, 16)
                elif o.need_inc:
                    ins.then_inc(o.sem, 1)

        with nc.Block() as block:
            @block.tensor
            def _(eng):
                replay("pe", eng)

            @block.scalar
            def _(eng):
                replay("act", eng)

            @block.vector
            def _(eng):
                replay("dve", eng)

            @block.gpsimd
            def _(eng):
                replay("pool", eng)

            @block.sync
            def _(eng):
                replay("sp", eng)
        ctx.barrier = ctx.all_signals()
        self.stack.close()


class Rot:
    def __init__(self, ph, n, shape, dtype, psum=False):
        self.bufs = [(ph.psum(shape, dtype) if psum else ph.sbuf(shape, dtype), Res()) for _ in range(n)]
        self.i = 0

    def next(self):
        b = self.bufs[self.i % len(self.bufs)]
        self.i += 1
        return b


def emit_rmsnorm_fm(ph, src, r_src, nchunk, T0, TN, gain, r_gain, dst, r_dst, env, dim, gain_col0=0):
    ones, r_ones, mh, r_mh = env["ones"], env["r_ones"], env["mh"], env["r_mh"]
    pst, r_pst = env["psrot"].next()
    for c in range(nchunk):
        sq, r_sq = env["sqrot"].next()
        ph.op("act", lambda e, c=c, sq=sq: e.activation(out=sq[:, :TN], in_=src[:, c, T0:T0 + TN], func=AF.Square),
              reads=[r_src], writes=[r_sq])
        ph.op("pe", lambda e, c=c, sq=sq, pst=pst: e.matmul(pst[:, :TN], ones[:], sq[:, :TN], start=(c == 0), stop=(c == nchunk - 1)),
              reads=[r_ones, r_sq], writes=[r_pst])
    tmp, r_tmp = env["tmprot"].next()
    ph.op("act", lambda e, tmp=tmp, pst=pst: e.activation(out=tmp[:, :TN], in_=pst[:, :TN], func=AF.Ln, scale=1.0 / dim, bias=env["epsc"][:, 0:1]),
          reads=[r_pst, r_mh], writes=[r_tmp])
    rstd, r_rstd = env["tmprot"].next()
    ph.op("act", lambda e, tmp=tmp, rstd=rstd: e.activation(out=rstd[:, :TN], in_=tmp[:, :TN], func=AF.Exp, scale=-0.5),
          reads=[r_tmp], writes=[r_rstd])
    for c in range(nchunk):
        ph.op("dve", lambda e, c=c, rstd=rstd: e.scalar_tensor_tensor(
            out=dst[:, c, T0:T0 + TN], in0=src[:, c, T0:T0 + TN], scalar=gain[:, gain_col0 + c:gain_col0 + c + 1], in1=rstd[:, :TN],
            op0=ALU.mult, op1=ALU.mult), reads=[r_src, r_gain, r_rstd], writes=[r_dst])


def make_env(ph):
    env = {}
    env["ones"] = ph.sbuf([128, 128], BF16)
    env["r_ones"] = Res()
    env["mh"] = ph.sbuf([128, 512], F32)
    env["r_mh"] = Res()
    ph.op("pool", lambda e: e.memset(env["ones"][:], 1.0), writes=[env["r_ones"]])
    ph.op("pool", lambda e: e.memset(env["mh"][:], -0.5), writes=[env["r_mh"]])
    env["epsc"] = ph.sbuf([128, 1], F32)
    ph.op("pool", lambda e: e.memset(env["epsc"][:], EPS), writes=[env["r_mh"]])
    env["sqrot"] = Rot(ph, 3, [128, 512], BF16)
    env["tmprot"] = Rot(ph, 4, [128, 512], F32)
    return env


TT = 1024
ST = 512


def phase_token_local(ctx, t, kout, ntok, first, last):
    ph = Phase(ctx)
    env = make_env(ph)
    psrot = Rot(ph, 8, [128, 512], F32, psum=True)
    env["psrot"] = psrot
    kc_out = kout // 128
    vec = ph.sbuf([128, 32], F32)
    r_vec = Res()
    ph.dma("sp", vec[:], t["vecs"], writes=[r_vec])
    h = ph.sbuf([128, 8, TT], F32)
    r_hs = [Res(), Res()]
    hn = ph.sbuf([128, 8, TT], BF16)
    r_hns = [Res(), Res()]
    if not first:
        yg = ph.sbuf([128, 22, TT], BF16)
        r_ygs = [Res(), Res()]
        pt = ph.sbuf([128, 2, TT], BF16)
        r_pt = Res()
        et = ph.sbuf([128, 8, TT], F32)
        r_ets = [Res(), Res()]
        wrot = Rot(ph, 4, [128, 4096], BF16)
        sarot = Rot(ph, 3, [128, 512], F32)

    hT_in = t["hT"].rearrange("(c p) t -> p c t", p=128)
    for tt in range(ntok // TT):
        tsl = slice(tt * TT, (tt + 1) * TT)
        for st in range(TT // ST):
            ph.dma("sp", h[:, :, st * ST:(st + 1) * ST], hT_in[:, :, tt * TT + st * ST: tt * TT + (st + 1) * ST], writes=[r_hs[st]])
        if first:
            for st in range(TT // ST):
                emit_rmsnorm_fm(ph, h, r_hs[st], 8, st * ST, ST, vec, r_vec, hn, r_hns[st], env, D, gain_col0=24)
                ph.dma("sp", t["hnT_out"].rearrange("(c p) t -> p c t", p=128)[:, :, tt * TT + st * ST: tt * TT + (st + 1) * ST],
                       hn[:, :, st * ST:(st + 1) * ST], reads=[r_hns[st]])
            continue
        for st in range(TT // ST):
            ph.dma("sp", yg[:, :kc_out, st * ST:(st + 1) * ST],
                   t["yT"].rearrange("(c p) t -> p c t", p=128)[:, :, tt * TT + st * ST: tt * TT + (st + 1) * ST], writes=[r_ygs[st]])
        ph.dma("pool", pt[:], t["pT"].rearrange("(c p) t -> p c t", p=128)[:, :, tsl], writes=[r_pt])
        ncol = 4096 // kc_out
        wv = t["wout"].rearrange("(c p) n -> p c n", p=128)
        for blk in range(D // ncol):
            wb, r_wb = wrot.next()
            wbv = wb[:, :kc_out * ncol].rearrange("p (c n) -> p c n", c=kc_out)
            ph.dma("pool", wbv, wv[:, :, blk * ncol:(blk + 1) * ncol], writes=[r_wb])
            for st in range(TT // ST):
                for fi in range(ncol // 128):
                    f = blk * (ncol // 128) + fi
                    ps, r_ps = psrot.next()
                    for k in range(kc_out):
                        ph.op("pe", lambda e, ps=ps, wbv=wbv, fi=fi, k=k, st=st: e.matmul(
                            ps[:], wbv[:, k, fi * 128:(fi + 1) * 128], yg[:, k, st * ST:(st + 1) * ST],
                            start=(k == 0), stop=(k == kc_out - 1)), reads=[r_wb, r_ygs[st]], writes=[r_ps])
                    ph.op("dve", lambda e, ps=ps, f=f, st=st: e.tensor_tensor(
                        out=h[:, f, st * ST:(st + 1) * ST], in0=h[:, f, st * ST:(st + 1) * ST], in1=ps[:], op=ALU.add),
                        reads=[r_ps, r_hs[st]], writes=[r_hs[st]])
        for st in range(TT // ST):
            emit_rmsnorm_fm(ph, h, r_hs[st], 8, st * ST, ST, vec, r_vec, hn, r_hns[st], env, D, gain_col0=0)
        wv = t["win"].rearrange("(c p) n -> p c n", p=128)
        for blk in range(HID // 256):
            wb, r_wb = wrot.next()
            wbv = wb[:].rearrange("p (c n) -> p c n", c=8)
            ph.dma("pool", wbv[:, :, 0:256], wv[:, :, blk * 256:(blk + 1) * 256], writes=[r_wb])
            ph.dma("pool", wbv[:, :, 256:512], wv[:, :, HID + blk * 256:HID + (blk + 1) * 256], writes=[r_wb])
            for st in range(TT // ST):
                for jj in range(2):
                    j = blk * 2 + jj
                    pa, r_pa = psrot.next()
                    pb, r_pb = psrot.next()
                    for c in range(8):
                        ph.op("pe", lambda e, pa=pa, wbv=wbv, jj=jj, c=c, st=st: e.matmul(
                            pa[:], wbv[:, c, jj * 128:(jj + 1) * 128], hn[:, c, st * ST:(st + 1) * ST],
                            start=(c == 0), stop=(c == 7)), reads=[r_wb, r_hns[st]], writes=[r_pa])
                    for c in range(8):
                        ph.op("pe", lambda e, pb=pb, wbv=wbv, jj=jj, c=c, st=st: e.matmul(
                            pb[:], wbv[:, c, 256 + jj * 128:256 + (jj + 1) * 128], hn[:, c, st * ST:(st + 1) * ST],
                            start=(c == 0), stop=(c == 7)), reads=[r_wb, r_hns[st]], writes=[r_pb])
                    sa, r_sa = sarot.next()
                    ph.op("act", lambda e, sa=sa, pa=pa: e.activation(out=sa[:], in_=pa[:], func=AF.Silu), reads=[r_pa], writes=[r_sa])
                    ph.op("dve", lambda e, sa=sa, pb=pb, j=j, st=st: e.tensor_tensor(
                        out=yg[:, j, st * ST:(st + 1) * ST], in0=sa[:], in1=pb[:], op=ALU.mult),
                        reads=[r_sa, r_pb], writes=[r_ygs[st]])
        wv = t["wo"].rearrange("(c p) n -> p c n", p=128)
        for f in range(8):
            wb, r_wb = wrot.next()
            wbv = wb[:, :22 * 128].rearrange("p (c n) -> p c n", c=22)
            ph.dma("pool", wbv, wv[:, :, f * 128:(f + 1) * 128], writes=[r_wb])
            for st in range(TT // ST):
                ps, r_ps = psrot.next()
                for k in range(22):
                    ph.op("pe", lambda e, ps=ps, wbv=wbv, k=k, st=st: e.matmul(
                        ps[:], wbv[:, k, :], yg[:, k, st * ST:(st + 1) * ST], start=(k == 0), stop=(k == 21)),
                        reads=[r_wb, r_ygs[st]], writes=[r_ps])
                ph.op("dve", lambda e, ps=ps, f=f, st=st: e.tensor_tensor(
                    out=h[:, f, st * ST:(st + 1) * ST], in0=h[:, f, st * ST:(st + 1) * ST], in1=ps[:], op=ALU.add),
                    reads=[r_ps, r_hs[st]], writes=[r_hs[st]])
        for st in range(TT // ST):
            emit_rmsnorm_fm(ph, h, r_hs[st], 8, st * ST, ST, vec, r_vec, hn, r_hns[st], env, D, gain_col0=8)
        wb, r_wb = wrot.next()
        wbv = wb[:, :2048].rearrange("p (c n) -> p c n", c=2)
        ph.dma("pool", wbv, t["wp"].rearrange("(c p) n -> p c n", p=128), writes=[r_wb])
        for st in range(TT // ST):
            for f in range(8):
                ps, r_ps = psrot.next()
                for k in range(2):
                    ph.op("pe", lambda e, ps=ps, wbv=wbv, k=k, f=f, st=st: e.matmul(
                        ps[:], wbv[:, k, f * 128:(f + 1) * 128], pt[:, k, st * ST:(st + 1) * ST], start=(k == 0), stop=(k == 1)),
                        reads=[r_wb, r_pt], writes=[r_ps])
                ph.op("act", lambda e, ps=ps, f=f, st=st: e.activation(out=et[:, f, st * ST:(st + 1) * ST], in_=ps[:], func=AF.Copy),
                      reads=[r_ps], writes=[r_ets[st]])
            emit_rmsnorm_fm(ph, et, r_ets[st], 8, st * ST, ST, vec, r_vec, et, r_ets[st], env, D, gain_col0=16)
        wv = t["wg"].rearrange("(c p) n -> p c n", p=128)
        for blk in range(2):
            wb, r_wb = wrot.next()
            wbv = wb[:].rearrange("p (c n) -> p c n", c=8)
            ph.dma("pool", wbv, wv[:, :, blk * 512:(blk + 1) * 512], writes=[r_wb])
            for st in range(TT // ST):
                for fi in range(4):
                    f = blk * 4 + fi
                    ps, r_ps = psrot.next()
                    for c in range(8):
                        ph.op("pe", lambda e, ps=ps, wbv=wbv, fi=fi, c=c, st=st: e.matmul(
                            ps[:], wbv[:, c, fi * 128:(fi + 1) * 128], hn[:, c, st * ST:(st + 1) * ST],
                            start=(c == 0), stop=(c == 7)), reads=[r_wb, r_hns[st]], writes=[r_ps])
                    sa, r_sa = sarot.next()
                    ph.op("act", lambda e, sa=sa, ps=ps: e.activation(out=sa[:], in_=ps[:], func=AF.Tanh, scale=0.5),
                          reads=[r_ps], writes=[r_sa])
                    ph.op("dve", lambda e, sa=sa: e.tensor_scalar(out=sa[:], in0=sa[:], scalar1=0.5, scalar2=0.5, op0=ALU.mult, op1=ALU.add),
                          reads=[r_sa], writes=[r_sa])
                    ph.op("dve", lambda e, sa=sa, f=f, st=st: e.tensor_tensor(
                        out=sa[:], in0=sa[:], in1=et[:, f, st * ST:(st + 1) * ST], op=ALU.mult), reads=[r_sa, r_ets[st]], writes=[r_sa])
                    ph.op("dve", lambda e, sa=sa, f=f, st=st: e.tensor_tensor(
                        out=h[:, f, st * ST:(st + 1) * ST], in0=h[:, f, st * ST:(st + 1) * ST], in1=sa[:], op=ALU.add),
                        reads=[r_sa, r_hs[st]], writes=[r_hs[st]])
        for st in range(TT // ST):
            gsl = slice(tt * TT + st * ST, tt * TT + (st + 1) * ST)
            ph.dma("sp", t["hT_out"].rearrange("(c p) t -> p c t", p=128)[:, :, gsl], h[:, :, st * ST:(st + 1) * ST], reads=[r_hs[st]])
            if not last:
                emit_rmsnorm_fm(ph, h, r_hs[st], 8, st * ST, ST, vec, r_vec, hn, r_hns[st], env, D, gain_col0=24)
                ph.dma("sp", t["hnT_out"].rearrange("(c p) t -> p c t", p=128)[:, :, gsl], hn[:, :, st * ST:(st + 1) * ST], reads=[r_hns[st]])
    ph.end()


def ret_gamma(h):
    return float(np.float32(1.0) - np.float32(2.0) ** np.float32(-5.0 - h))


def ret_consts(r):
    import ml_dtypes
    pos = np.arange(512) % 128
    dq = np.zeros((128, 2, 512), np.float32)
    dkt = np.zeros((128, 2, 512), np.float32)
    dkm = np.zeros((128, 2), np.float32)
    for hh in range(2):
        lg = np.log(np.float64(ret_gamma(2 * r + hh)))
        dq[:, hh, :] = np.exp((pos + 1.0) * lg)[None, :]
        dkt[:, hh, :] = (np.exp(-(pos + 1.0) * lg) * (256.0 ** -0.5))[None, :]
        dkm[:, hh] = np.exp((127.0 - np.arange(128)) * lg) * (256.0 ** -0.5)
    m = np.arange(128)
    mask = (m[:, None] <= m[None, :]).astype(np.float32)
    ident = np.eye(128, dtype=np.float32).astype(ml_dtypes.bfloat16)
    return dict(dq=dq, dkt=dkt, dkm=dkm, mask=mask, ident=ident)


def phase_retention(ctx, t, r_role_gammas, ntok):
    ph = Phase(ctx)
    NT = ntok // 512
    gC = [g ** 128 for g in r_role_gammas]
    w = ph.sbuf([128, 8, 3072], BF16)
    r_w = Res()
    wv = t["w"].rearrange("(c p) n -> p c n", p=128)
    for hh in range(2):
        for part in range(3):
            sl = slice(hh * 1536 + part * 512, hh * 1536 + (part + 1) * 512)
            ph.dma("pool", w[:, :, sl], wv[:, :, sl], writes=[r_w])
    dq = ph.sbuf([128, 2, 512], F32); r_c = Res()
    dkt = ph.sbuf([128, 2, 512], F32)
    dkm = ph.sbuf([128, 2], F32)
    mask = ph.sbuf([128, 128], F32)
    ident = ph.sbuf([128, 128], BF16)
    mh = ph.sbuf([128, 1], F32)
    ph.dma("sp", dq[:], t["dq"], writes=[r_c])
    ph.dma("sp", dkt[:], t["dkt"], writes=[r_c])
    ph.dma("sp", dkm[:], t["dkm"], writes=[r_c])
    ph.dma("sp", mask[:], t["mask"], writes=[r_c])
    ph.dma("sp", ident[:], t["ident"], writes=[r_c])
    epsc = ph.sbuf([128, 1], F32)
    ph.op("pool", lambda e: e.memset(epsc[:], EPS), writes=[r_c])

    hnrot = Rot(ph, 2, [128, 8, 512], BF16)
    qrot = Rot(ph, 2, [128, 2, 512], BF16)
    ktrot = Rot(ph, 2, [128, 2, 512], BF16)
    kmrot = Rot(ph, 2, [128, 4, 256], BF16)
    vrot = Rot(ph, 2, [128, 4, 512], BF16)
    sgrot = Rot(ph, 2, [128, 4, 512], BF16)
    R32 = [ph.sbuf([128, 2, 512], F32) for _ in range(2)]
    r_R32 = [Res(), Res()]
    Rb = [ph.sbuf([128, 2, 512], BF16) for _ in range(2)]
    r_Rb = [Res(), Res()]
    attrot = Rot(ph, 2, [128, 128], BF16)
    ygrot = Rot(ph, 2, [128, 512], BF16)
    ygtrot = Rot(ph, 2, [128, 4, 512], BF16)
    junk = ph.sbuf([128, 512], BF16); r_junk = Res()
    ssrot = Rot(ph, 4, [128, 1], F32)
    projrot = Rot(ph, 2, [128, 512], F32, psum=True)
    attp = ph.psum([128, 512], F32)
    r_attp = Res()
    attps = [(attp[:, i * 128:(i + 1) * 128], r_attp) for i in range(4)]
    yrot = Rot(ph, 2, [128, 512], F32, psum=True)
    rprot = Rot(ph, 2, [128, 512], F32, psum=True)
    trp = ph.psum([128, 1024], BF16)
    r_trp = Res()
    trps = [(trp[:, i * 512:(i + 1) * 512], r_trp) for i in range(2)]
    cnt = {"att": 0, "tr": 0}

    hn_v = t["hnT"].rearrange("(c p) t -> p c t", p=128)
    y_v = t["yT"].rearrange("(c p) t -> p c t", p=128)

    def stage_a(tt, hh, hn, r_hn, res):
        q, r_q = qrot.next(); kt, r_kt = ktrot.next(); km, r_km = kmrot.next(); v, r_v = vrot.next(); sg, r_sg = sgrot.next()
        base = hh * 1536
        for dc in range(2):
            ps, r_ps = projrot.next()
            for c in range(8):
                ph.op("pe", lambda e, ps=ps, c=c, dc=dc: e.matmul(ps[:], w[:, c, base + dc * 128: base + (dc + 1) * 128], hn[:, c, :],
                                                                 start=(c == 0), stop=(c == 7)), reads=[r_w, r_hn], writes=[r_ps])
            ph.op("dve", lambda e, ps=ps, dc=dc: e.tensor_tensor(out=q[:, dc, :], in0=ps[:], in1=dq[:, hh, :], op=ALU.mult),
                  reads=[r_ps, r_c], writes=[r_q])
            yield
        for dc in range(2):
            ps, r_ps = projrot.next()
            for c in range(8):
                ph.op("pe", lambda e, ps=ps, c=c, dc=dc: e.matmul(ps[:], w[:, c, base + 256 + dc * 128: base + 256 + (dc + 1) * 128], hn[:, c, :],
                                                                 start=(c == 0), stop=(c == 7)), reads=[r_w, r_hn], writes=[r_ps])
            ph.op("dve", lambda e, ps=ps, dc=dc: e.tensor_tensor(out=kt[:, dc, :], in0=ps[:], in1=dkt[:, hh, :], op=ALU.mult),
                  reads=[r_ps, r_c], writes=[r_kt])
            yield
        for ci in range(4):
            ps, r_ps = projrot.next()
            for c in range(8):
                ph.op("pe", lambda e, ps=ps, c=c, ci=ci: e.matmul(ps[:, :256], hn[:, c, ci * 128:(ci + 1) * 128], w[:, c, base + 256: base + 512],
                                                                 start=(c == 0), stop=(c == 7)), reads=[r_w, r_hn], writes=[r_ps])
            ph.op("act", lambda e, ps=ps, ci=ci: e.activation(out=km[:, ci, :], in_=ps[:, :256], func=AF.Identity, scale=dkm[:, hh:hh + 1]),
                  reads=[r_ps, r_c], writes=[r_km])
            yield
            ps, r_ps = projrot.next()
            for c in range(8):
                ph.op("pe", lambda e, ps=ps, c=c, ci=ci: e.matmul(ps[:], hn[:, c, ci * 128:(ci + 1) * 128], w[:, c, base + 512: base + 1024],
                                                                 start=(c == 0), stop=(c == 7)), reads=[r_w, r_hn], writes=[r_ps])
            ph.op("act", lambda e, ps=ps, ci=ci: e.activation(out=v[:, ci, :], in_=ps[:], func=AF.Copy), reads=[r_ps], writes=[r_v])
            yield
            ps, r_ps = projrot.next()
            for c in range(8):
                ph.op("pe", lambda e, ps=ps, c=c, ci=ci: e.matmul(ps[:], hn[:, c, ci * 128:(ci + 1) * 128], w[:, c, base + 1024: base + 1536],
                                                                 start=(c == 0), stop=(c == 7)), reads=[r_w, r_hn], writes=[r_ps])
            ph.op("act", lambda e, ps=ps, ci=ci: e.activation(out=sg[:, ci, :], in_=ps[:], func=AF.Silu), reads=[r_ps], writes=[r_sg])
            yield
        res["bufs"] = (q, r_q, kt, r_kt, km, r_km, v, r_v, sg, r_sg)

    def stage_b(tt, hh, bufs, gen):
        q, r_q, kt, r_kt, km, r_km, v, r_v, sg, r_sg = bufs

        def pull(n):
            if gen is not None:
                for _ in range(n):
                    next(gen, None)
        ygt, r_ygt = ygtrot.next()
        for ci in range(4):
            first = (tt == 0 and ci == 0)
            csl = slice(ci * 128, (ci + 1) * 128)
            ap_, r_ap = attps[cnt["att"] % 4]; cnt["att"] += 1
            for dc in range(2):
                ph.op("pe", lambda e, ap_=ap_, dc=dc, csl=csl: e.matmul(ap_, kt[:, dc, csl], q[:, dc, csl], start=(dc == 0), stop=(dc == 1)),
                      reads=[r_kt, r_q], writes=[r_ap])
            pull(1)
            at, r_at = attrot.next()
            ph.op("dve", lambda e, at=at, ap_=ap_: e.tensor_tensor(out=at[:], in0=ap_, in1=mask[:], op=ALU.mult),
                  reads=[r_ap, r_c], writes=[r_at])
            yp, r_yp = yrot.next()
            ph.op("pe", lambda e, yp=yp, at=at, ci=ci, first=first: e.matmul(yp[:], at[:], v[:, ci, :], start=True, stop=first),
                  reads=[r_at, r_v], writes=[r_yp])
            if not first:
                for dc in range(2):
                    ph.op("pe", lambda e, yp=yp, dc=dc, csl=csl: e.matmul(yp[:], q[:, dc, csl], Rb[hh][:, dc, :], start=False, stop=(dc == 1)),
                          reads=[r_q, r_Rb[hh]], writes=[r_yp])
            for dc in range(2):
                rp, r_rp = rprot.next()
                ph.op("pe", lambda e, rp=rp, dc=dc, ci=ci: e.matmul(rp[:], km[:, ci, dc * 128:(dc + 1) * 128], v[:, ci, :], start=True, stop=True),
                      reads=[r_km, r_v], writes=[r_rp])
                if first:
                    ph.op("dve", lambda e, rp=rp, dc=dc: e.tensor_copy(out=R32[hh][:, dc, :], in_=rp[:]), reads=[r_rp], writes=[r_R32[hh]])
                else:
                    ph.op("dve", lambda e, rp=rp, dc=dc: e.scalar_tensor_tensor(
                        out=R32[hh][:, dc, :], in0=R32[hh][:, dc, :], scalar=gC[hh], in1=rp[:], op0=ALU.mult, op1=ALU.add),
                        reads=[r_rp, r_R32[hh]], writes=[r_R32[hh]])
            ph.op("act", lambda e: e.activation(out=Rb[hh][:], in_=R32[hh][:], func=AF.Copy), reads=[r_R32[hh]], writes=[r_Rb[hh]])
            pull(1)
            ss, r_ss = ssrot.next()
            ph.op("act", lambda e, yp=yp, ss=ss: e.activation(out=junk[:], in_=yp[:], func=AF.Square, accum_out=ss[:]),
                  reads=[r_yp], writes=[r_junk, r_ss])
            ph.op("act", lambda e, ss=ss: e.activation(out=ss[:], in_=ss[:], func=AF.Ln, scale=1.0 / 512, bias=epsc[:, 0:1]),
                  reads=[r_ss, r_c], writes=[r_ss])
            ph.op("act", lambda e, ss=ss: e.activation(out=ss[:], in_=ss[:], func=AF.Exp, scale=-0.5), reads=[r_ss], writes=[r_ss])
            yg, r_yg = ygrot.next()
            ph.op("dve", lambda e, yg=yg, yp=yp, ss=ss, ci=ci: e.scalar_tensor_tensor(
                out=yg[:], in0=yp[:], scalar=ss[:, 0:1], in1=sg[:, ci, :], op0=ALU.mult, op1=ALU.mult),
                reads=[r_yp, r_ss, r_sg], writes=[r_yg])
            pull(2)
            tp, r_tp = trps[cnt["tr"] % 2]; cnt["tr"] += 1
            for ec in range(4):
                ph.op("pe", lambda e, tp=tp, yg=yg, ec=ec: e.transpose(tp[:, ec * 128:(ec + 1) * 128], yg[:, ec * 128:(ec + 1) * 128], ident[:]),
                      reads=[r_yg, r_c], writes=[r_tp])
            ph.op("act", lambda e, tp=tp, ygt=ygt, csl=csl: e.activation(
                out=ygt[:, :, csl], in_=tp.rearrange("p (c n) -> p c n", c=4), func=AF.Copy), reads=[r_tp], writes=[r_ygt])
        ph.dma("sp", y_v[:, hh * 4:(hh + 1) * 4, tt * 512:(tt + 1) * 512], ygt[:], reads=[r_ygt])

    pend = None
    for tt in range(NT):
        hn, r_hn = hnrot.next()
        ph.dma("sp", hn[:], hn_v[:, :, tt * 512:(tt + 1) * 512], writes=[r_hn])
        for hh in range(2):
            res = {}
            gen = stage_a(tt, hh, hn, r_hn, res)
            if pend is not None:
                stage_b(*pend, gen)
            for _ in gen:
                pass
            pend = (tt, hh, res["bufs"])
    stage_b(*pend, None)
    ph.end()


DIL_GROUPS = ((128, 1), (512, 4), (2048, 16))
NEGB = -1e30


def alibi_slopes8():
    ratio = 2.0 ** (-8.0 / 8)
    return [ratio ** (h + 1) for h in range(8)]


def dil_consts():
    import ml_dtypes
    sl = alibi_slopes8()
    bias = np.zeros((128, 24, 256), np.float32)
    kj = np.arange(128)[:, None]
    qi = np.arange(128)[None, :]
    for g, (win, dil) in enumerate(DIL_GROUPS):
        for h in range(8):
            for part, off in ((0, 0), (1, 128)):
                rel = off + qi - kj
                ok = (rel >= 0) & (rel <= win // dil)
                bias[:, g * 8 + h, part * 128:(part + 1) * 128] = np.where(ok, -(sl[h] * dil) * rel, NEGB)
    return dict(bias=bias.astype(ml_dtypes.bfloat16), ident=np.eye(128, dtype=np.float32).astype(ml_dtypes.bfloat16))


def phase_dilated(ctx, t, ntok):
    ph = Phase(ctx)
    NT = ntok // 512
    bias = ph.sbuf([128, 24, 256], BF16); r_c = Res()
    ident = ph.sbuf([128, 128], BF16)
    onesb = ph.sbuf([128, 128], BF16)
    ones32 = ph.sbuf([128, 128], F32)
    mh = ph.sbuf([128, 512], F32)
    gqk = ph.sbuf([128, 2], F32)
    ph.dma("sp", bias[:], t["bias"], writes=[r_c])
    ph.dma("sp", ident[:], t["ident"], writes=[r_c])
    ph.dma("sp", gqk[:, 0:1], t["gq"], writes=[r_c])
    ph.dma("sp", gqk[:, 1:2], t["gk"], writes=[r_c])
    ph.op("pool", lambda e: e.memset(onesb[:], 1.0), writes=[r_c])
    ph.op("pool", lambda e: e.memset(ones32[:], 1.0), writes=[r_c])
    epsc = ph.sbuf([128, 1], F32)
    ph.op("pool", lambda e: e.memset(epsc[:], EPS), writes=[r_c])
    ph.op("dve", lambda e: e.tensor_scalar(out=gqk[:, 0:1], in0=gqk[:, 0:1], scalar1=128.0 ** -0.5, scalar2=None, op0=ALU.mult),
          reads=[r_c], writes=[r_c])

    whrot = Rot(ph, 2, [128, 8, 7, 128], BF16)
    hnrot = Rot(ph, 2, [128, 8, 512], BF16)
    qd = ph.sbuf([128, ntok], BF16); r_qd = Res()
    kd = ph.sbuf([128, ntok], BF16); r_kd = Res()
    vd = ph.sbuf([128, ntok], BF16); r_vd = Res()
    uz = ph.sbuf([128, 2, ntok], F32); r_uz = Res()
    ob, r_ob = qd, r_qd
    sqrot = Rot(ph, 3, [128, 512], BF16)
    tmprot = Rot(ph, 4, [128, 512], F32)
    ptrot = Rot(ph, 4, [128, 256], BF16)
    vtrot = Rot(ph, 4, [128, 128], BF16)
    projrot = Rot(ph, 2, [128, 512], F32, psum=True)
    nrmrot = Rot(ph, 1, [128, 512], F32, psum=True)
    scrot = Rot(ph, 2, [128, 512], F32, psum=True)
    trp = ph.psum([128, 1024], BF16)
    r_trp = Res()
    trps = [(trp[:, i * 128:(i + 1) * 128], r_trp) for i in range(4)]
    uzrot = Rot(ph, 2, [128, 512], F32, psum=True)
    cnt = {"tr": 0}
    projrot.bufs = projrot.bufs + scrot.bufs

    hn_v = t["hnT"].rearrange("(c p) t -> p c t", p=128)
    wv = t["w"].rearrange("(c p) n -> p c n", p=128)
    y_v = t["yT"].rearrange("(c p) t -> p c t", p=128)

    def dview(buf, dil, l0, nl):
        if dil == 1:
            return buf[:, l0:l0 + nl]
        return buf[:].rearrange("p (r l) -> p r l", r=dil)[:, :, l0:l0 + nl].rearrange("p r l -> p l r")

    def pview(ps, dil):
        if dil == 1:
            return ps[:]
        return ps[:].rearrange("p (l r) -> p l r", r=dil)

    for h in range(8):
        wh, r_wh = whrot.next()
        for i in range(7):
            col = (i * 1024 + h * 128) if i < 6 else (6144 + h * 128)
            ph.dma("pool", wh[:, :, i, :], wv[:, :, col:col + 128], writes=[r_wh])
        ph.op("pool", lambda e: e.memset(uz[:], 0.0), writes=[r_uz])
        for g, (win, dil) in enumerate(DIL_GROUPS):
            L = ntok // dil
            nblk = L // 128
            for tt in range(NT):
                hn, r_hn = hnrot.next()
                ph.dma("sp", hn[:], hn_v[:, :, tt * 512:(tt + 1) * 512], writes=[r_hn])
                l0, nl = tt * 512 // dil, 512 // dil
                for which, dst, r_dst in ((0, qd, r_qd), (1, kd, r_kd)):
                    ps, r_ps = projrot.next()
                    wi = which * 3 + g
                    for c in range(8):
                        ph.op("pe", lambda e, ps=ps, c=c, wi=wi, hn=hn, wh=wh: e.matmul(ps[:], wh[:, c, wi, :], hn[:, c, :], start=(c == 0), stop=(c == 7)),
                              reads=[r_wh, r_hn], writes=[r_ps])
                    sq, r_sq = sqrot.next()
                    ph.op("act", lambda e, sq=sq, ps=ps: e.activation(out=sq[:], in_=ps[:], func=AF.Square), reads=[r_ps], writes=[r_sq])
                    pn, r_pn = nrmrot.next()
                    ph.op("pe", lambda e, pn=pn, sq=sq: e.matmul(pn[:], onesb[:], sq[:], start=True, stop=True), reads=[r_sq, r_c], writes=[r_pn])
                    tmp, r_tmp = tmprot.next()
                    ph.op("act", lambda e, tmp=tmp, pn=pn: e.activation(out=tmp[:], in_=pn[:], func=AF.Ln, scale=1.0 / 128, bias=epsc[:, 0:1]),
                          reads=[r_pn, r_c], writes=[r_tmp])
                    rstd, r_rstd = tmprot.next()
                    ph.op("act", lambda e, tmp=tmp, rstd=rstd: e.activation(out=rstd[:], in_=tmp[:], func=AF.Exp, scale=-0.5),
                          reads=[r_tmp], writes=[r_rstd])
                    ph.op("dve", lambda e, ps=ps, rstd=rstd, dst=dst, which=which, dil=dil, l0=l0, nl=nl: e.scalar_tensor_tensor(
                        out=dview(dst, dil, l0, nl), in0=pview(ps, dil), scalar=gqk[:, which:which + 1], in1=pview(rstd, dil),
                        op0=ALU.mult, op1=ALU.mult), reads=[r_ps, r_rstd, r_c], writes=[r_dst])
                ps, r_ps = projrot.next()
                for c in range(8):
                    ph.op("pe", lambda e, ps=ps, c=c, hn=hn, wh=wh: e.matmul(ps[:], wh[:, c, 6, :], hn[:, c, :], start=(c == 0), stop=(c == 7)),
                          reads=[r_wh, r_hn], writes=[r_ps])
                ph.op("act", lambda e, ps=ps, dil=dil, l0=l0, nl=nl: e.activation(out=dview(vd, dil, l0, nl), in_=pview(ps, dil), func=AF.Copy),
                      reads=[r_ps], writes=[r_vd])
            uzv = uz[:].rearrange("p a (l r) -> p a r l", r=dil) if dil > 1 else None
            for r in range(dil):
                base = r * L
                cur = None
                pend = None
                for step in range(nblk + 1):
                    newblk = None
                    if step < nblk:
                        j = step
                        nq = 256 if j < nblk - 1 else 128
                        ksl = slice(base + j * 128, base + (j + 1) * 128)
                        sc, r_sc = scrot.next()
                        ph.op("pe", lambda e, sc=sc, ksl=ksl, nq=nq, base=base, j=j: e.matmul(
                            sc[:, :nq], kd[:, ksl], qd[:, base + j * 128: base + j * 128 + nq], start=True, stop=False),
                            reads=[r_kd, r_qd], writes=[r_sc])
                        ph.op("pe", lambda e, sc=sc, nq=nq, g=g, h=h: e.matmul(sc[:, :nq], ident[:], bias[:, g * 8 + h, :nq], start=False, stop=True),
                              reads=[r_c], writes=[r_sc])
                        pt, r_pt = ptrot.next()
                        ph.op("act", lambda e, pt=pt, sc=sc, nq=nq: e.activation(out=pt[:, :nq], in_=sc[:, :nq], func=AF.Exp), reads=[r_sc], writes=[r_pt])
                        tp, r_tp = trps[cnt["tr"] % 4]; cnt["tr"] += 1
                        ph.op("pe", lambda e, tp=tp, ksl=ksl: e.transpose(tp, vd[:, ksl], ident[:]), reads=[r_vd, r_c], writes=[r_tp])
                        vt, r_vt = vtrot.next()
                        ph.op("dve", lambda e, vt=vt, tp=tp: e.tensor_copy(out=vt[:], in_=tp), reads=[r_tp], writes=[r_vt])
                        newblk = (j, pt, r_pt, vt, r_vt)
                    if pend is not None:
                        j, pt, r_pt, vt, r_vt = pend
                        if cur is None:
                            cur = uzrot.next()
                        cu, r_cu = cur
                        ph.op("pe", lambda e, cu=cu, vt=vt, pt=pt, j=j: e.matmul(cu[:, 0:128], vt[:], pt[:, 0:128], start=(j == 0), stop=True, skip_group_check=True),
                              reads=[r_vt, r_pt], writes=[r_cu])
                        ph.op("pe", lambda e, cu=cu, pt=pt, j=j: e.matmul(cu[:, 128:256], onesb[:], pt[:, 0:128], start=False, stop=True, skip_group_check=True),
                              reads=[r_pt, r_c], writes=[r_cu])
                        nxt = None
                        if j < nblk - 1:
                            nxt = uzrot.next()
                            nu, r_nu = nxt
                            ph.op("pe", lambda e, nu=nu, vt=vt, pt=pt: e.matmul(nu[:, 0:128], vt[:], pt[:, 128:256], start=True, stop=False, skip_group_check=True),
                                  reads=[r_vt, r_pt], writes=[r_nu])
                            ph.op("pe", lambda e, nu=nu, pt=pt: e.matmul(nu[:, 128:256], onesb[:], pt[:, 128:256], start=False, stop=False, skip_group_check=True),
                                  reads=[r_pt, r_c], writes=[r_nu])
                        if dil == 1:
                            dstv = uz[:, :, j * 128:(j + 1) * 128]
                        else:
                            dstv = uzv[:, :, r, j * 128:(j + 1) * 128]
                        ph.op("dve", lambda e, dstv=dstv, cu=cu: e.tensor_tensor(
                            out=dstv, in0=dstv, in1=cu[:, 0:256].rearrange("p (a n) -> p a n", a=2), op=ALU.add),
                            reads=[r_cu, r_uz], writes=[r_uz])
                        cur = nxt
                    pend = newblk
        for tt in range(NT):
            tsl = slice(tt * 512, (tt + 1) * 512)
            rz, r_rz = tmprot.next()
            ph.op("dve", lambda e, rz=rz, tsl=tsl: e.reciprocal(out=rz[:], in_=uz[:, 1, tsl]), reads=[r_uz], writes=[r_rz])
            ph.op("dve", lambda e, rz=rz, tsl=tsl: e.tensor_tensor(out=ob[:, tsl], in0=uz[:, 0, tsl], in1=rz[:], op=ALU.mult),
                  reads=[r_uz, r_rz], writes=[r_ob])
        ph.dma("sp", y_v[:, h, :], ob[:], reads=[r_ob])
    ph.end()


def diff_lambda_init(layer_idx):
    return 0.8 - 0.6 * math.exp(-0.3 * layer_idx)


def diff_consts(ntok):
    import ml_dtypes
    sl = alibi_slopes8()
    col = np.arange(512)
    qaug = np.zeros((3, 8, 512), np.float32)
    kaug = np.zeros((3, 8, ntok), np.float32)
    kj = np.arange(ntok) % 128
    for i in range(8):
        qaug[0, i] = -sl[i] * 128.0 * (col // 128)
        qaug[1, i] = -sl[i] * (col % 128)
        qaug[2, i] = 1.0
        kaug[0, i] = 1.0
        kaug[1, i] = 1.0
        kaug[2, i] = sl[i] * kj
    maskb = np.zeros((128, 4, 512), np.float32)
    kk = np.arange(128)[:, None]
    for d in range(4):
        maskb[:, d, :] = np.where(128 * d + kk <= col[None, :], 0.0, NEGB)
    bf = ml_dtypes.bfloat16
    return dict(qaug=qaug.astype(bf), kaug=kaug.astype(bf), maskb=maskb.astype(bf), ident=np.eye(128, dtype=np.float32).astype(bf))


def phase_diff(ctx, t, ntok, layer_idx):
    ph = Phase(ctx)
    NT = ntok // 512
    lam_init = diff_lambda_init(layer_idx)
    sl = alibi_slopes8()
    r_c = Res()
    maskb = ph.sbuf([128, 4, 512], BF16)
    ident = ph.sbuf([128, 128], BF16)
    ones32 = ph.sbuf([128, 128], F32)
    mh = ph.sbuf([128, 512], F32)
    gqk = ph.sbuf([64, 2], F32)
    lamv = ph.sbuf([64, 4], F32)
    gsub = ph.sbuf([128, 1], F32)
    lamw = ph.sbuf([128, 4], F32)
    ph.dma("sp", maskb[:], t["maskb"], writes=[r_c])
    ph.dma("sp", ident[:], t["ident"], writes=[r_c])
    ph.dma("sp", gqk[:, 0:1], t["gq"], writes=[r_c])
    ph.dma("sp", gqk[:, 1:2], t["gk"], writes=[r_c])
    ph.dma("sp", lamv[:], t["lam"], writes=[r_c])
    ph.dma("sp", gsub[:], t["gsub"], writes=[r_c])
    ph.op("pool", lambda e: e.memset(ones32[:], 1.0), writes=[r_c])
    epsc = ph.sbuf([128, 1], F32)
    ph.op("pool", lambda e: e.memset(epsc[:], EPS), writes=[r_c])
    ph.op("dve", lambda e: e.tensor_scalar(out=gqk[:, 0:1], in0=gqk[:, 0:1], scalar1=64.0 ** -0.5, scalar2=None, op0=ALU.mult),
          reads=[r_c], writes=[r_c])
    ph.op("dve", lambda e: e.tensor_scalar(out=gsub[:], in0=gsub[:], scalar1=1.0 - lam_init, scalar2=None, op0=ALU.mult),
          reads=[r_c], writes=[r_c])
    projrot = Rot(ph, 2, [128, 512], F32, psum=True)
    nrmrot = Rot(ph, 1, [128, 512], F32, psum=True)
    scrot = Rot(ph, 3, [128, 512], F32, psum=True)
    urot = Rot(ph, 1, [128, 512], F32, psum=True)
    trp = ph.psum([128, 1024], BF16)
    r_trp = Res()
    trps = [(trp[:, i * 512:(i + 1) * 512], r_trp) for i in range(2)]
    projrot.bufs = projrot.bufs + scrot.bufs
    ph.op("dve", lambda e: e.tensor_tensor(out=lamv[:, 0:1], in0=lamv[:, 0:1], in1=lamv[:, 1:2], op=ALU.mult), reads=[r_c], writes=[r_c])
    ph.op("dve", lambda e: e.tensor_tensor(out=lamv[:, 1:2], in0=lamv[:, 2:3], in1=lamv[:, 3:4], op=ALU.mult), reads=[r_c], writes=[r_c])
    pn, r_pn = nrmrot.next()
    ph.op("pe", lambda e: e.matmul(pn[:, 0:2], ones32[0:64, :], lamv[:, 0:2], start=True, stop=True), reads=[r_c], writes=[r_pn])
    ph.op("act", lambda e: e.activation(out=lamw[:, 0:2], in_=pn[:, 0:2], func=AF.Exp), reads=[r_pn], writes=[r_c])
    ph.op("dve", lambda e: e.tensor_tensor(out=lamw[:, 2:3], in0=lamw[:, 1:2], in1=lamw[:, 0:1], op=ALU.subtract), reads=[r_c], writes=[r_c])
    ph.op("dve", lambda e: e.tensor_scalar(out=lamw[:, 2:3], in0=lamw[:, 2:3], scalar1=-lam_init, scalar2=None, op0=ALU.add),
          reads=[r_c], writes=[r_c])

    whrot = Rot(ph, 2, [128, 8, 384], BF16)
    hnrot = Rot(ph, 2, [128, 8, 512], BF16)
    ka = [ph.sbuf([128, ntok], BF16) for _ in range(2)]
    r_ka = [[Res() for _ in range(NT)] for _ in range(2)]
    r_kaug = Res()
    vtm = ph.sbuf([128, ntok // 128, 128], BF16)
    r_vtm = [Res() for _ in range(NT)]
    qa = [[ph.sbuf([128, 512], BF16) for _ in range(2)] for _ in range(2)]
    r_qa = [[Res() for _ in range(2)] for _ in range(2)]
    vtrot = Rot(ph, 2, [128, 512], BF16)
    sqrot = Rot(ph, 3, [128, 512], BF16)
    onesb = ph.sbuf([128, 128], BF16)
    ph.op("pool", lambda e: e.memset(onesb[:], 1.0), writes=[r_c])
    tmprot = Rot(ph, 4, [128, 512], F32)
    ptrot = Rot(ph, 6, [128, 512], BF16)
    zD = Rot(ph, 2, [128, 512], F32)
    zP = Rot(ph, 2, [128, 512], F32)
    o1rot = Rot(ph, 2, [128, 512], F32)
    outrot = Rot(ph, 2, [128, 512], BF16)
    cnt = {"tr": 0}

    hn_v = t["hnT"].rearrange("(c p) t -> p c t", p=128)
    wv = t["w"].rearrange("(c p) n -> p c n", p=128)
    y_v = t["yT"].rearrange("(c p) t -> p c t", p=128)

    def project(i, tt, wh, r_wh):
        hn, r_hn = hnrot.next()
        ph.dma("sp", hn[:], hn_v[:, :, tt * 512:(tt + 1) * 512], writes=[r_hn])
        tsl = slice(tt * 512, (tt + 1) * 512)
        for hd in range(2):
            for which in range(2):
                ps, r_ps = projrot.next()
                c0 = which * 128 + hd * 64
                for c in range(8):
                    ph.op("pe", lambda e, ps=ps, c=c, c0=c0, hn=hn, wh=wh: e.matmul(ps[0:64, :], wh[:, c, c0:c0 + 64], hn[:, c, :], start=(c == 0), stop=(c == 7)),
                          reads=[r_wh, r_hn], writes=[r_ps])
                sq, r_sq = sqrot.next()
                ph.op("act", lambda e, sq=sq, ps=ps: e.activation(out=sq[0:64, :], in_=ps[0:64, :], func=AF.Square), reads=[r_ps], writes=[r_sq])
                pn, r_pn = nrmrot.next()
                ph.op("pe", lambda e, pn=pn, sq=sq: e.matmul(pn[0:64, :], onesb[0:64, 0:64], sq[0:64, :], start=True, stop=True), reads=[r_sq, r_c], writes=[r_pn])
                tmp, r_tmp = tmprot.next()
                ph.op("act", lambda e, tmp=tmp, pn=pn: e.activation(out=tmp[0:64, :], in_=pn[0:64, :], func=AF.Ln, scale=1.0 / 64, bias=epsc[0:64, 0:1]),
                      reads=[r_pn, r_c], writes=[r_tmp])
                rstd, r_rstd = tmprot.next()
                ph.op("act", lambda e, tmp=tmp, rstd=rstd: e.activation(out=rstd[0:64, :], in_=tmp[0:64, :], func=AF.Exp, scale=-0.5),
                      reads=[r_tmp], writes=[r_rstd])
                if which == 0:
                    dst, r_dst = qa[hd][tt % 2][0:64, :], r_qa[hd][tt % 2]
                else:
                    dst, r_dst = ka[hd][0:64, tsl], r_ka[hd][tt]
                ph.op("dve", lambda e, ps=ps, rstd=rstd, dst=dst, which=which: e.scalar_tensor_tensor(
                    out=dst, in0=ps[0:64, :], scalar=gqk[:, which:which + 1], in1=rstd[0:64, :], op0=ALU.mult, op1=ALU.mult),
                    reads=[r_ps, r_rstd, r_c], writes=[r_dst])
        ps, r_ps = projrot.next()
        for c in range(8):
            ph.op("pe", lambda e, ps=ps, c=c, hn=hn, wh=wh: e.matmul(ps[:], wh[:, c, 256:384], hn[:, c, :], start=(c == 0), stop=(c == 7)),
                  reads=[r_wh, r_hn], writes=[r_ps])
        vt, r_vt = vtrot.next()
        ph.op("act", lambda e, vt=vt, ps=ps: e.activation(out=vt[:], in_=ps[:], func=AF.Copy), reads=[r_ps], writes=[r_vt])
        tp, r_tp = trps[cnt["tr"] % 2]; cnt["tr"] += 1
        for b in range(4):
            ph.op("pe", lambda e, tp=tp, vt=vt, b=b: e.transpose(tp[:, b * 128:(b + 1) * 128], vt[:, b * 128:(b + 1) * 128], ident[:]),
                  reads=[r_vt, r_c], writes=[r_tp])
        ph.op("act", lambda e, tp=tp, tt=tt: e.activation(out=vtm[:, tt * 4:(tt + 1) * 4, :], in_=tp.rearrange("p (b n) -> p b n", b=4), func=AF.Copy),
              reads=[r_tp], writes=[r_vtm[tt]])

    def attend(i, tt):
        nkb = 4 * tt + 4
        o1 = None
        for hd in range(2):
            q, r_q = qa[hd][tt % 2], r_qa[hd][tt % 2]
            U, r_U = urot.next()
            zd, r_zd = zD.next()
            zp, r_zp = zP.next()
            LAG = 2
            pts = {}
            kbs = [kb for kb in range(nkb) if sl[i] * (128.0 * (4 * tt - kb) - 127.0) <= 200.0]
            nk = len(kbs)
            for step in range(nk + LAG):
                if step < nk:
                    kb = kbs[step]
                    diag = kb >= 4 * tt
                    sc, r_sc = scrot.next()
                    ph.op("pe", lambda e, sc=sc, kb=kb, q=q, hd=hd, diag=diag: e.matmul(
                        sc[:], ka[hd][0:67, kb * 128:(kb + 1) * 128], q[0:67, :], start=True, stop=(not diag)),
                        reads=[r_ka[hd][kb // 4], r_kaug, r_q], writes=[r_sc])
                    if diag:
                        ph.op("pe", lambda e, sc=sc, d=kb - 4 * tt: e.matmul(sc[:], ident[:], maskb[:, d, :], start=False, stop=True),
                              reads=[r_c], writes=[r_sc])
                    pt, r_pt = ptrot.next()
                    pts[step] = (pt, r_pt, kb)
                    cbias = -sl[i] * 128.0 * (4 * tt - kb)
                    ph.op("act", lambda e, pt=pt, sc=sc, cbias=cbias: e.activation(out=pt[:], in_=sc[:], func=AF.Exp, bias=cbias),
                          reads=[r_sc], writes=[r_pt])
                    eng, z, r_z = ("dve", zd, r_zd) if step % 2 == 0 else ("pool", zp, r_zp)
                    if step < 2:
                        ph.op(eng, lambda e, z=z, pt=pt: e.tensor_copy(out=z[:], in_=pt[:]), reads=[r_pt], writes=[r_z])
                    else:
                        ph.op(eng, lambda e, z=z, pt=pt: e.tensor_tensor(out=z[:], in0=z[:], in1=pt[:], op=ALU.add), reads=[r_pt, r_z], writes=[r_z])
                sv = step - LAG
                if sv >= 0:
                    pt, r_pt, kv = pts.pop(sv)
                    ph.op("pe", lambda e, U=U, pt=pt, kv=kv, sv=sv, nk=nk: e.matmul(U[:], vtm[:, kv, :], pt[:], start=(sv == 0), stop=(sv == nk - 1)),
                          reads=[r_vtm[kv // 4], r_pt], writes=[r_U])
            pn, r_pn = nrmrot.next()
            ph.op("pe", lambda e, pn=pn, zd=zd: e.matmul(pn[:], ones32[:], zd[:], start=True, stop=False), reads=[r_zd, r_c], writes=[r_pn])
            ph.op("pe", lambda e, pn=pn, zp=zp: e.matmul(pn[:], ones32[:], zp[:], start=False, stop=True), reads=[r_zp, r_c], writes=[r_pn])
            rz, r_rz = tmprot.next()
            ph.op("dve", lambda e, rz=rz, pn=pn: e.reciprocal(out=rz[:], in_=pn[:]), reads=[r_pn], writes=[r_rz])
            if hd == 0:
                o1, r_o1 = o1rot.next()
                ph.op("dve", lambda e, o1=o1, U=U, rz=rz: e.tensor_tensor(out=o1[:], in0=U[:], in1=rz[:], op=ALU.mult), reads=[r_U, r_rz], writes=[r_o1])
            else:
                ph.op("dve", lambda e, U=U, rz=rz: e.tensor_tensor(out=rz[:], in0=U[:], in1=rz[:], op=ALU.mult), reads=[r_U, r_rz], writes=[r_rz])
                ph.op("dve", lambda e, o1=o1, rz=rz: e.scalar_tensor_tensor(out=o1[:], in0=rz[:], scalar=lamw[:, 2:3], in1=o1[:], op0=ALU.mult, op1=ALU.add),
                      reads=[r_rz, r_o1, r_c], writes=[r_o1])
        sq, r_sq = sqrot.next()
        ph.op("act", lambda e, sq=sq, o1=o1: e.activation(out=sq[:], in_=o1[:], func=AF.Square), reads=[r_o1], writes=[r_sq])
        pn, r_pn = nrmrot.next()
        ph.op("pe", lambda e, pn=pn, sq=sq: e.matmul(pn[:], onesb[:], sq[:], start=True, stop=True), reads=[r_sq, r_c], writes=[r_pn])
        tmp, r_tmp = tmprot.next()
        ph.op("act", lambda e, tmp=tmp, pn=pn: e.activation(out=tmp[:], in_=pn[:], func=AF.Ln, scale=1.0 / 128, bias=epsc[:, 0:1]),
              reads=[r_pn, r_c], writes=[r_tmp])
        rstd, r_rstd = tmprot.next()
        ph.op("act", lambda e, tmp=tmp, rstd=rstd: e.activation(out=rstd[:], in_=tmp[:], func=AF.Exp, scale=-0.5), reads=[r_tmp], writes=[r_rstd])
        ot, r_ot = outrot.next()
        ph.op("dve", lambda e, ot=ot, o1=o1, rstd=rstd: e.scalar_tensor_tensor(out=ot[:], in0=o1[:], scalar=gsub[:, 0:1], in1=rstd[:], op0=ALU.mult, op1=ALU.mult),
              reads=[r_o1, r_rstd, r_c], writes=[r_ot])
        ph.dma("sp", y_v[:, i, tt * 512:(tt + 1) * 512], ot[:], reads=[r_ot])

    for i in range(8):
        wh, r_wh = whrot.next()
        ph.dma("pool", wh[:, :, 0:128], wv[:, :, i * 128:(i + 1) * 128], writes=[r_wh])
        ph.dma("pool", wh[:, :, 128:256], wv[:, :, 1024 + i * 128:1024 + (i + 1) * 128], writes=[r_wh])
        ph.dma("pool", wh[:, :, 256:384], wv[:, :, 2048 + i * 128:2048 + (i + 1) * 128], writes=[r_wh])
        for hd in range(2):
            ph.dma("sp", ka[hd][64:67, :], t["kaug"][:, i, :], writes=[r_kaug] if hd == 1 else [r_kaug])
            for par in range(2):
                ph.dma("sp", qa[hd][par][64:67, :], t["qaug"][:, i, :], writes=[r_qa[hd][par]])
        pend = None
        for tt in range(NT):
            project(i, tt, wh, r_wh)
            if pend is not None:
                attend(i, pend)
            pend = tt
        attend(i, pend)
    ph.end()


def build_program(ntok):
    nc = bass.Bass("TRN2", target_bir_lowering=False)

    def din(name, shape, dt=F32):
        return nc.dram_tensor(name, list(shape), dt, kind="ExternalInput").ap()

    def dint(name, shape, dt=F32):
        return nc.dram_tensor(name, list(shape), dt).ap()

    xT = din("xT", [D, ntok])
    pT = din("pT", [DEPTH, 256, ntok])
    vecs = din("vecs", [DEPTH + 1, 128, 32])
    a_w = din("a_w", [2, 2, D, 3072])
    a_wo = din("a_wo", [2, 2048, D])
    b_w = din("b_w", [D, 7168])
    b_wo = din("b_wo", [D, D])
    c_w = din("c_w", [D, 3072])
    c_wo = din("c_wo", [D, D])
    f_wi = din("f_wi", [DEPTH, D, 2 * HID])
    f_wo = din("f_wo", [DEPTH, HID, D])
    p_wp = din("p_wp", [DEPTH, 256, D])
    p_wg = din("p_wg", [DEPTH, D, D])
    rc = dict(dq=din("r_dq", [2, 128, 2, 512]), dkt=din("r_dkt", [2, 128, 2, 512]), dkm=din("r_dkm", [2, 128, 2]),
              mask=din("r_mask", [128, 128]))
    ident = din("ident", [128, 128], BF16)
    b_gq = din("b_gq", [128, 1]); b_gk = din("b_gk", [128, 1]); b_bias = din("b_bias", [128, 24, 256], BF16)
    c_gq = din("c_gq", [64, 1]); c_gk = din("c_gk", [64, 1]); c_lam = din("c_lam", [64, 4]); c_gsub = din("c_gsub", [128, 1])
    c_qaug = din("c_qaug", [3, 8, 512], BF16); c_kaug = din("c_kaug", [3, 8, ntok], BF16); c_maskb = din("c_maskb", [128, 4, 512], BF16)
    outT = nc.dram_tensor("outT", [D, ntok], F32, kind="ExternalOutput").ap()
    hA = dint("hA", [D, ntok]); hB = dint("hB", [D, ntok])
    hnT = dint("hnT", [D, ntok], BF16)
    yT = dint("yT", [2048, ntok], BF16)

    ctx = Ctx(nc)
    phase_token_local(ctx, dict(hT=xT, vecs=vecs[0], hnT_out=hnT), 0, ntok, True, False)
    h_in = xT
    for i in range(DEPTH):
        kind, j = i % 3, i // 3
        if kind == 0:
            for r in range(2):
                phase_retention(ctx, dict(hnT=hnT, w=a_w[j, r], dq=rc["dq"][r], dkt=rc["dkt"][r], dkm=rc["dkm"][r], mask=rc["mask"],
                                          ident=ident, yT=yT[r * 1024:(r + 1) * 1024, :]),
                                [ret_gamma(2 * r), ret_gamma(2 * r + 1)], ntok)
            kout, wout = 2048, a_wo[j]
        elif kind == 1:
            phase_dilated(ctx, dict(hnT=hnT, w=b_w, gq=b_gq, gk=b_gk, bias=b_bias, ident=ident, yT=yT[0:1024, :]), ntok)
            kout, wout = 1024, b_wo
        else:
            phase_diff(ctx, dict(hnT=hnT, w=c_w, gq=c_gq, gk=c_gk, lam=c_lam, gsub=c_gsub, qaug=c_qaug, kaug=c_kaug, maskb=c_maskb,
                                 ident=ident, yT=yT[0:1024, :]), ntok, i)
            kout, wout = 1024, c_wo
        last = (i == DEPTH - 1)
        h_out = outT if last else (hA if i % 2 == 0 else hB)
        phase_token_local(ctx, dict(hT=h_in, yT=yT[0:kout, :], pT=pT[i], wout=wout, win=f_wi[i], wo=f_wo[i], wg=p_wg[i], wp=p_wp[i],
                                    vecs=vecs[i + 1], hT_out=h_out, hnT_out=hnT), kout, ntok, False, last)
        h_in = h_out
    ctx.finish()
    return nc


def _colvec(v):
    return np.ascontiguousarray(np.asarray(v, np.float32).reshape(-1, 128).T)


def prepare_inputs(inp, nb, ntok):
    f32 = np.float32
    g = {k: np.asarray(v) for k, v in inp.items()}
    vecs = np.zeros((DEPTH + 1, 128, 32), f32)
    vecs[0, :, 24:32] = _colvec(g["mix_norm"][0])
    for i in range(DEPTH):
        vecs[i + 1, :, 0:8] = _colvec(g["ffn_norm"][i])
        vecs[i + 1, :, 8:16] = _colvec(g["ple_gate_norm"][i])
        vecs[i + 1, :, 16:24] = _colvec(g["ple_norm"][i])
        if i + 1 < DEPTH:
            vecs[i + 1, :, 24:32] = _colvec(g["mix_norm"][i + 1])
    a_w = np.zeros((2, 2, D, 3072), f32)
    for j in range(2):
        w = g["a_w_in"][j]
        for r in range(2):
            parts = []
            for h in (2 * r, 2 * r + 1):
                parts += [w[:, h * 256:(h + 1) * 256], w[:, 1024 + h * 256:1024 + (h + 1) * 256],
                          w[:, 2048 + h * 512:2048 + (h + 1) * 512], w[:, 4096 + h * 512:4096 + (h + 1) * 512]]
            a_w[j, r] = np.concatenate(parts, axis=1)
    rcs = [ret_consts(r) for r in range(2)]
    dc = dil_consts()
    fc = diff_consts(ntok)
    shared = dict(
        vecs=vecs, a_w=a_w, a_wo=g["a_w_out"].astype(f32), b_w=g["b_w_in"][0], b_wo=g["b_w_out"][0], c_w=g["c_w_in"][0], c_wo=g["c_w_out"][0],
        f_wi=g["ffn_w_in"], f_wo=g["ffn_w_out"], p_wp=g["ple_w_proj"], p_wg=g["ple_w_gate"],
        r_dq=np.stack([c["dq"] for c in rcs]), r_dkt=np.stack([c["dkt"] for c in rcs]), r_dkm=np.stack([c["dkm"] for c in rcs]),
        r_mask=rcs[0]["mask"], ident=rcs[0]["ident"],
        b_gq=g["b_q_norm"][0].reshape(128, 1), b_gk=g["b_k_norm"][0].reshape(128, 1), b_bias=dc["bias"],
        c_gq=g["c_q_norm"][0].reshape(64, 1), c_gk=g["c_k_norm"][0].reshape(64, 1),
        c_lam=np.ascontiguousarray(np.stack([g["c_lambda_q1"][0], g["c_lambda_k1"][0], g["c_lambda_q2"][0], g["c_lambda_k2"][0]], axis=1).astype(f32)),
        c_gsub=g["c_subln"][0].reshape(128, 1), c_qaug=fc["qaug"], c_kaug=fc["kaug"], c_maskb=fc["maskb"])
    shared = {k: np.ascontiguousarray(v) for k, v in shared.items()}
    in_maps = []
    for b in range(nb):
        m = dict(shared)
        m["xT"] = np.ascontiguousarray(g["x"][b, :ntok].T)
        m["pT"] = np.ascontiguousarray(np.transpose(g["p"][:, b, :ntok, :], (0, 2, 1)))
        in_maps.append(m)
    return in_maps


def run_model(inp, nb, ntok):
    nc = build_program(ntok)
    in_maps = prepare_inputs(inp, nb, ntok)
    res = run_bass_kernel_spmd(nc, in_maps, core_ids=list(range(nb)))
    out = np.stack([np.ascontiguousarray(res.results[b]["outT"].T) for b in range(nb)], axis=0)
    return out.astype(np.float32)


def kernel(**inputs):
    return run_model(inputs, B, S)
```
